# Optimizing a Trainium2 kernel written in Bass

```python
import math
import jax, jax.numpy as jnp
from jax import lax
import numpy as np

D_MODEL = 2048
BATCH = 2
SEQ = 8192
DEPTH = 4

ALPHA = (2.0 * DEPTH) ** 0.25
BETA = (8.0 * DEPTH) ** -0.25
LN_EPS = 1e-5
RMS_EPS = 1e-5

SSD_D_INNER = 2 * D_MODEL
SSD_HEAD_DIM = 64
SSD_N_HEADS = SSD_D_INNER // SSD_HEAD_DIM
SSD_N_GROUPS = 8
SSD_HEADS_PER_GROUP = SSD_N_HEADS // SSD_N_GROUPS
SSD_D_STATE = 128
SSD_CONV_W = 4
SSD_CONV_DIM = SSD_D_INNER + 2 * SSD_N_GROUPS * SSD_D_STATE
SSD_CHUNK = 128

GLA_N_HEADS = 4
GLA_D_KEY = D_MODEL // 2
GLA_D_VALUE = D_MODEL
GLA_HEAD_K = GLA_D_KEY // GLA_N_HEADS
GLA_HEAD_V = GLA_D_VALUE // GLA_N_HEADS
GLA_GATE_RANK = 16
GLA_GATE_NORMALIZER = 16.0
GLA_CHUNK = 64

D_FF = 4 * D_MODEL

IN_SPLIT_SIZES = (SSD_D_INNER, SSD_CONV_DIM, SSD_N_HEADS, GLA_D_KEY, GLA_D_KEY, GLA_D_VALUE, GLA_D_VALUE, GLA_GATE_RANK, D_MODEL, D_MODEL)
D_IN_PROJ = SSD_D_INNER + SSD_CONV_DIM + SSD_N_HEADS + 2 * GLA_D_KEY + 2 * GLA_D_VALUE + GLA_GATE_RANK + 2 * D_MODEL

kernel_name = 'hybrid_ssd_gla_deepnorm'


def _split_cols(a, sizes):
    idx = np.cumsum(np.array(sizes))[:-1].tolist()
    return jnp.split(a, idx, axis=-1)


def layer_norm(x, g, b):
    xf = x.astype(jnp.float32)
    mu = jnp.mean(xf, axis=-1, keepdims=True)
    xc = xf - mu
    var = jnp.mean(xc * xc, axis=-1, keepdims=True)
    y = xc * lax.rsqrt(var + LN_EPS) * g.astype(jnp.float32) + b.astype(jnp.float32)
    return y.astype(x.dtype)


def grouped_rms_norm(y, w, n_groups):
    shp = y.shape
    yf = y.astype(jnp.float32).reshape(shp[:-1] + (n_groups, shp[-1] // n_groups))
    yf = yf * lax.rsqrt(jnp.mean(yf * yf, axis=-1, keepdims=True) + RMS_EPS)
    return (yf.reshape(shp) * w.astype(jnp.float32)).astype(y.dtype)


def causal_depthwise_conv(u, w, b):
    out = lax.conv_general_dilated(
        u, w[:, None, :].astype(u.dtype), window_strides=(1,),
        padding=[(SSD_CONV_W - 1, 0)], dimension_numbers=('NWC', 'WIO', 'NWC'),
        feature_group_count=u.shape[-1])
    return out + b.astype(u.dtype)


def ssd_chunked(xh, dt, A, Bm, Cm):
    Bsz, L = xh.shape[0], xh.shape[1]
    Q = SSD_CHUNK
    nc = L // Q
    G, Hg, P, N = SSD_N_GROUPS, SSD_HEADS_PER_GROUP, SSD_HEAD_DIM, SSD_D_STATE
    x = xh.reshape(Bsz, nc, Q, G, Hg, P)
    dtc = dt.reshape(Bsz, nc, Q, G, Hg)
    Bc = Bm.reshape(Bsz, nc, Q, G, N)
    Cc = Cm.reshape(Bsz, nc, Q, G, N)
    a_cs = jnp.cumsum(dtc * A.reshape(G, Hg), axis=2)
    xdt = x * dtc[..., None]
    mask = jnp.tril(jnp.ones((Q, Q), dtype=bool))[:, :, None, None]
    seg = a_cs[:, :, :, None] - a_cs[:, :, None, :]
    decay = jnp.exp(jnp.where(mask, seg, -jnp.inf))
    cb = jnp.einsum('bcign,bcjgn->bcijg', Cc, Bc)
    y_diag = jnp.einsum('bcijgh,bcjghp->bcighp', cb[..., None] * decay, xdt)
    a_last = a_cs[:, :, -1]
    xw = xdt * jnp.exp(a_last[:, :, None] - a_cs)[..., None]
    states = jnp.einsum('bcjgn,bcjghp->bcghpn', Bc, xw)

    def step(carry, inp):
        st, al = inp
        new = carry * jnp.exp(al)[..., None, None] + st
        return new, carry

    init = jnp.zeros((Bsz, G, Hg, P, N), dtype=states.dtype)
    _, prev = lax.scan(step, init, (jnp.moveaxis(states, 1, 0), jnp.moveaxis(a_last, 1, 0)))
    prev = jnp.moveaxis(prev, 0, 1)
    y_off = jnp.einsum('bcign,bcghpn->bcighp', Cc, prev) * jnp.exp(a_cs)[..., None]
    return (y_diag + y_off).reshape(Bsz, L, SSD_N_HEADS, P)


def gla_chunked(q, k, v, gk):
    Bsz, L = q.shape[0], q.shape[1]
    C = GLA_CHUNK
    nc = L // C
    H, K, V = GLA_N_HEADS, GLA_HEAD_K, GLA_HEAD_V
    scale = K ** -0.5
    qc = q.reshape(Bsz, nc, C, H, K)
    kc = k.reshape(Bsz, nc, C, H, K)
    vc = v.reshape(Bsz, nc, C, H, V)
    G = jnp.cumsum(gk.reshape(Bsz, nc, C, H, K), axis=2)
    G_last = G[:, :, -1:]
    qg = qc * scale * jnp.exp(G)
    kg = kc * jnp.exp(-G)
    kd = kc * jnp.exp(G_last - G)
    att = jnp.einsum('bcihk,bcjhk->bchij', qg, kg)
    att = jnp.where(jnp.tril(jnp.ones((C, C), dtype=bool)), att, 0.0)
    o_intra = jnp.einsum('bchij,bcjhv->bcihv', att, vc)

    def step(S, inp):
        q_i, k_i, v_i, d_i = inp
        o = jnp.einsum('bihk,bhkv->bihv', q_i, S)
        S = S * d_i[..., None] + jnp.einsum('bjhk,bjhv->bhkv', k_i, v_i)
        return S, o

    init = jnp.zeros((Bsz, H, K, V), dtype=q.dtype)
    xs = (jnp.moveaxis(qg, 1, 0), jnp.moveaxis(kd, 1, 0), jnp.moveaxis(vc, 1, 0),
          jnp.moveaxis(jnp.exp(G_last[:, :, 0]), 1, 0))
    _, o_inter = lax.scan(step, init, xs)
    o_inter = jnp.moveaxis(o_inter, 0, 1)
    return (o_intra + o_inter).reshape(Bsz, L, H, V)


def hybrid_mixer(h, w_in, ssd_conv_w, ssd_conv_b, ssd_dt_bias, ssd_A_log, ssd_D, ssd_norm_w,
                 gla_gk_w, gla_gk_b, gla_norm_w, w_ssd_branch, w_gla_branch, gate_bias, w_out):
    f32 = jnp.float32
    Bsz, L, _ = h.shape
    proj = h @ w_in
    z, xbc, dt_raw, q, k, v, g, gk_lr, gate_ssd, gate_gla = _split_cols(proj, IN_SPLIT_SIZES)

    xbc = jax.nn.silu(causal_depthwise_conv(xbc, ssd_conv_w, ssd_conv_b))
    xs, Bm, Cm = _split_cols(xbc, (SSD_D_INNER, SSD_N_GROUPS * SSD_D_STATE, SSD_N_GROUPS * SSD_D_STATE))
    xh = xs.reshape(Bsz, L, SSD_N_HEADS, SSD_HEAD_DIM).astype(f32)
    dt = jax.nn.softplus(dt_raw.astype(f32) + ssd_dt_bias.astype(f32))
    A = -jnp.exp(ssd_A_log.astype(f32))
    y = ssd_chunked(xh, dt, A,
                    Bm.reshape(Bsz, L, SSD_N_GROUPS, SSD_D_STATE).astype(f32),
                    Cm.reshape(Bsz, L, SSD_N_GROUPS, SSD_D_STATE).astype(f32))
    y = y + ssd_D.astype(f32)[:, None] * xh
    y = y.reshape(Bsz, L, SSD_D_INNER).astype(h.dtype)
    y = grouped_rms_norm(y * jax.nn.silu(z), ssd_norm_w, SSD_N_GROUPS)

    gk = jax.nn.log_sigmoid((gk_lr @ gla_gk_w + gla_gk_b).astype(f32)) / GLA_GATE_NORMALIZER
    o = gla_chunked(q.reshape(Bsz, L, GLA_N_HEADS, GLA_HEAD_K).astype(f32),
                    k.reshape(Bsz, L, GLA_N_HEADS, GLA_HEAD_K).astype(f32),
                    v.reshape(Bsz, L, GLA_N_HEADS, GLA_HEAD_V).astype(f32),
                    gk.reshape(Bsz, L, GLA_N_HEADS, GLA_HEAD_K))
    o = grouped_rms_norm(o.astype(h.dtype), gla_norm_w, 1) * jax.nn.silu(g).reshape(Bsz, L, GLA_N_HEADS, GLA_HEAD_V)
    o = o.reshape(Bsz, L, GLA_D_VALUE)

    branch_ssd = y @ w_ssd_branch
    branch_gla = o @ w_gla_branch
    merged = (jax.nn.sigmoid(gate_ssd + gate_bias[0]) * branch_ssd
              + jax.nn.sigmoid(gate_gla + gate_bias[1]) * branch_gla)
    return merged @ w_out


def squared_relu_mlp(h, w_up, w_down):
    return jnp.square(jax.nn.relu(h @ w_up)) @ w_down


def setup_inputs(seed: int = 0) -> dict:
    key = jax.random.key(seed)
    ks = jax.random.split(key, 21)
    nrm = jax.random.normal
    H = SSD_N_HEADS
    x = nrm(ks[0], (BATCH, SEQ, D_MODEL), jnp.float32)
    w_in = nrm(ks[1], (DEPTH, D_MODEL, D_IN_PROJ), jnp.float32) * D_MODEL ** -0.5
    ssd_conv_w = nrm(ks[2], (DEPTH, SSD_CONV_W, SSD_CONV_DIM), jnp.float32) * SSD_CONV_W ** -0.5
    ssd_conv_b = 0.02 * nrm(ks[3], (DEPTH, SSD_CONV_DIM), jnp.float32)
    dt0 = jnp.exp(jax.random.uniform(ks[4], (DEPTH, H), jnp.float32, minval=math.log(1e-3), maxval=math.log(1e-1)))
    ssd_dt_bias = dt0 + jnp.log(-jnp.expm1(-dt0))
    ssd_A_log = jnp.log(jax.random.uniform(ks[5], (DEPTH, H), jnp.float32, minval=1.0, maxval=16.0))
    ssd_D = 1.0 + 0.1 * nrm(ks[6], (DEPTH, H), jnp.float32)
    ssd_norm_w = 1.0 + 0.02 * nrm(ks[7], (DEPTH, SSD_D_INNER), jnp.float32)
    gla_gk_w = nrm(ks[8], (DEPTH, GLA_GATE_RANK, GLA_D_KEY), jnp.float32) * GLA_GATE_RANK ** -0.5
    gla_gk_b = 0.02 * nrm(ks[9], (DEPTH, GLA_D_KEY), jnp.float32)
    gla_norm_w = 1.0 + 0.02 * nrm(ks[10], (DEPTH, GLA_HEAD_V), jnp.float32)
    w_ssd_branch = nrm(ks[11], (DEPTH, SSD_D_INNER, D_MODEL), jnp.float32) * SSD_D_INNER ** -0.5
    w_gla_branch = nrm(ks[12], (DEPTH, GLA_D_VALUE, D_MODEL), jnp.float32) * GLA_D_VALUE ** -0.5
    gate_bias = 0.02 * nrm(ks[13], (DEPTH, 2, D_MODEL), jnp.float32)
    w_out = nrm(ks[14], (DEPTH, D_MODEL, D_MODEL), jnp.float32) * (D_MODEL ** -0.5 * BETA)
    ln1_g = 1.0 + 0.02 * nrm(ks[15], (DEPTH, D_MODEL), jnp.float32)
    ln1_b = 0.02 * nrm(ks[16], (DEPTH, D_MODEL), jnp.float32)
    w_up = nrm(ks[17], (DEPTH, D_MODEL, D_FF), jnp.float32) * D_MODEL ** -0.5
    w_down = nrm(ks[18], (DEPTH, D_FF, D_MODEL), jnp.float32) * (D_FF ** -0.5 * BETA)
    ln2_g = 1.0 + 0.02 * nrm(ks[19], (DEPTH, D_MODEL), jnp.float32)
    ln2_b = 0.02 * nrm(ks[20], (DEPTH, D_MODEL), jnp.float32)
    return {'x': x, 'w_in': w_in, 'ssd_conv_w': ssd_conv_w, 'ssd_conv_b': ssd_conv_b,
            'ssd_dt_bias': ssd_dt_bias, 'ssd_A_log': ssd_A_log, 'ssd_D': ssd_D, 'ssd_norm_w': ssd_norm_w,
            'gla_gk_w': gla_gk_w, 'gla_gk_b': gla_gk_b, 'gla_norm_w': gla_norm_w,
            'w_ssd_branch': w_ssd_branch, 'w_gla_branch': w_gla_branch, 'gate_bias': gate_bias,
            'w_out': w_out, 'ln1_g': ln1_g, 'ln1_b': ln1_b, 'w_up': w_up, 'w_down': w_down,
            'ln2_g': ln2_g, 'ln2_b': ln2_b}


def reference(x, w_in, ssd_conv_w, ssd_conv_b, ssd_dt_bias, ssd_A_log, ssd_D, ssd_norm_w,
              gla_gk_w, gla_gk_b, gla_norm_w, w_ssd_branch, w_gla_branch, gate_bias,
              w_out, ln1_g, ln1_b, w_up, w_down, ln2_g, ln2_b):
    for l in range(DEPTH):
        mix = hybrid_mixer(x, w_in[l], ssd_conv_w[l], ssd_conv_b[l], ssd_dt_bias[l], ssd_A_log[l],
                           ssd_D[l], ssd_norm_w[l], gla_gk_w[l], gla_gk_b[l], gla_norm_w[l],
                           w_ssd_branch[l], w_gla_branch[l], gate_bias[l], w_out[l])
        x = layer_norm(ALPHA * x + mix, ln1_g[l], ln1_b[l])
        x = layer_norm(ALPHA * x + squared_relu_mlp(x, w_up[l], w_down[l]), ln2_g[l], ln2_b[l])
    return x
```

```python
import contextlib
import numpy as np
import concourse.bass as bass
import concourse.mybir as mybir
from concourse.bass_utils import run_bass_kernel_spmd

F32 = mybir.dt.float32
BF16 = mybir.dt.bfloat16
AF = mybir.ActivationFunctionType
ALU = mybir.AluOpType

N_CORES_USED = 2


class Cfg:
    def __init__(self, d_model=2048, seq=8192, depth=4, ssd_heads=64, ssd_groups=8, gla_heads=4):
        self.D = d_model
        self.L = seq
        self.depth = depth
        self.P = 64
        self.N = 128
        self.H = ssd_heads
        self.G = ssd_groups
        self.DI = self.H * self.P
        self.CONV = self.DI + 2 * self.G * self.N
        self.GH = gla_heads
        self.DK = d_model // 2
        self.DV = d_model
        self.R = 16
        self.FF = 4 * d_model
        self.splits = (self.DI, self.CONV, self.H, self.DK, self.DK, self.DV, self.DV, self.R, self.D, self.D)
        self.DIN = sum(self.splits)
        self.alpha = (2.0 * depth) ** 0.25


class Buf:
    __slots__ = ("name", "w", "r")

    def __init__(self, name=""):
        self.name = name
        self.w = None
        self.r = []


class Sched:
    SEM_ROT = 30000

    def __init__(self, nc, stack, n_dma_sems=40):
        self.nc = nc
        self.stack = stack
        self.eng = {"pe": nc.tensor, "act": nc.scalar, "dve": nc.vector, "pool": nc.gpsimd, "sp": nc.sync}
        self.sem = {}
        self.cnt = {}
        self.nsem = 0
        self.pe_sems = set()
        for e in ("pe", "act", "dve", "pool"):
            self._new_sem(e)
        self.dma_sems = [stack.enter_context(nc.semaphore(f"dsem{i}")) for i in range(n_dma_sems)]
        self.dma_cnt = [0] * n_dma_sems
        self.dma_next = 0
        self.waited = {e: {} for e in self.eng}
        self.semobj = {}
        self.sw = {}
        self.n_ins = {e: 0 for e in self.eng}

    def _new_sem(self, e):
        s = self.stack.enter_context(self.nc.semaphore(f"sem_{e}_{self.nsem}"))
        self.nsem += 1
        self.sem[e] = s
        self.cnt[e] = 0
        if e == "pe":
            self.pe_sems.add(id(s))

    def _wait(self, e, tok):
        if tok is None:
            return
        sem, val = tok
        d = self.waited[e]
        k = id(sem)
        if e == "pe" and k in self.pe_sems:
            return
        if d.get(k, 0) >= val:
            return
        d[k] = val
        self.eng[e].wait_ge(sem, val)

    def _deps(self, e, reads, writes, skip_same_pe=False):
        for b in reads:
            self._wait(e, b.w)
        for b in writes:
            self._wait(e, b.w)
            for t in b.r:
                self._wait(e, t)

    def _commit(self, tok, reads, writes):
        for b in reads:
            b.r.append(tok)
            if len(b.r) > 12:
                latest = {}
                for (s, v) in b.r:
                    if id(s) not in latest or latest[id(s)][1] < v:
                        latest[id(s)] = (s, v)
                b.r = list(latest.values())
        for b in writes:
            b.w = tok
            b.r = []

    def op(self, e, fn, reads=(), writes=(), inc=True):
        if self.cnt[e] >= self.SEM_ROT:
            self._new_sem(e)
        self._deps(e, reads, writes)
        ins = fn()
        self.n_ins[e] += 1
        if inc:
            self.cnt[e] += 1
            ins.then_inc(self.sem[e], 1)
            tok = (self.sem[e], self.cnt[e])
        else:
            tok = (self.sem[e], self.cnt[e] + 1)
        self._commit(tok, reads, writes)
        return ins

    def dma(self, q, out, in_, reads=(), writes=(), **kw):
        i = self.dma_next
        self.dma_next = (self.dma_next + 1) % len(self.dma_sems)
        sem = self.dma_sems[i]
        if self.dma_cnt[i] > 0:
            self._wait(q, (sem, 16 * self.dma_cnt[i]))
        self._deps(q, reads, writes)
        ins = self.eng[q].dma_start(out=out, in_=in_, **kw)
        self.n_ins[q] += 1
        self.dma_cnt[i] += 1
        ins.then_inc(sem, 16)
        tok = (sem, 16 * self.dma_cnt[i])
        self._commit(tok, reads, writes)
        return tok

    def dma_sw(self, key, out, in_, reads=(), writes=(), **kw):
        sem = self.sw.get(key)
        first = sem is None
        if first:
            sem = self.stack.enter_context(self.nc.semaphore(f"swsem{len(self.sw)}"))
            self.sw[key] = sem
        self._deps("pool", reads, writes)
        if not first:
            self.eng["pool"].wait_ge(sem, 16)
            self.eng["pool"].sem_clear(sem)
            for e in self.waited:
                self.waited[e].pop(id(sem), None)
        ins = self.eng["pool"].dma_start(out=out, in_=in_, **kw)
        self.n_ins["pool"] += 1
        ins.then_inc(sem, 16)
        tok = (sem, 16)
        self._commit(tok, reads, writes)
        return tok

    def wait_all(self, e, bufs):
        for b in bufs:
            self._wait(e, b.w)
            for t in b.r:
                self._wait(e, t)


class Ctx:
    def __init__(self, nc, st, cfg):
        self.nc, self.st, self.cfg = nc, st, cfg
        self.S = Sched(nc, st)
        self._n = 0
        self.ps = [st.enter_context(nc.psum_tensor(f"ps{i}", [128, 512], F32)) for i in range(8)]
        self.bps = [Buf(f"ps{i}") for i in range(8)]
        self.NW = 3
        self.wbase = {}
        self.WBF = None
        self.wb = None
        self.bwb = None
        self.wi = 0
        self.psi = 0
        self.ev = 0

    def sb(self, shape, dt):
        self._n += 1
        return self.st.enter_context(self.nc.sbuf_tensor(f"t{self._n}", list(shape), dt))

    def dram(self, shape, dt):
        self._n += 1
        return self.nc.dram_tensor(f"scr{self._n}", list(shape), dt, kind="Internal").ap()


def wblock_list(K, n0, n1):
    KT = K // 128
    out = []
    for c0 in range(n0, n1, 512):
        cw = min(512, n1 - c0)
        for k0 in range(0, KT, 16):
            out.append((k0, min(16, KT - k0), c0, cw))
    return out


def wstream_begin(C, entries):
    C.wq = list(entries)
    C.wq_issued = 0
    C.wq_pos = 0


def wstream_entries(key, K, n0, n1):
    return [(key, j, nk, cw) for j, (k0, nk, c0, cw) in enumerate(wblock_list(K, n0, n1))]


def wnext(C, key, j):
    pos = C.wq_pos
    assert C.wq[pos][0] == key and C.wq[pos][1] == j, (C.wq[pos], key, j)
    while C.wq_issued < min(len(C.wq), pos + C.NW):
        k_, j_, nk, cw = C.wq[C.wq_issued]
        w = C.wq_issued % C.NW
        blk = C.wbase[k_] + j_
        src = C.WBF[blk][:, 0:nk * cw].rearrange("p (a b) -> p a b", b=cw)
        C.S.dma("sp", C.wb[w][:, 0:nk, 0:cw], src, writes=[C.bwb[w]])
        C.wq_issued += 1
    C.wq_pos += 1
    return pos % C.NW


def wload(C, key, j, w, nk, cw):
    blk = C.wbase[key] + j
    src = C.WBF[blk][:, 0:nk * cw].rearrange("p (a b) -> p a b", b=cw)
    C.S.dma("act", C.wb[w][:, 0:nk, 0:cw], src, writes=[C.bwb[w]])


def gemm_fm(C, W, K, n0, n1, rhs_tiles, brhs, T, epilogue, ps_banks=(0, 1, 2, 3, 4, 5, 6, 7)):
    nc, S = C.nc, C.S
    KT = K // 128
    assert K % 128 == 0 and T <= 512
    jblk = 0
    for c0 in range(n0, n1, 512):
        cw = min(512, n1 - c0)
        kblocks = [(k0, min(16, KT - k0)) for k0 in range(0, KT, 16)]
        ntiles = [(c0 + j, min(128, c0 + cw - (c0 + j))) for j in range(0, cw, 128)]
        banks = [ps_banks[(C.psi + i) % len(ps_banks)] for i in range(len(ntiles))]
        C.psi += len(ntiles)
        for bi, (k0, nk) in enumerate(kblocks):
            w = wnext(C, W, jblk)
            jblk += 1
            for ti, (col, ncol) in enumerate(ntiles):
                b = banks[ti]
                for kk in range(nk):
                    kt = k0 + kk
                    first = (kt == 0)
                    last = (kt == KT - 1)
                    lo = col - c0
                    S.op("pe", lambda: nc.tensor.matmul(C.ps[b][0:ncol, 0:T], C.wb[w][:, kk, lo:lo + ncol],
                                                        rhs_tiles(kt), start=first, stop=last),
                         reads=[C.bwb[w], brhs], writes=[C.bps[b]], inc=last)
        for ti, (col, ncol) in enumerate(ntiles):
            b = banks[ti]
            epilogue(col, ncol, C.ps[b][0:ncol, 0:T], C.bps[b])


def layernorm_fm(C, v32, bv, KT, T, gam, bet, out32, bo32, outbf, bobf, ones_bf, bones, eps=1e-5):
    nc, S = C.nc, C.S
    D = KT * 128
    vb = C.ln_vb; sq = C.ln_sq; st = C.ln_st
    S.op("dve", lambda: nc.vector.tensor_copy(vb[:, 0:KT, 0:T], v32[:, 0:KT, 0:T]), reads=[bv], writes=[C.b_ln_vb])
    S.op("pool", lambda: nc.gpsimd.tensor_tensor(sq[:, 0:KT, 0:T], v32[:, 0:KT, 0:T], v32[:, 0:KT, 0:T], ALU.mult),
         reads=[bv], writes=[C.b_ln_sq])
    b_s, b_q = 4, 5
    for kt in range(KT):
        S.op("pe", lambda: nc.tensor.matmul(C.ps[b_s][:, 0:T], ones_bf[:, 0:128], vb[:, kt, 0:T], start=(kt == 0), stop=(kt == KT - 1)),
             reads=[bones, C.b_ln_vb], writes=[C.bps[b_s]], inc=(kt == KT - 1))
    for kt in range(KT):
        S.op("pe", lambda: nc.tensor.matmul(C.ps[b_q][:, 0:T], ones_bf[:, 0:128], sq[:, kt, 0:T], start=(kt == 0), stop=(kt == KT - 1)),
             reads=[bones, C.b_ln_sq], writes=[C.bps[b_q]], inc=(kt == KT - 1))
    S.op("act", lambda: nc.scalar.mul(st[:, 0, 0:T], C.ps[b_s][:, 0:T], 1.0 / D), reads=[C.bps[b_s]], writes=[C.b_ln_st])
    S.op("dve", lambda: nc.vector.tensor_tensor(st[:, 2, 0:T], st[:, 0, 0:T], st[:, 0, 0:T], ALU.mult), reads=[C.b_ln_st], writes=[C.b_ln_st])
    S.op("dve", lambda: nc.vector.scalar_tensor_tensor(st[:, 1, 0:T], C.ps[b_q][:, 0:T], 1.0 / D, st[:, 2, 0:T], ALU.mult, ALU.subtract),
         reads=[C.bps[b_q], C.b_ln_st], writes=[C.b_ln_st])
    S.op("act", lambda: nc.scalar.activation(st[:, 1, 0:T], st[:, 1, 0:T], AF.Sqrt, bias=eps), reads=[C.b_ln_st], writes=[C.b_ln_st])
    S.op("dve", lambda: nc.vector.reciprocal(st[:, 1, 0:T], st[:, 1, 0:T]), reads=[C.b_ln_st], writes=[C.b_ln_st])
    for kt in range(KT):
        e = "dve" if kt % 2 == 0 else "pool"
        eng = nc.vector if e == "dve" else nc.gpsimd
        S.op(e, lambda: eng.tensor_tensor(out32[:, kt, 0:T], v32[:, kt, 0:T], st[:, 0, 0:T], ALU.subtract), reads=[bv, C.b_ln_st], writes=[bo32])
        S.op(e, lambda: eng.tensor_tensor(out32[:, kt, 0:T], out32[:, kt, 0:T], st[:, 1, 0:T], ALU.mult), reads=[bo32, C.b_ln_st], writes=[bo32])
        S.op("act", lambda: nc.scalar.activation(out32[:, kt, 0:T], out32[:, kt, 0:T], AF.Identity, bias=bet[:, kt:kt + 1], scale=gam[:, kt:kt + 1]),
             reads=[bo32], writes=[bo32])
        S.op("act", lambda: nc.scalar.copy(outbf[:, kt, 0:T], out32[:, kt, 0:T]), reads=[bo32], writes=[bobf])


def ln_alloc(C, KT, T):
    C.ln_vb = C.sb([128, KT, T], BF16); C.b_ln_vb = Buf()
    C.ln_sq = C.sb([128, KT, T], BF16); C.b_ln_sq = Buf()
    C.ln_st = C.sb([128, 3, T], F32); C.b_ln_st = Buf()


def _bank(C):
    b = C.ev % 8
    C.ev += 1
    return b


def _bc_last(ap, shape):
    return ap.unsqueeze(2).to_broadcast(list(shape))


def _bc_mid(ap, shape):
    return ap.unsqueeze(1).to_broadcast(list(shape))


def transpose_tiles(C, src_tile, ntiles, bsrc, dst2d, bdst, ident_bf, bident, evac=None):
    nc, S = C.nc, C.S
    for t0 in range(0, ntiles, 8):
        n = min(8, ntiles - t0)
        b = _bank(C)
        psb = C.ps[b][:].bitcast(BF16)
        for t in range(n):
            S.op("pe", lambda: nc.tensor.transpose(psb[:, t * 128:(t + 1) * 128], src_tile(t0 + t), ident_bf[:]),
                 reads=[bsrc, bident], writes=[C.bps[b]], inc=(t == n - 1))
        if evac is not None:
            evac(t0, n, psb, C.bps[b])
        else:
            e = "act" if (C.ev % 2) else "dve"
            if e == "act":
                S.op("act", lambda: nc.scalar.copy(dst2d[:, t0 * 128:(t0 + n) * 128], psb[:, 0:n * 128]), reads=[C.bps[b]], writes=[bdst])
            else:
                S.op("dve", lambda: nc.vector.tensor_copy(dst2d[:, t0 * 128:(t0 + n) * 128], psb[:, 0:n * 128]), reads=[C.bps[b]], writes=[bdst])


def ssd_pass(C, K, lp, ZT, XBCT, DTT, YO, NCH):
    nc, S, cfg = C.nc, C.S, C.cfg
    H, G, P, N = cfg.H, cfg.G, cfg.P, cfg.N
    Hg = H // G
    assert Hg in (4, 8) and N == 128 and P == 64
    DI = cfg.DI
    NXT = DI // 128
    NCT = NXT + 2 * G
    GW = Hg * P
    NSL = H // 8
    I_bf, bI = K["ident_bf"], K["b"]
    U, ones32 = K["U"], K["ones32"]
    bK = K["b"]
    dtb, IA, Drep, normw, bl = lp["dtb"], lp["IA"], lp["Drep"], lp["normw"], lp["b"]
    sb = C.sb
    zt = [sb([128, NXT, 128], BF16) for _ in range(2)]; bzt = [Buf(), Buf()]
    dtr = [sb([H, 128], F32) for _ in range(2)]; bdtr = [Buf(), Buf()]
    xcbs = [sb([128, NCT, 128], BF16) for _ in range(2)]; bxcbs = [Buf(), Buf()]
    xs_tok = sb([128, DI], BF16); bxs = Buf()
    B_tok = sb([128, G * 128], BF16); bB = Buf()
    z_tok = sb([128, DI], BF16); bz = Buf()
    e1 = sb([H, 128], F32); be1 = Buf()
    dtT = sb([H, 128], F32); bdtT = Buf()
    dtk = sb([128, 2 * H], F32); bdtk = Buf()
    acs = sb([128, 2 * H], F32); bacs = Buf()
    dte = sb([128, H], F32); bdte = Buf()
    ea = sb([128, 2 * H], F32); bea = Buf()
    xdt = sb([128, DI], BF16); bxdt = Buf()
    xw = sb([128, DI], BF16); bxw = Buf()
    cbm = sb([128, G, 128], F32); bcbm = Buf()
    NR = 4
    R1 = [sb([128, 4, 128], F32) for _ in range(NR)]; bR1 = [Buf() for _ in range(NR)]
    tt = [sb([128, 4, 128], F32) for _ in range(NR)]; btt = [Buf() for _ in range(NR)]
    Mt = [sb([128, 4, 128], BF16) for _ in range(NR)]; bMt = [Buf() for _ in range(NR)]
    y = sb([128, DI], F32); by = Buf()
    y2 = sb([128, 512], F32); by2 = Buf()
    S32 = sb([128, DI], F32); bS32 = Buf()
    Sbf = sb([128, DI], BF16); bSbf = Buf()
    szs = [sb([128, 512], F32) for _ in range(2)]; bszs = [Buf(), Buf()]
    junk = sb([128, GW], F32); bjunk = Buf()
    ssq = sb([128, G], F32); bssq = Buf()
    yn, byn = xw, bxw
    yT1 = sb([128, NXT, 128], BF16); byT1 = Buf()
    yT = [yT1, yT1]; byT = [byT1, byT1]

    S.op("dve", lambda: nc.vector.memset(S32[:], 0.0), writes=[bS32])
    S.op("dve", lambda: nc.vector.memset(Sbf[:], 0.0), writes=[bSbf])
    def load(c):
        s = c % 2
        S.dma("sp", xcbs[s][:], XBCT[c], writes=[bxcbs[s]])
        S.dma("sp", zt[s][:], ZT[c], writes=[bzt[s]])
        S.dma("sp", dtr[s][:], DTT[c], writes=[bdtr[s]])

    load(0)
    for c in range(NCH):
        s = c % 2
        if c + 1 < NCH:
            load(c + 1)
        xcb, bxcb = xcbs[s], bxcbs[s]
        transpose_tiles(C, lambda t: xcb[:, t, :], NXT, bxcb, xs_tok, bxs, I_bf, bI)
        transpose_tiles(C, lambda t: xcb[:, NXT + t, :], G, bxcb, B_tok, bB, I_bf, bI)
        transpose_tiles(C, lambda t: zt[s][:, t, :], NXT, bzt[s], z_tok, bz, I_bf, bI)
        S.op("act", lambda: nc.scalar.activation(e1[:], dtr[s][:], AF.Exp, bias=dtb[0:H, 0:1]), reads=[bdtr[s], bl], writes=[be1])
        S.op("act", lambda: nc.scalar.activation(dtT[:], e1[:], AF.Ln, bias=1.0), reads=[be1], writes=[bdtT])
        b = _bank(C)
        S.op("pe", lambda: nc.tensor.matmul(C.ps[b][:, 0:2 * H], dtT[:], IA[0:H, 0:2 * H], start=True, stop=True),
             reads=[bdtT, bl], writes=[C.bps[b]])
        S.op("dve", lambda: nc.vector.tensor_copy(dtk[:], C.ps[b][:, 0:2 * H]), reads=[C.bps[b]], writes=[bdtk])
        dt_tok, dtA = dtk[:, 0:H], dtk[:, H:2 * H]
        b = _bank(C)
        S.op("pe", lambda: nc.tensor.matmul(C.ps[b][:, 0:H], U[:], dtA, start=True, stop=True), reads=[bK, bdtk], writes=[C.bps[b]], inc=False)
        S.op("pe", lambda: nc.tensor.matmul(C.ps[b][:, H:2 * H], ones32[:], dtA, start=True, stop=True), reads=[bK, bdtk], writes=[C.bps[b]])
        S.op("dve", lambda: nc.vector.tensor_copy(acs[:], C.ps[b][:, 0:2 * H]), reads=[C.bps[b]], writes=[bacs])
        a_cs, a_tot = acs[:, 0:H], acs[:, H:2 * H]
        S.op("dve", lambda: nc.vector.tensor_tensor(dte[:], a_tot, a_cs, ALU.subtract), reads=[bacs], writes=[bdte])
        S.op("act", lambda: nc.scalar.activation(dte[:], dte[:], AF.Exp), reads=[bdte], writes=[bdte])
        S.op("dve", lambda: nc.vector.tensor_tensor(dte[:], dte[:], dt_tok, ALU.mult), reads=[bdte, bdtk], writes=[bdte])
        S.op("act", lambda: nc.scalar.activation(ea[:], acs[:], AF.Exp), reads=[bacs], writes=[bea])
        shp3 = [128, H, P]
        S.op("dve", lambda: nc.vector.tensor_tensor(xdt[:].rearrange("p (h q) -> p h q", q=P), xs_tok[:].rearrange("p (h q) -> p h q", q=P),
                                                    _bc_last(dt_tok, shp3), ALU.mult), reads=[bxs, bdtk], writes=[bxdt])
        S.op("pool", lambda: nc.gpsimd.tensor_tensor(xw[:].rearrange("p (h q) -> p h q", q=P), xs_tok[:].rearrange("p (h q) -> p h q", q=P),
                                                     _bc_last(dte[:], shp3), ALU.mult), reads=[bxs, bdte], writes=[bxw])
        for g0 in range(0, G, 4):
            b = _bank(C)
            for g in range(g0, g0 + 4):
                S.op("pe", lambda: nc.tensor.matmul(C.ps[b][:, (g - g0) * 128:(g - g0 + 1) * 128], xcb[:, NXT + g, :], xcb[:, NXT + G + g, :],
                                                    start=True, stop=True), reads=[bxcb], writes=[C.bps[b]], inc=(g == g0 + 3))
            S.op("dve", lambda: nc.vector.tensor_tensor(cbm[:, g0:g0 + 4, :], C.ps[b][:].rearrange("p (a i) -> p a i", i=128),
                                                        _bc_mid(U[:], [128, 4, 128]), ALU.mult), reads=[C.bps[b], bK], writes=[bcbm])
        for sl in range(NSL):
            h8 = slice(sl * 8, sl * 8 + 8)
            cols = slice(sl * 512, sl * 512 + 512)
            byo = _bank(C)
            byd = _bank(C)
            glist = list(range(sl * 8 // Hg, (sl * 8 + 8) // Hg))
            for gi, g in enumerate(glist):
                S.op("pe", lambda: nc.tensor.matmul(C.ps[byo][:, gi * GW:(gi + 1) * GW], xcb[:, NXT + G + g, :], Sbf[:, g * GW:(g + 1) * GW],
                                                    start=True, stop=True), reads=[bxcb, bSbf], writes=[C.bps[byo]], inc=(gi == len(glist) - 1))
            for q4 in range(2):
                h0 = sl * 8 + q4 * 4
                r = (sl * 2 + q4) % NR
                g = h0 // Hg
                sh4 = [128, 4, 128]
                S.op("pool", lambda: nc.gpsimd.tensor_tensor(R1[r][:], _bc_mid(U[:], sh4), _bc_last(dtA[:, h0:h0 + 4], sh4), ALU.mult),
                     reads=[bK, bdtk], writes=[bR1[r]])
                b = _bank(C)
                S.op("pe", lambda: nc.tensor.matmul(C.ps[b][:, 0:512], ones32[:], R1[r][:].rearrange("p a i -> p (a i)"), start=True, stop=True),
                     reads=[bK, bR1[r]], writes=[C.bps[b]])
                acs_b = _bc_last(a_cs[:, h0:h0 + 4], sh4)
                S.op("dve", lambda: nc.vector.tensor_tensor(tt[r][:], C.ps[b][:].rearrange("p (a i) -> p a i", i=128), acs_b, ALU.min),
                     reads=[C.bps[b], bacs], writes=[btt[r]])
                S.op("dve", lambda: nc.vector.tensor_tensor(tt[r][:], tt[r][:], acs_b, ALU.subtract), reads=[btt[r], bacs], writes=[btt[r]])
                S.op("act", lambda: nc.scalar.activation(tt[r][:], tt[r][:], AF.Exp), reads=[btt[r]], writes=[btt[r]])
                S.op("dve", lambda: nc.vector.tensor_tensor(Mt[r][:], tt[r][:], _bc_mid(cbm[:, g, :], sh4), ALU.mult),
                     reads=[btt[r], bcbm], writes=[bMt[r]])
                for k in range(4):
                    h = h0 + k
                    co = (h - sl * 8) * P
                    S.op("pe", lambda: nc.tensor.matmul(C.ps[byd][:, co:co + P], Mt[r][:, k, :], xdt[:, h * P:(h + 1) * P], start=True, stop=True),
                         reads=[bMt[r], bxdt], writes=[C.bps[byd]], inc=(k == 3))
            sh8 = [128, 8, P]
            v3 = lambda ap: ap.rearrange("p (h q) -> p h q", q=P)
            S.op("dve", lambda: nc.vector.tensor_tensor(v3(y[:, cols]), v3(C.ps[byo][:, 0:512]), _bc_last(ea[:, h8], sh8), ALU.mult),
                 reads=[C.bps[byo], bea], writes=[by])
            S.op("dve", lambda: nc.vector.tensor_tensor(y[:, cols], y[:, cols], C.ps[byd][:, 0:512], ALU.add), reads=[by, C.bps[byd]], writes=[by])
            S.op("pool", lambda: nc.gpsimd.tensor_tensor(v3(y2[:]), v3(xs_tok[:, cols]), _bc_last(Drep[:, h8], sh8), ALU.mult),
                 reads=[bxs, bl], writes=[by2])
            S.op("pool", lambda: nc.gpsimd.tensor_tensor(y[:, cols], y[:, cols], y2[:], ALU.add), reads=[by, by2], writes=[by])
        for g in range(G):
            gc = slice(g * GW, (g + 1) * GW)
            b = _bank(C)
            S.op("pe", lambda: nc.tensor.matmul(C.ps[b][:, 0:GW], B_tok[:, g * 128:(g + 1) * 128], xw[:, gc], start=True, stop=True),
                 reads=[bB, bxw], writes=[C.bps[b]])
            vg = lambda ap: ap.rearrange("p (h q) -> p h q", q=P)
            S.op("dve", lambda: nc.vector.tensor_tensor(vg(S32[:, gc]), vg(S32[:, gc]), _bc_last(ea[:, H + g * Hg:H + (g + 1) * Hg], [128, Hg, P]), ALU.mult),
                 reads=[bS32, bea], writes=[bS32])
            S.op("dve", lambda: nc.vector.tensor_tensor(S32[:, gc], S32[:, gc], C.ps[b][:, 0:GW], ALU.add), reads=[bS32, C.bps[b]], writes=[bS32])
            S.op("act", lambda: nc.scalar.copy(Sbf[:, gc], S32[:, gc]), reads=[bS32], writes=[bSbf])
        for sl in range(NSL):
            cols = slice(sl * 512, sl * 512 + 512)
            S.op("act", lambda: nc.scalar.activation(szs[sl % 2][:], z_tok[:, cols], AF.Silu), reads=[bz], writes=[bszs[sl % 2]])
            S.op("dve", lambda: nc.vector.tensor_tensor(y[:, cols], y[:, cols], szs[sl % 2][:], ALU.mult), reads=[bszs[sl % 2], by], writes=[by])
        for g in range(G):
            S.op("act", lambda: nc.scalar.activation(junk[:], y[:, g * GW:(g + 1) * GW], AF.Square, accum_out=ssq[:, g:g + 1]),
                 reads=[by], writes=[bjunk, bssq])
        S.op("act", lambda: nc.scalar.activation(ssq[:], ssq[:], AF.Sqrt, bias=1e-5, scale=1.0 / GW), reads=[bssq], writes=[bssq])
        S.op("dve", lambda: nc.vector.reciprocal(ssq[:], ssq[:]), reads=[bssq], writes=[bssq])
        S.op("dve", lambda: nc.vector.tensor_tensor(yn[:].rearrange("p (g q) -> p g q", q=GW), y[:].rearrange("p (g q) -> p g q", q=GW),
                                                    _bc_last(ssq[:], [128, G, GW]), ALU.mult), reads=[by, bssq], writes=[byn])

        def evac(t0, n, psb, bps):
            S.op("dve", lambda: nc.vector.tensor_tensor(yT[s][:, t0:t0 + n, :], psb[:, 0:n * 128].rearrange("p (a i) -> p a i", i=128),
                                                        _bc_last(normw[:, t0:t0 + n], [128, n, 128]), ALU.mult),
                 reads=[bps, bl], writes=[byT[s]])
        transpose_tiles(C, lambda t: yn[:, t * 128:(t + 1) * 128], NXT, byn, None, None, I_bf, bI, evac=evac)
        S.dma("sp", YO[c][:, 0:NXT, :], yT[s][:], reads=[byT[s]])


def gla_pass(C, K, lp, QT, KTd, VT, GT, GKL, YO, NCH, tile_off):
    nc, S, cfg = C.nc, C.S, C.cfg
    GH, DK, DV, R = cfg.GH, cfg.DK, cfg.DV, cfg.R
    HK, HV = DK // GH, DV // GH
    NKT, NVT = DK // 128, DV // 128
    KPH = HK // 128
    assert HK % 128 == 0 and HV <= 512 and R == 16
    scale = float(HK) ** -0.5
    I_bf, bK = K["ident_bf"], K["b"]
    U, ones32, SU = K["U"], K["ones32"], K["SU"]
    gkw, gnw, bl = lp["gkw"], lp["gnw"], lp["b"]
    sb = C.sb
    qT = [sb([128, NKT, 128], BF16) for _ in range(2)]; bq = [Buf(), Buf()]
    kT = [sb([128, NKT, 128], BF16) for _ in range(2)]; bk = [Buf(), Buf()]
    vT = [sb([128, NVT, 128], BF16) for _ in range(2)]; bv = [Buf(), Buf()]
    gT = [sb([128, NVT, 128], BF16) for _ in range(2)]; bg = [Buf(), Buf()]
    gl = [sb([32, 128], F32) for _ in range(2)]; bgl = [Buf(), Buf()]
    sp = sb([128, DK], F32); bsp = Buf()
    eg = sb([128, NKT, 128], F32); beg = Buf()
    ek = sb([128, NKT, 128], F32); bek = Buf()
    qg = sb([128, NKT, 128], BF16); bqg = Buf()
    kg = sb([128, NKT, 128], BF16); bkg = Buf()
    k_tok = sb([128, DK], BF16); bkt = Buf()
    v_tok = sb([128, DV], BF16); bvt = Buf()
    g_tok = sb([128, DV], BF16); bgt = Buf()
    erev = sb([128, DK], F32); berev = Buf()
    kd = sb([128, DK], BF16); bkd = Buf()
    attm = sb([128, GH, 128], BF16); battm = Buf()
    Sg32 = sb([128, NKT, HV], F32); bSg = Buf()
    Sgb = sb([128, NKT, HV], BF16); bSgb = Buf()
    o32 = sb([128, DV], F32); bo = Buf()
    sg = sb([128, DV], F32); bsg = Buf()
    junk = sb([128, HV], F32); bjunk = Buf()
    ssq = sb([128, GH], F32); bssq = Buf()
    on = sb([128, DV], BF16); bon = Buf()
    oT = [sb([128, NVT, 128], BF16) for _ in range(2)]; boT = [Buf(), Buf()]

    S.op("dve", lambda: nc.vector.memset(Sg32[:], 0.0), writes=[bSg])
    S.op("dve", lambda: nc.vector.memset(Sgb[:], 0.0), writes=[bSgb])
    for s in range(2):
        S.op("pool", lambda: nc.gpsimd.memset(gl[s][:], 1.0), writes=[bgl[s]])

    def load(c):
        s = c % 2
        S.dma("sp", qT[s][:], QT[c], writes=[bq[s]])
        S.dma("sp", kT[s][:], KTd[c], writes=[bk[s]])
        S.dma("sp", vT[s][:], VT[c], writes=[bv[s]])
        S.dma("sp", gT[s][:], GT[c], writes=[bg[s]])
        S.dma("sp", gl[s][0:16, :], GKL[c], writes=[bgl[s]])

    load(0)
    for c in range(NCH):
        s = c % 2
        if c + 1 < NCH:
            load(c + 1)
        for n0 in range(0, DK, 512):
            nw = min(512, DK - n0)
            b = _bank(C)
            S.op("pe", lambda: nc.tensor.matmul(C.ps[b][:, 0:nw], gl[s][0:17, :], gkw[0:17, n0:n0 + nw], start=True, stop=True),
                 reads=[bgl[s], bl], writes=[C.bps[b]])
            S.op("act", lambda: nc.scalar.activation(sp[:, n0:n0 + nw], C.ps[b][:, 0:nw], AF.Exp, scale=-1.0), reads=[C.bps[b]], writes=[bsp])
        S.op("act", lambda: nc.scalar.activation(sp[:], sp[:], AF.Ln, bias=1.0), reads=[bsp], writes=[bsp])
        if getattr(cfg, "gla_stop", 99) == 1:
            return
        for t0 in range(0, NKT, 4):
            n = min(4, NKT - t0)
            b = _bank(C)
            for t in range(n):
                S.op("pe", lambda: nc.tensor.matmul(C.ps[b][:, t * 128:(t + 1) * 128], sp[:, (t0 + t) * 128:(t0 + t + 1) * 128], U[:], start=True, stop=True),
                     reads=[bsp, bK], writes=[C.bps[b]], inc=(t == n - 1))
            pv = C.ps[b][:, 0:n * 128].rearrange("p (a i) -> p a i", i=128)
            S.op("act", lambda: nc.scalar.activation(eg[:, t0:t0 + n, :], pv, AF.Exp, scale=-1.0 / 16), reads=[C.bps[b]], writes=[beg])
            S.op("act", lambda: nc.scalar.activation(ek[:, t0:t0 + n, :], pv, AF.Exp, scale=1.0 / 16), reads=[C.bps[b]], writes=[bek])
        S.op("dve", lambda: nc.vector.scalar_tensor_tensor(qg[:], qT[s][:], scale, eg[:], ALU.mult, ALU.mult), reads=[bq[s], beg], writes=[bqg])
        S.op("pool", lambda: nc.gpsimd.tensor_tensor(kg[:], kT[s][:], ek[:], ALU.mult), reads=[bk[s], bek], writes=[bkg])
        if getattr(cfg, "gla_stop", 99) == 2:
            return
        transpose_tiles(C, lambda t: kT[s][:, t, :], NKT, bk[s], k_tok, bkt, I_bf, bK)
        transpose_tiles(C, lambda t: vT[s][:, t, :], NVT, bv[s], v_tok, bvt, I_bf, bK)
        transpose_tiles(C, lambda t: gT[s][:, t, :], NVT, bg[s], g_tok, bgt, I_bf, bK)
        if getattr(cfg, "gla_stop", 99) == 3:
            return
        for n0 in range(0, DK, 512):
            nw = min(512, DK - n0)
            b = _bank(C)
            S.op("pe", lambda: nc.tensor.matmul(C.ps[b][:, 0:nw], SU[:], sp[:, n0:n0 + nw], start=True, stop=True), reads=[bK, bsp], writes=[C.bps[b]])
            S.op("act", lambda: nc.scalar.activation(erev[:, n0:n0 + nw], C.ps[b][:, 0:nw], AF.Exp, scale=-1.0 / 16), reads=[C.bps[b]], writes=[berev])
        S.op("dve", lambda: nc.vector.tensor_tensor(kd[:], k_tok[:], erev[:], ALU.mult), reads=[bkt, berev], writes=[bkd])
        if getattr(cfg, "gla_stop", 99) == 4:
            return
        for h0 in range(0, GH, 4):
            nh = min(4, GH - h0)
            b = _bank(C)
            for hh in range(nh):
                h = h0 + hh
                for kk in range(KPH):
                    t = h * KPH + kk
                    S.op("pe", lambda: nc.tensor.matmul(C.ps[b][:, hh * 128:(hh + 1) * 128], kg[:, t, :], qg[:, t, :], start=(kk == 0), stop=(kk == KPH - 1)),
                         reads=[bkg, bqg], writes=[C.bps[b]], inc=(hh == nh - 1 and kk == KPH - 1))
            S.op("dve", lambda: nc.vector.tensor_tensor(attm[:, h0:h0 + nh, :], C.ps[b][:, 0:nh * 128].rearrange("p (a i) -> p a i", i=128),
                                                        _bc_mid(U[:], [128, nh, 128]), ALU.mult), reads=[C.bps[b], bK], writes=[battm])
        if getattr(cfg, "gla_stop", 99) == 5:
            return
        for h in range(GH):
            b = _bank(C)
            vc = slice(h * HV, (h + 1) * HV)
            S.op("pe", lambda: nc.tensor.matmul(C.ps[b][:, 0:HV], attm[:, h, :], v_tok[:, vc], start=True, stop=False),
                 reads=[battm, bvt], writes=[C.bps[b]], inc=False)
            for kk in range(KPH):
                t = h * KPH + kk
                S.op("pe", lambda: nc.tensor.matmul(C.ps[b][:, 0:HV], qg[:, t, :], Sgb[:, t, :], start=False, stop=(kk == KPH - 1)),
                     reads=[bqg, bSgb], writes=[C.bps[b]], inc=(kk == KPH - 1))
            S.op("dve", lambda: nc.vector.tensor_copy(o32[:, vc], C.ps[b][:, 0:HV]), reads=[C.bps[b]], writes=[bo])
            S.op("act", lambda: nc.scalar.activation(junk[:], o32[:, vc], AF.Square, accum_out=ssq[:, h:h + 1]), reads=[bo], writes=[bjunk, bssq])
        if getattr(cfg, "gla_stop", 99) == 6:
            return
        for t in range(NKT):
            h = t // KPH
            b = _bank(C)
            S.op("pe", lambda: nc.tensor.matmul(C.ps[b][:, 0:HV], kd[:, t * 128:(t + 1) * 128], v_tok[:, h * HV:(h + 1) * HV], start=True, stop=True),
                 reads=[bkd, bvt], writes=[C.bps[b]])
            S.op("dve", lambda: nc.vector.scalar_tensor_tensor(Sg32[:, t, :], Sg32[:, t, :], eg[:, t, 127:128], C.ps[b][:, 0:HV], ALU.mult, ALU.add),
                 reads=[bSg, beg, C.bps[b]], writes=[bSg])
            S.op("act", lambda: nc.scalar.copy(Sgb[:, t, :], Sg32[:, t, :]), reads=[bSg], writes=[bSgb])
        if getattr(cfg, "gla_stop", 99) == 7:
            return
        S.op("act", lambda: nc.scalar.activation(ssq[:], ssq[:], AF.Sqrt, bias=1e-5, scale=1.0 / HV), reads=[bssq], writes=[bssq])
        S.op("dve", lambda: nc.vector.reciprocal(ssq[:], ssq[:]), reads=[bssq], writes=[bssq])
        S.op("act", lambda: nc.scalar.activation(sg[:], g_tok[:], AF.Silu), reads=[bgt], writes=[bsg])
        S.op("pool", lambda: nc.gpsimd.tensor_tensor(sg[:], sg[:], o32[:], ALU.mult), reads=[bsg, bo], writes=[bsg])
        S.op("dve", lambda: nc.vector.tensor_tensor(on[:].rearrange("p (h q) -> p h q", q=HV), sg[:].rearrange("p (h q) -> p h q", q=HV),
                                                    _bc_last(ssq[:], [128, GH, HV]), ALU.mult), reads=[bsg, bssq], writes=[bon])

        def evac(t0, n, psb, bps):
            S.op("dve", lambda: nc.vector.tensor_tensor(oT[s][:, t0:t0 + n, :], psb[:, 0:n * 128].rearrange("p (a i) -> p a i", i=128),
                                                        _bc_last(gnw[:, t0:t0 + n], [128, n, 128]), ALU.mult),
                 reads=[bps, bl], writes=[boT[s]])
        transpose_tiles(C, lambda t: on[:, t * 128:(t + 1) * 128], NVT, bon, None, None, I_bf, bK, evac=evac)
        S.dma("sp", YO[c][:, tile_off:tile_off + NVT, :], oT[s][:], reads=[boT[s]])


def barrier(C):
    S = C.S
    toks = [(S.sem[e], S.cnt[e]) for e in ("pe", "act", "dve", "pool") if S.cnt[e] > 0]
    toks += [(sem, 16 * S.dma_cnt[i]) for i, sem in enumerate(S.dma_sems) if S.dma_cnt[i] > 0]
    toks += [(sem, 16) for sem in S.sw.values()]
    for e in ("pe", "act", "dve", "pool", "sp"):
        for t in toks:
            S._wait(e, t)


class Phase:
    def __init__(self, C):
        self.C = C

    def __enter__(self):
        self.stack = contextlib.ExitStack()
        self.stack.__enter__()
        self.prev = self.C.st
        self.C.st = self.stack
        return self

    def __exit__(self, *a):
        barrier(self.C)
        self.C.st = self.prev
        return self.stack.__exit__(*a)


def gemm_multi(C, W, K, n0, n1, rhs_tiles, brhs, tbs, T, epilogue):
    nc, S = C.nc, C.S
    KT = K // 128
    assert KT <= 16
    bc = 512
    jblk = 0
    for c0 in range(n0, n1, bc):
        cw = min(bc, n1 - c0)
        w = wnext(C, W, jblk)
        wv = C.wb[w]
        jblk += 1
        for lo in range(0, cw, 128):
            ncol = min(128, cw - lo)
            for tb in tbs:
                b = C.psi % 8
                C.psi += 1
                for kt in range(KT):
                    S.op("pe", lambda: nc.tensor.matmul(C.ps[b][0:ncol, 0:T], wv[:, kt, lo:lo + ncol], rhs_tiles(kt, tb),
                                                        start=(kt == 0), stop=(kt == KT - 1)),
                         reads=[C.bwb[w], brhs], writes=[C.bps[b]], inc=(kt == KT - 1))
                epilogue(c0 + lo, ncol, C.ps[b][0:ncol, 0:T], C.bps[b], tb)


def alloc_wbufs(C):
    C.wb = [C.sb([128, 16, 512], BF16) for _ in range(C.NW)]
    C.bwb = [Buf(f"wb{i}") for i in range(C.NW)]


_CACHE = {}


def lp_layout(cfg):
    NXT = cfg.DI // 128
    NCT = NXT + 2 * cfg.G
    NVT = cfg.DV // 128
    NDT = cfg.D // 128
    off, o = {}, 0
    for name, n in [("cw", 4 * NCT), ("cb", NCT), ("dtb", 1), ("alog", 1), ("Drep", cfg.H), ("normw", NXT), ("gnw", NVT),
                    ("gb0", NDT), ("gb1", NDT), ("ln1g", NDT), ("ln1b", NDT), ("ln2g", NDT), ("ln2b", NDT)]:
        off[name] = (o, n)
        o += n
    return off, o


def build_program(cfg):
    nc = bass.Bass("TRN2", target_bir_lowering=False)
    D, L, depth, H, G = cfg.D, cfg.L, cfg.depth, cfg.H, cfg.G
    DI, CONV, DK, DV, FF, DIN = cfg.DI, cfg.CONV, cfg.DK, cfg.DV, cfg.FF, cfg.DIN
    NDT, NXT, NVT, NKT = D // 128, DI // 128, DV // 128, DK // 128
    NCT = NXT + 2 * G
    NB, NCH, T = L // 512, L // 128, 512
    NYO = NXT + NVT
    off, NLP = lp_layout(cfg)
    ext = lambda n, sh, dt=F32: nc.dram_tensor(n, list(sh), dt, kind="ExternalInput").ap()
    X0 = ext("X0", [NB, 128, NDT, T])
    OUT = nc.dram_tensor("OUT", [NB, 128, NDT, T], F32, kind="ExternalOutput").ap()
    Win = ext("w_in", [depth, D, DIN]); Wsb = ext("w_ssd_branch", [depth, DI, D]); Wgb = ext("w_gla_branch", [depth, DV, D])
    Wout = ext("w_out", [depth, D, D]); Wup = ext("w_up", [depth, D, FF]); Wdn = ext("w_down", [depth, FF, D])
    LP = ext("LP", [depth, 128, NLP]); GKW = ext("GKW", [depth, 17, DK]); KC = ext("KC", [128, 4, 128])
    scr = lambda n, sh, dt: nc.dram_tensor(n, list(sh), dt, kind="Internal").ap()
    XR = scr("XR", [NB, 128, NDT, T], F32); XB = scr("XB", [NB, 128, NDT, T], BF16)
    ZT = scr("ZT", [NCH, 128, NXT, 128], BF16); XBCT = scr("XBCT", [NCH, 128, NCT, 128], BF16)
    DTT = scr("DTT", [NCH, H, 128], F32); GKL = scr("GKL", [NCH, 16, 128], F32)
    QT = scr("QT", [NCH, 128, NKT, 128], BF16); KTd = scr("KTd", [NCH, 128, NKT, 128], BF16)
    VT = scr("VT", [NCH, 128, NVT, 128], BF16); GT = scr("GT", [NCH, 128, NVT, 128], BF16)
    GATE = scr("GATE", [NB, 128, 2 * NDT, T], BF16); YO = scr("YO", [NCH, 128, NYO, 128], BF16)
    NFF = 4
    FFh_ = FF // NFF
    names = ["z", "xbc", "dt", "q", "k", "v", "g", "gkl", "gs", "gg"]
    segs, o = {}, 0
    for nme, n in zip(names, cfg.splits):
        segs[nme] = (o, o + n)
        o += n
    with contextlib.ExitStack() as st:
        C = Ctx(nc, st, cfg)
        S = C.S
        kc = C.sb([128, 4, 128], F32); bK = Buf()
        ib = C.sb([128, 128], BF16)
        ones_bf = C.sb([128, 128], BF16)
        lpt = C.sb([128, NLP], F32); bl = Buf()
        gkw = C.sb([32, DK], F32)
        IA = C.sb([128, 2 * H], F32)
        negA = C.sb([128, 1], F32)
        S.dma("sp", kc[:], KC[:], writes=[bK])
        S.op("dve", lambda: nc.vector.tensor_copy(ib[:], kc[:, 0, :]), reads=[bK], writes=[bK])
        S.op("dve", lambda: nc.vector.tensor_copy(ones_bf[:], kc[:, 2, :]), reads=[bK], writes=[bK])
        K = dict(ident_bf=ib, U=kc[:, 1, :], ones32=kc[:, 2, :], SU=kc[:, 3, :], b=bK)
        P = lambda nme: lpt[:, off[nme][0]:off[nme][0] + off[nme][1]]

        with Phase(C):
            xf = [C.sb([128, NDT, T], F32) for _ in range(2)]; bxf = [Buf(), Buf()]
            xb = [C.sb([128, NDT, T], BF16) for _ in range(2)]; bxb = [Buf(), Buf()]
            for tb in range(NB):
                s = tb % 2
                S.dma("sp", xf[s][:], X0[tb], writes=[bxf[s]])
                S.op("dve" if s else "act", (lambda: nc.vector.tensor_copy(xb[s][:], xf[s][:])) if s else (lambda: nc.scalar.copy(xb[s][:], xf[s][:])),
                     reads=[bxf[s]], writes=[bxb[s]])
                S.dma("sp", XB[tb], xb[s][:], reads=[bxb[s]])

        stop_after = getattr(cfg, "stop_after", None)
        for l in range(depth):
            XRsrc = X0 if l == 0 else XR
            XRdst = OUT if l == depth - 1 else XR
            S.dma("sp", lpt[:], LP[l], writes=[bl])
            S.dma("sp", gkw[0:17, :], GKW[l], writes=[bl])
            S.op("act", lambda: nc.scalar.activation(negA[0:H, :], P("alog")[0:H, :], AF.Exp), reads=[bl], writes=[bl])
            S.op("dve", lambda: nc.vector.tensor_copy(IA[0:H, 0:H], kc[0:H, 0, 0:H]), reads=[bK, bl], writes=[bl])
            S.op("dve", lambda: nc.vector.tensor_scalar(IA[0:H, H:2 * H], kc[0:H, 0, 0:H], negA[0:H, 0:1], -1.0, ALU.mult, ALU.mult),
                 reads=[bK, bl], writes=[bl])
            for e in ("pe", "act", "dve", "pool"):
                S.wait_all(e, [bl, bK])
            lp = dict(cw=P("cw").rearrange("p (k t) -> p k t", t=NCT), cb=P("cb"), dtb=P("dtb"), IA=IA, Drep=P("Drep"), normw=P("normw"),
                      gkw=gkw, gnw=P("gnw"), b=bl)

            specs = [(("in", nme), Win[l], D, segs[nme][0], segs[nme][1]) for nme in names]
            specs += [("sb", Wsb[l], DI, 0, D), ("gb", Wgb[l], DV, 0, D), ("out", Wout[l], D, 0, D)]
            for hf in range(NFF):
                specs += [(("up", hf), Wup[l], D, hf * FFh_, (hf + 1) * FFh_), (("dn", hf), Wdn[l][hf * FFh_:(hf + 1) * FFh_, :], FFh_, 0, D)]
            if C.WBF is None:
                nblk = sum(len(wblock_list(Kk, a, b_)) for (_, _, Kk, a, b_) in specs)
                C.WBF = scr("WBF", [nblk, 128, 8192], BF16)
            with Phase(C):
                wf = [C.sb([128, 16, 512], F32) for _ in range(2)]; bwf = [Buf(), Buf()]
                wq = [C.sb([128, 16, 512], BF16) for _ in range(3)]; bwq = [Buf(), Buf(), Buf()]
                idx = 0
                for (key, Wap, Kk, a, b_) in specs:
                    C.wbase[key] = idx
                    for (k0, nk, c0, cw) in wblock_list(Kk, a, b_):
                        s2 = idx % 2
                        s3 = idx % 3
                        S.dma("sp", wf[s2][:, 0:nk, 0:cw], Wap[k0 * 128:(k0 + nk) * 128, c0:c0 + cw].rearrange("(kt p) n -> p kt n", p=128), writes=[bwf[s2]])
                        if s3 == 0:
                            S.op("act", lambda: nc.scalar.copy(wq[s3][:, 0:nk, 0:cw], wf[s2][:, 0:nk, 0:cw]), reads=[bwf[s2]], writes=[bwq[s3]])
                        elif s3 == 1:
                            S.op("dve", lambda: nc.vector.tensor_copy(wq[s3][:, 0:nk, 0:cw], wf[s2][:, 0:nk, 0:cw]), reads=[bwf[s2]], writes=[bwq[s3]])
                        else:
                            S.op("pool", lambda: nc.gpsimd.tensor_copy(wq[s3][:, 0:nk, 0:cw], wf[s2][:, 0:nk, 0:cw]), reads=[bwf[s2]], writes=[bwq[s3]])
                        S.dma("sp", C.WBF[idx][:, 0:nk * cw].rearrange("p (a b) -> p a b", b=cw), wq[s3][:, 0:nk, 0:cw], reads=[bwq[s3]])
                        idx += 1

            if stop_after == "precast":
                break
            with Phase(C):
                alloc_wbufs(C)
                TBn = min(4, NB)
                xT = C.sb([128, NDT, TBn * T], BF16); bxT = Buf()
                stg = [C.sb([128, T], BF16) for _ in range(4)]; bstg = [Buf() for _ in range(4)]
                stf = [C.sb([128, T], F32) for _ in range(2)]; bstf = [Buf() for _ in range(2)]
                pre = [C.sb([128, T + 3], F32) for _ in range(2)]; bpre = [Buf(), Buf()]
                cacc = [C.sb([128, T], F32) for _ in range(2)]; bcacc = [Buf(), Buf()]
                ct1 = [C.sb([128, T], F32) for _ in range(2)]; bct1 = [Buf(), Buf()]
                ct2 = [C.sb([128, T], F32) for _ in range(2)]; bct2 = [Buf(), Buf()]
                hal = C.sb([128, NCT, 3], F32); bhal = Buf()
                S.op("pool", lambda: nc.gpsimd.memset(hal[:], 0.0), writes=[bhal])
                cwv, cbv = lp["cw"], lp["cb"]
                cnt = [0]
                ccnt = [0]

                def make_epi(nme, sb0):
                    n0 = segs[nme][0]

                    def epi(col, ncol, ps, bps, tb):
                        tile = (col - n0) // 128
                        i = cnt[0]; cnt[0] += 1
                        if nme in ("dt", "gkl"):
                            j = i % 2
                            S.op("dve", lambda: nc.vector.tensor_copy(stf[j][0:ncol, :], ps), reads=[bps], writes=[bstf[j]])
                            dst = (DTT if nme == "dt" else GKL)[tb * 4:tb * 4 + 4, 0:ncol, :].rearrange("c p i -> p c i")
                            S.dma("sp", dst, stf[j][0:ncol, :].rearrange("p (c i) -> p c i", i=128), reads=[bstf[j]])
                            return
                        j = i % 4
                        if nme == "xbc":
                            q = ccnt[0] % 2; ccnt[0] += 1
                            assert ncol == 128
                            S.op("pool", lambda: nc.gpsimd.tensor_copy(pre[q][:, 0:3], hal[:, tile, :]), reads=[bhal], writes=[bpre[q]])
                            S.op("act", lambda: nc.scalar.copy(pre[q][:, 3:T + 3], ps), reads=[bps], writes=[bpre[q]])
                            S.op("pool", lambda: nc.gpsimd.tensor_copy(hal[:, tile, :], pre[q][:, T:T + 3]), reads=[bpre[q]], writes=[bhal])
                            S.op("act", lambda: nc.scalar.activation(ct1[q][:], pre[q][:, 1:1 + T], AF.Identity, scale=cwv[:, 1, tile:tile + 1]),
                                 reads=[bpre[q], bl], writes=[bct1[q]])
                            S.op("pool", lambda: nc.gpsimd.tensor_scalar_mul(ct2[q][:], pre[q][:, 2:2 + T], cwv[:, 2, tile:tile + 1]),
                                 reads=[bpre[q], bl], writes=[bct2[q]])
                            S.op("pool", lambda: nc.gpsimd.tensor_tensor(ct2[q][:], ct2[q][:], ct1[q][:], ALU.add), reads=[bct2[q], bct1[q]], writes=[bct2[q]])
                            S.op("dve", lambda: nc.vector.scalar_tensor_tensor(cacc[q][:], pre[q][:, 0:T], cwv[:, 0, tile:tile + 1], ct2[q][:], ALU.mult, ALU.add),
                                 reads=[bpre[q], bct2[q], bl], writes=[bcacc[q]])
                            S.op("dve", lambda: nc.vector.scalar_tensor_tensor(cacc[q][:], pre[q][:, 3:3 + T], cwv[:, 3, tile:tile + 1], cacc[q][:], ALU.mult, ALU.add),
                                 reads=[bpre[q], bcacc[q], bl], writes=[bcacc[q]])
                            S.op("act", lambda: nc.scalar.activation(stg[j][:, :], cacc[q][:], AF.Silu, bias=cbv[:, tile:tile + 1]), reads=[bcacc[q], bl], writes=[bstg[j]])
                        elif i % 2:
                            S.op("act", lambda: nc.scalar.copy(stg[j][0:ncol, :], ps), reads=[bps], writes=[bstg[j]])
                        else:
                            S.op("dve", lambda: nc.vector.tensor_copy(stg[j][0:ncol, :], ps), reads=[bps], writes=[bstg[j]])
                        if nme in ("gs", "gg"):
                            S.dma("sp", GATE[tb, 0:ncol, tile + (NDT if nme == "gg" else 0), :], stg[j][0:ncol, :], reads=[bstg[j]])
                        else:
                            dt_ = dict(z=ZT, xbc=XBCT, q=QT, k=KTd, v=VT, g=GT)[nme]
                            dst = dt_[tb * 4:tb * 4 + 4, 0:ncol, tile, :].rearrange("c p i -> p c i")
                            S.dma("sp", dst, stg[j][0:ncol, :].rearrange("p (c i) -> p c i", i=128), reads=[bstg[j]])
                    return epi

                ent = []
                for sb0 in range(0, NB, TBn):
                    for nme in names:
                        ent += wstream_entries(("in", nme), D, segs[nme][0], segs[nme][1])
                wstream_begin(C, ent)
                for sb0 in range(0, NB, TBn):
                    tbs = list(range(sb0, min(sb0 + TBn, NB)))
                    for j, tb in enumerate(tbs):
                        S.dma("sp", xT[:, :, j * T:(j + 1) * T], XB[tb], writes=[bxT])
                    for nme in names:
                        gemm_multi(C, ("in", nme), D, segs[nme][0], segs[nme][1], lambda kt, tb: xT[:, kt, (tb - sb0) * T:(tb - sb0 + 1) * T],
                                   bxT, tbs, T, make_epi(nme, sb0))

            if stop_after == "p1":
                break
            with Phase(C):
                ssd_pass(C, K, lp, ZT, XBCT, DTT, YO, NCH)
            if stop_after == "ssd":
                break
            with Phase(C):
                gla_pass(C, K, lp, QT, KTd, VT, GT, GKL, YO, NCH, NXT)
            if stop_after == "gla":
                break

            with Phase(C):
                alloc_wbufs(C)
                FFh = FF // NFF
                yo = C.sb([128, max(NXT, NVT), T], BF16); byo = Buf()
                ubf = C.sb([128, FFh // 128, T], BF16); bubf = Buf()
                assert max(NXT, NVT) >= 2 * NDT
                C.ln_vb = yo[:, 0:NDT, :]; C.b_ln_vb = byo
                C.ln_sq = yo[:, NDT:2 * NDT, :]; C.b_ln_sq = byo
                C.ln_st = C.sb([128, 3, T], F32); C.b_ln_st = Buf()
                f32a = C.sb([128, NDT, T], F32); bfa32 = Buf()
                f32b = C.sb([128, NDT, T], F32); bfb32 = Buf()
                bfa = C.sb([128, NDT, T], BF16); bbfa = Buf()
                gt = [C.sb([128, T], BF16) for _ in range(2)]; bgt = [Buf(), Buf()]
                tmp = [C.sb([128, T], F32) for _ in range(2)]; btmp = [Buf(), Buf()]
                tm2 = [C.sb([128, T], F32) for _ in range(2)]; btm2 = [Buf(), Buf()]
                ec = [0]
                gb0, gb1 = P("gb0"), P("gb1")
                alpha = float(cfg.alpha)
                ent = []
                for tb in range(NB):
                    ent += wstream_entries("sb", DI, 0, D) + wstream_entries("gb", DV, 0, D) + wstream_entries("out", D, 0, D)
                    for hf in range(NFF):
                        ent += wstream_entries(("up", hf), D, hf * FFh, (hf + 1) * FFh) + wstream_entries(("dn", hf), FFh, 0, D)
                wstream_begin(C, ent)
                for tb in range(NB):
                    S.dma("sp", f32b[:], XRsrc[tb], writes=[bfb32])
                    for c4 in range(4):
                        S.dma("sp", yo[:, 0:NXT, c4 * 128:(c4 + 1) * 128], YO[tb * 4 + c4][:, 0:NXT, :], writes=[byo])

                    def epi_bs(col, ncol, ps, bps):
                        nt = col // 128
                        i = ec[0] % 2; ec[0] += 1
                        S.dma("sp", gt[i][:], GATE[tb, :, nt, :], writes=[bgt[i]])
                        S.op("act", lambda: nc.scalar.activation(tmp[i][:], gt[i][:], AF.Sigmoid, bias=gb0[:, nt:nt + 1]), reads=[bgt[i], bl], writes=[btmp[i]])
                        S.op("dve", lambda: nc.vector.tensor_tensor(f32a[:, nt, :], tmp[i][:], ps, ALU.mult), reads=[btmp[i], bps], writes=[bfa32])
                    gemm_fm(C, "sb", DI, 0, D, lambda kt: yo[:, kt, :], byo, T, epi_bs)
                    for c4 in range(4):
                        S.dma("sp", yo[:, 0:NVT, c4 * 128:(c4 + 1) * 128], YO[tb * 4 + c4][:, NXT:NXT + NVT, :], writes=[byo])

                    def epi_bg(col, ncol, ps, bps):
                        nt = col // 128
                        i = ec[0] % 2; ec[0] += 1
                        S.dma("sp", gt[i][:], GATE[tb, :, NDT + nt, :], writes=[bgt[i]])
                        S.op("act", lambda: nc.scalar.activation(tmp[i][:], gt[i][:], AF.Sigmoid, bias=gb1[:, nt:nt + 1]), reads=[bgt[i], bl], writes=[btmp[i]])
                        S.op("dve", lambda: nc.vector.tensor_tensor(tm2[i][:], tmp[i][:], ps, ALU.mult), reads=[btmp[i], bps], writes=[btm2[i]])
                        S.op("pool", lambda: nc.gpsimd.tensor_tensor(bfa[:, nt, :], tm2[i][:], f32a[:, nt, :], ALU.add), reads=[btm2[i], bfa32], writes=[bbfa])
                    gemm_fm(C, "gb", DV, 0, D, lambda kt: yo[:, kt, :], byo, T, epi_bg)

                    def epi_out(col, ncol, ps, bps):
                        nt = col // 128
                        S.op("dve", lambda: nc.vector.scalar_tensor_tensor(f32a[:, nt, :], f32b[:, nt, :], alpha, ps, ALU.mult, ALU.add),
                             reads=[bfb32, bps], writes=[bfa32])
                    gemm_fm(C, "out", D, 0, D, lambda kt: bfa[:, kt, :], bbfa, T, epi_out)
                    layernorm_fm(C, f32a, bfa32, NDT, T, P("ln1g"), P("ln1b"), f32b, bfb32, bfa, bbfa, ones_bf, bK)
                    for hf in range(NFF):
                        def epi_up(col, ncol, ps, bps):
                            ft = (col - hf * FFh) // 128
                            i = ec[0] % 2; ec[0] += 1
                            S.op("act", lambda: nc.scalar.activation(tmp[i][:], ps, AF.Relu), reads=[bps], writes=[btmp[i]])
                            if i:
                                S.op("pool", lambda: nc.gpsimd.tensor_tensor(ubf[:, ft, :], tmp[i][:], tmp[i][:], ALU.mult), reads=[btmp[i]], writes=[bubf])
                            else:
                                S.op("dve", lambda: nc.vector.tensor_tensor(ubf[:, ft, :], tmp[i][:], tmp[i][:], ALU.mult), reads=[btmp[i]], writes=[bubf])
                        gemm_fm(C, ("up", hf), D, hf * FFh, (hf + 1) * FFh, lambda kt: bfa[:, kt, :], bbfa, T, epi_up)

                        def epi_dn(col, ncol, ps, bps):
                            nt = col // 128
                            if hf == 0:
                                S.op("dve", lambda: nc.vector.scalar_tensor_tensor(f32a[:, nt, :], f32b[:, nt, :], alpha, ps, ALU.mult, ALU.add),
                                     reads=[bfb32, bps], writes=[bfa32])
                            else:
                                S.op("dve", lambda: nc.vector.tensor_tensor(f32a[:, nt, :], f32a[:, nt, :], ps, ALU.add), reads=[bfa32, bps], writes=[bfa32])
                        gemm_fm(C, ("dn", hf), FFh, 0, D, lambda kt: ubf[:, kt, :], bubf, T, epi_dn)
                    layernorm_fm(C, f32a, bfa32, NDT, T, P("ln2g"), P("ln2b"), f32b, bfb32, bfa, bbfa, ones_bf, bK)
                    S.dma("sp", XRdst[tb], f32b[:], reads=[bfb32])
                    if l < depth - 1:
                        S.dma("sp", XB[tb], bfa[:], reads=[bbfa])
        barrier(C)
        _CACHE["n_ins"] = dict(S.n_ins)
        _CACHE["nsem"] = S.nsem
    return nc


def host_inputs(cfg, inp, b):
    D, L, depth, H, G = cfg.D, cfg.L, cfg.depth, cfg.H, cfg.G
    NDT, NXT, NVT = D // 128, cfg.DI // 128, cfg.DV // 128
    NCT = NXT + 2 * G
    NB = L // 512
    HV = cfg.DV // cfg.GH
    off, NLP = lp_layout(cfg)
    f = lambda a: np.asarray(a, dtype=np.float32)
    x = f(inp["x"])[b]
    X0 = np.ascontiguousarray(x.reshape(NB, 512, NDT, 128).transpose(0, 3, 2, 1))
    LP = np.zeros((depth, 128, NLP), np.float32)
    tile_pm = lambda v, nt: v.reshape(nt, 128).T

    def put(l, nme, arr):
        o, n = off[nme]
        LP[l, :arr.shape[0], o:o + n] = arr
    for l in range(depth):
        put(l, "cw", f(inp["ssd_conv_w"])[l].reshape(4, NCT, 128).transpose(2, 0, 1).reshape(128, 4 * NCT))
        put(l, "cb", tile_pm(f(inp["ssd_conv_b"])[l], NCT))
        put(l, "dtb", f(inp["ssd_dt_bias"])[l][:, None])
        put(l, "alog", f(inp["ssd_A_log"])[l][:, None])
        put(l, "Drep", np.broadcast_to(f(inp["ssd_D"])[l][None, :], (128, H)))
        put(l, "normw", tile_pm(f(inp["ssd_norm_w"])[l], NXT))
        put(l, "gnw", tile_pm(np.tile(f(inp["gla_norm_w"])[l], cfg.GH), NVT))
        put(l, "gb0", tile_pm(f(inp["gate_bias"])[l, 0], NDT))
        put(l, "gb1", tile_pm(f(inp["gate_bias"])[l, 1], NDT))
        for nme in ("ln1_g", "ln1_b", "ln2_g", "ln2_b"):
            put(l, nme.replace("_", ""), tile_pm(f(inp[nme])[l], NDT))
    GKW = np.concatenate([f(inp["gla_gk_w"]), f(inp["gla_gk_b"])[:, None, :]], axis=1)
    KC = np.zeros((128, 4, 128), np.float32)
    KC[:, 0, :] = np.eye(128)
    KC[:, 1, :] = np.triu(np.ones((128, 128)))
    KC[:, 2, :] = 1.0
    KC[:, 3, :] = np.tril(np.ones((128, 128)), -1)
    m = {"X0": X0, "LP": LP, "GKW": np.ascontiguousarray(GKW), "KC": KC}
    for k in ("w_in", "w_ssd_branch", "w_gla_branch", "w_out", "w_up", "w_down"):
        m[k] = f(inp[k])
    return m


def host_output(cfg, OUT):
    NB, _, NDT, T = OUT.shape
    return OUT.transpose(0, 3, 2, 1).reshape(NB * T, NDT * 128)


def kernel(**inputs):
    cfg = Cfg()
    B = np.asarray(inputs["x"]).shape[0]
    assert B == N_CORES_USED
    if "nc" not in _CACHE:
        _CACHE["nc"] = build_program(cfg)
    nc = _CACHE["nc"]
    in_maps = [host_inputs(cfg, inputs, b) for b in range(B)]
    res = run_bass_kernel_spmd(nc, in_maps, core_ids=list(range(B)))
    out = np.stack([host_output(cfg, np.asarray(res.results[b]["OUT"], dtype=np.float32)) for b in range(B)], 0)
    return out.astype(np.float32)
```

```python
import contextlib
import numpy as np
import concourse.bass as bass
import concourse.mybir as mybir
from concourse.bass_utils import run_bass_kernel_spmd

F32 = mybir.dt.float32
BF16 = mybir.dt.bfloat16
AF = mybir.ActivationFunctionType
ALU = mybir.AluOpType

N_CORES_USED = 2


class Cfg:
    def __init__(self, d_model=2048, seq=8192, depth=4, ssd_heads=64, ssd_groups=8, gla_heads=4):
        self.D = d_model
        self.L = seq
        self.depth = depth
        self.P = 64
        self.N = 128
        self.H = ssd_heads
        self.G = ssd_groups
        self.DI = self.H * self.P
        self.CONV = self.DI + 2 * self.G * self.N
        self.GH = gla_heads
        self.DK = d_model // 2
        self.DV = d_model
        self.R = 16
        self.FF = 4 * d_model
        self.splits = (self.DI, self.CONV, self.H, self.DK, self.DK, self.DV, self.DV, self.R, self.D, self.D)
        self.DIN = sum(self.splits)
        self.alpha = (2.0 * depth) ** 0.25


class Buf:
    __slots__ = ("name", "w", "r")

    def __init__(self, name=""):
        self.name = name
        self.w = None
        self.r = []


class Sched:
    SEM_ROT = 30000

    def __init__(self, nc, stack, n_dma_sems=40):
        self.nc = nc
        self.stack = stack
        self.eng = {"pe": nc.tensor, "act": nc.scalar, "dve": nc.vector, "pool": nc.gpsimd, "sp": nc.sync}
        self.sem = {}
        self.cnt = {}
        self.nsem = 0
        self.pe_sems = set()
        for e in ("pe", "act", "dve", "pool"):
            self._new_sem(e)
        self.dma_sems = [stack.enter_context(nc.semaphore(f"dsem{i}")) for i in range(n_dma_sems)]
        self.dma_cnt = [0] * n_dma_sems
        self.dma_next = 0
        self.waited = {e: {} for e in self.eng}
        self.semobj = {}
        self.sw = {}
        self.n_ins = {e: 0 for e in self.eng}

    def _new_sem(self, e):
        s = self.stack.enter_context(self.nc.semaphore(f"sem_{e}_{self.nsem}"))
        self.nsem += 1
        self.sem[e] = s
        self.cnt[e] = 0
        if e == "pe":
            self.pe_sems.add(id(s))

    def _wait(self, e, tok):
        if tok is None:
            return
        sem, val = tok
        d = self.waited[e]
        k = id(sem)
        if e == "pe" and k in self.pe_sems:
            return
        if d.get(k, 0) >= val:
            return
        d[k] = val
        self.eng[e].wait_ge(sem, val)

    def _deps(self, e, reads, writes, skip_same_pe=False):
        for b in reads:
            self._wait(e, b.w)
        for b in writes:
            self._wait(e, b.w)
            for t in b.r:
                self._wait(e, t)

    def _commit(self, tok, reads, writes):
        for b in reads:
            b.r.append(tok)
            if len(b.r) > 12:
                latest = {}
                for (s, v) in b.r:
                    if id(s) not in latest or latest[id(s)][1] < v:
                        latest[id(s)] = (s, v)
                b.r = list(latest.values())
        for b in writes:
            b.w = tok
            b.r = []

    def op(self, e, fn, reads=(), writes=(), inc=True):
        if self.cnt[e] >= self.SEM_ROT:
            self._new_sem(e)
        self._deps(e, reads, writes)
        ins = fn()
        self.n_ins[e] += 1
        if inc:
            self.cnt[e] += 1
            ins.then_inc(self.sem[e], 1)
            tok = (self.sem[e], self.cnt[e])
        else:
            tok = (self.sem[e], self.cnt[e] + 1)
        self._commit(tok, reads, writes)
        return ins

    def dma(self, q, out, in_, reads=(), writes=(), **kw):
        i = self.dma_next
        self.dma_next = (self.dma_next + 1) % len(self.dma_sems)
        sem = self.dma_sems[i]
        if self.dma_cnt[i] > 0:
            self._wait(q, (sem, 16 * self.dma_cnt[i]))
        self._deps(q, reads, writes)
        ins = self.eng[q].dma_start(out=out, in_=in_, **kw)
        self.n_ins[q] += 1
        self.dma_cnt[i] += 1
        ins.then_inc(sem, 16)
        tok = (sem, 16 * self.dma_cnt[i])
        self._commit(tok, reads, writes)
        return tok

    def dma_sw(self, key, out, in_, reads=(), writes=(), **kw):
        sem = self.sw.get(key)
        first = sem is None
        if first:
            sem = self.stack.enter_context(self.nc.semaphore(f"swsem{len(self.sw)}"))
            self.sw[key] = sem
        self._deps("pool", reads, writes)
        if not first:
            self.eng["pool"].wait_ge(sem, 16)
            self.eng["pool"].sem_clear(sem)
            for e in self.waited:
                self.waited[e].pop(id(sem), None)
        ins = self.eng["pool"].dma_start(out=out, in_=in_, **kw)
        self.n_ins["pool"] += 1
        ins.then_inc(sem, 16)
        tok = (sem, 16)
        self._commit(tok, reads, writes)
        return tok

    def wait_all(self, e, bufs):
        for b in bufs:
            self._wait(e, b.w)
            for t in b.r:
                self._wait(e, t)


class Ctx:
    def __init__(self, nc, st, cfg):
        self.nc, self.st, self.cfg = nc, st, cfg
        self.S = Sched(nc, st)
        self._n = 0
        self.ps = [st.enter_context(nc.psum_tensor(f"ps{i}", [128, 512], F32)) for i in range(8)]
        self.bps = [Buf(f"ps{i}") for i in range(8)]
        self.NW = 3
        self.wbase = {}
        self.WBF = None
        self.wb = None
        self.bwb = None
        self.wi = 0
        self.psi = 0
        self.ev = 0

    def sb(self, shape, dt):
        self._n += 1
        return self.st.enter_context(self.nc.sbuf_tensor(f"t{self._n}", list(shape), dt))

    def dram(self, shape, dt):
        self._n += 1
        return self.nc.dram_tensor(f"scr{self._n}", list(shape), dt, kind="Internal").ap()


def wblock_list(K, n0, n1):
    KT = K // 128
    out = []
    for c0 in range(n0, n1, 512):
        cw = min(512, n1 - c0)
        for k0 in range(0, KT, 16):
            out.append((k0, min(16, KT - k0), c0, cw))
    return out


def wstream_begin(C, entries):
    C.wq = list(entries)
    C.wq_issued = 0
    C.wq_pos = 0


def wstream_entries(key, K, n0, n1):
    return [(key, j, nk, cw) for j, (k0, nk, c0, cw) in enumerate(wblock_list(K, n0, n1))]


def wnext(C, key, j):
    pos = C.wq_pos
    assert C.wq[pos][0] == key and C.wq[pos][1] == j, (C.wq[pos], key, j)
    while C.wq_issued < min(len(C.wq), pos + C.NW):
        k_, j_, nk, cw = C.wq[C.wq_issued]
        w = C.wq_issued % C.NW
        blk = C.wbase[k_] + j_
        src = C.WBF[blk][:, 0:nk * cw].rearrange("p (a b) -> p a b", b=cw)
        C.S.dma("sp", C.wb[w][:, 0:nk, 0:cw], src, writes=[C.bwb[w]])
        C.wq_issued += 1
    C.wq_pos += 1
    return pos % C.NW


def wload(C, key, j, w, nk, cw):
    blk = C.wbase[key] + j
    src = C.WBF[blk][:, 0:nk * cw].rearrange("p (a b) -> p a b", b=cw)
    C.S.dma("act", C.wb[w][:, 0:nk, 0:cw], src, writes=[C.bwb[w]])


def gemm_fm(C, W, K, n0, n1, rhs_tiles, brhs, T, epilogue, ps_banks=(0, 1, 2, 3, 4, 5, 6, 7)):
    nc, S = C.nc, C.S
    KT = K // 128
    assert K % 128 == 0 and T <= 512
    jblk = 0
    for c0 in range(n0, n1, 512):
        cw = min(512, n1 - c0)
        kblocks = [(k0, min(16, KT - k0)) for k0 in range(0, KT, 16)]
        ntiles = [(c0 + j, min(128, c0 + cw - (c0 + j))) for j in range(0, cw, 128)]
        banks = [ps_banks[(C.psi + i) % len(ps_banks)] for i in range(len(ntiles))]
        C.psi += len(ntiles)
        for bi, (k0, nk) in enumerate(kblocks):
            w = wnext(C, W, jblk)
            jblk += 1
            for ti, (col, ncol) in enumerate(ntiles):
                b = banks[ti]
                for kk in range(nk):
                    kt = k0 + kk
                    first = (kt == 0)
                    last = (kt == KT - 1)
                    lo = col - c0
                    S.op("pe", lambda: nc.tensor.matmul(C.ps[b][0:ncol, 0:T], C.wb[w][:, kk, lo:lo + ncol],
                                                        rhs_tiles(kt), start=first, stop=last),
                         reads=[C.bwb[w], brhs], writes=[C.bps[b]], inc=last)
        for ti, (col, ncol) in enumerate(ntiles):
            b = banks[ti]
            epilogue(col, ncol, C.ps[b][0:ncol, 0:T], C.bps[b])


def layernorm_fm(C, v32, bv, KT, T, gam, bet, out32, bo32, outbf, bobf, ones_bf, bones, eps=1e-5):
    nc, S = C.nc, C.S
    D = KT * 128
    vb = C.ln_vb; sq = C.ln_sq; st = C.ln_st
    S.op("dve", lambda: nc.vector.tensor_copy(vb[:, 0:KT, 0:T], v32[:, 0:KT, 0:T]), reads=[bv], writes=[C.b_ln_vb])
    S.op("pool", lambda: nc.gpsimd.tensor_tensor(sq[:, 0:KT, 0:T], v32[:, 0:KT, 0:T], v32[:, 0:KT, 0:T], ALU.mult),
         reads=[bv], writes=[C.b_ln_sq])
    b_s, b_q = 4, 5
    for kt in range(KT):
        S.op("pe", lambda: nc.tensor.matmul(C.ps[b_s][:, 0:T], ones_bf[:, 0:128], vb[:, kt, 0:T], start=(kt == 0), stop=(kt == KT - 1)),
             reads=[bones, C.b_ln_vb], writes=[C.bps[b_s]], inc=(kt == KT - 1))
    for kt in range(KT):
        S.op("pe", lambda: nc.tensor.matmul(C.ps[b_q][:, 0:T], ones_bf[:, 0:128], sq[:, kt, 0:T], start=(kt == 0), stop=(kt == KT - 1)),
             reads=[bones, C.b_ln_sq], writes=[C.bps[b_q]], inc=(kt == KT - 1))
    S.op("act", lambda: nc.scalar.mul(st[:, 0, 0:T], C.ps[b_s][:, 0:T], 1.0 / D), reads=[C.bps[b_s]], writes=[C.b_ln_st])
    S.op("dve", lambda: nc.vector.tensor_tensor(st[:, 2, 0:T], st[:, 0, 0:T], st[:, 0, 0:T], ALU.mult), reads=[C.b_ln_st], writes=[C.b_ln_st])
    S.op("dve", lambda: nc.vector.scalar_tensor_tensor(st[:, 1, 0:T], C.ps[b_q][:, 0:T], 1.0 / D, st[:, 2, 0:T], ALU.mult, ALU.subtract),
         reads=[C.bps[b_q], C.b_ln_st], writes=[C.b_ln_st])
    S.op("act", lambda: nc.scalar.activation(st[:, 1, 0:T], st[:, 1, 0:T], AF.Sqrt, bias=eps), reads=[C.b_ln_st], writes=[C.b_ln_st])
    S.op("dve", lambda: nc.vector.reciprocal(st[:, 1, 0:T], st[:, 1, 0:T]), reads=[C.b_ln_st], writes=[C.b_ln_st])
    for kt in range(KT):
        e = "dve" if kt % 2 == 0 else "pool"
        eng = nc.vector if e == "dve" else nc.gpsimd
        S.op(e, lambda: eng.tensor_tensor(out32[:, kt, 0:T], v32[:, kt, 0:T], st[:, 0, 0:T], ALU.subtract), reads=[bv, C.b_ln_st], writes=[bo32])
        S.op(e, lambda: eng.tensor_tensor(out32[:, kt, 0:T], out32[:, kt, 0:T], st[:, 1, 0:T], ALU.mult), reads=[bo32, C.b_ln_st], writes=[bo32])
        S.op("act", lambda: nc.scalar.activation(out32[:, kt, 0:T], out32[:, kt, 0:T], AF.Identity, bias=bet[:, kt:kt + 1], scale=gam[:, kt:kt + 1]),
             reads=[bo32], writes=[bo32])
        S.op("act", lambda: nc.scalar.copy(outbf[:, kt, 0:T], out32[:, kt, 0:T]), reads=[bo32], writes=[bobf])


def ln_alloc(C, KT, T):
    C.ln_vb = C.sb([128, KT, T], BF16); C.b_ln_vb = Buf()
    C.ln_sq = C.sb([128, KT, T], BF16); C.b_ln_sq = Buf()
    C.ln_st = C.sb([128, 3, T], F32); C.b_ln_st = Buf()


def _bank(C):
    b = C.ev % 8
    C.ev += 1
    return b


def _bc_last(ap, shape):
    return ap.unsqueeze(2).to_broadcast(list(shape))


def _bc_mid(ap, shape):
    return ap.unsqueeze(1).to_broadcast(list(shape))


def transpose_tiles(C, src_tile, ntiles, bsrc, dst2d, bdst, ident_bf, bident, evac=None):
    nc, S = C.nc, C.S
    for t0 in range(0, ntiles, 8):
        n = min(8, ntiles - t0)
        b = _bank(C)
        psb = C.ps[b][:].bitcast(BF16)
        for t in range(n):
            S.op("pe", lambda: nc.tensor.transpose(psb[:, t * 128:(t + 1) * 128], src_tile(t0 + t), ident_bf[:]),
                 reads=[bsrc, bident], writes=[C.bps[b]], inc=(t == n - 1))
        if evac is not None:
            evac(t0, n, psb, C.bps[b])
        else:
            e = "act" if (C.ev % 2) else "dve"
            if e == "act":
                S.op("act", lambda: nc.scalar.copy(dst2d[:, t0 * 128:(t0 + n) * 128], psb[:, 0:n * 128]), reads=[C.bps[b]], writes=[bdst])
            else:
                S.op("dve", lambda: nc.vector.tensor_copy(dst2d[:, t0 * 128:(t0 + n) * 128], psb[:, 0:n * 128]), reads=[C.bps[b]], writes=[bdst])


def ssd_pass(C, K, lp, ZT, XBCT, DTT, YO, NCH):
    nc, S, cfg = C.nc, C.S, C.cfg
    H, G, P, N = cfg.H, cfg.G, cfg.P, cfg.N
    Hg = H // G
    assert Hg in (4, 8) and N == 128 and P == 64
    DI = cfg.DI
    NXT = DI // 128
    NCT = NXT + 2 * G
    GW = Hg * P
    NSL = H // 8
    I_bf, bI = K["ident_bf"], K["b"]
    U, ones32 = K["U"], K["ones32"]
    bK = K["b"]
    dtb, IA, Drep, normw, bl = lp["dtb"], lp["IA"], lp["Drep"], lp["normw"], lp["b"]
    sb = C.sb
    zt = [sb([128, NXT, 128], BF16) for _ in range(2)]; bzt = [Buf(), Buf()]
    dtr = [sb([H, 128], F32) for _ in range(2)]; bdtr = [Buf(), Buf()]
    xcbs = [sb([128, NCT, 128], BF16) for _ in range(2)]; bxcbs = [Buf(), Buf()]
    xs_tok = sb([128, DI], BF16); bxs = Buf()
    B_tok = sb([128, G * 128], BF16); bB = Buf()
    z_tok = sb([128, DI], BF16); bz = Buf()
    e1 = sb([H, 128], F32); be1 = Buf()
    dtT = sb([H, 128], F32); bdtT = Buf()
    dtk = sb([128, 2 * H], F32); bdtk = Buf()
    acs = sb([128, 2 * H], F32); bacs = Buf()
    dte = sb([128, H], F32); bdte = Buf()
    ea = sb([128, 2 * H], F32); bea = Buf()
    xdt = sb([128, DI], BF16); bxdt = Buf()
    xw = sb([128, DI], BF16); bxw = Buf()
    cbm = sb([128, G, 128], F32); bcbm = Buf()
    NR = 4
    R1 = [sb([128, 4, 128], F32) for _ in range(NR)]; bR1 = [Buf() for _ in range(NR)]
    tt = [sb([128, 4, 128], F32) for _ in range(NR)]; btt = [Buf() for _ in range(NR)]
    Mt = [sb([128, 4, 128], BF16) for _ in range(NR)]; bMt = [Buf() for _ in range(NR)]
    y = sb([128, DI], F32); by = Buf()
    y2f = sb([128, DI], F32); by2f = Buf()
    S32 = sb([128, DI], F32); bS32 = Buf()
    Sbf = sb([128, DI], BF16); bSbf = Buf()
    szs = [sb([128, 512], F32) for _ in range(2)]; bszs = [Buf(), Buf()]
    junk = sb([128, GW], F32); bjunk = Buf()
    ssq = sb([128, G], F32); bssq = Buf()
    yn, byn = xw, bxw
    yT1 = sb([128, NXT, 128], BF16); byT1 = Buf()
    yT = [yT1, yT1]; byT = [byT1, byT1]

    S.op("dve", lambda: nc.vector.memset(S32[:], 0.0), writes=[bS32])
    S.op("dve", lambda: nc.vector.memset(Sbf[:], 0.0), writes=[bSbf])
    def load(c):
        s = c % 2
        S.dma("sp", xcbs[s][:], XBCT[c], writes=[bxcbs[s]])
        S.dma("sp", zt[s][:], ZT[c], writes=[bzt[s]])
        S.dma("sp", dtr[s][:], DTT[c], writes=[bdtr[s]])

    load(0)
    for c in range(NCH):
        s = c % 2
        if c + 1 < NCH:
            load(c + 1)
        xcb, bxcb = xcbs[s], bxcbs[s]
        transpose_tiles(C, lambda t: xcb[:, t, :], NXT, bxcb, xs_tok, bxs, I_bf, bI)
        S.op("pool", lambda: nc.gpsimd.tensor_tensor(y2f[:].rearrange("p (h q) -> p h q", q=P), xs_tok[:].rearrange("p (h q) -> p h q", q=P),
                                                     _bc_last(Drep[:, 0:H], [128, H, P]), ALU.mult), reads=[bxs, bl], writes=[by2f])
        transpose_tiles(C, lambda t: xcb[:, NXT + t, :], G, bxcb, B_tok, bB, I_bf, bI)
        transpose_tiles(C, lambda t: zt[s][:, t, :], NXT, bzt[s], z_tok, bz, I_bf, bI)
        S.op("act", lambda: nc.scalar.activation(e1[:], dtr[s][:], AF.Exp, bias=dtb[0:H, 0:1]), reads=[bdtr[s], bl], writes=[be1])
        S.op("act", lambda: nc.scalar.activation(dtT[:], e1[:], AF.Ln, bias=1.0), reads=[be1], writes=[bdtT])
        b = _bank(C)
        S.op("pe", lambda: nc.tensor.matmul(C.ps[b][:, 0:2 * H], dtT[:], IA[0:H, 0:2 * H], start=True, stop=True),
             reads=[bdtT, bl], writes=[C.bps[b]])
        S.op("dve", lambda: nc.vector.tensor_copy(dtk[:], C.ps[b][:, 0:2 * H]), reads=[C.bps[b]], writes=[bdtk])
        dt_tok, dtA = dtk[:, 0:H], dtk[:, H:2 * H]
        b = _bank(C)
        S.op("pe", lambda: nc.tensor.matmul(C.ps[b][:, 0:H], U[:], dtA, start=True, stop=True), reads=[bK, bdtk], writes=[C.bps[b]], inc=False)
        S.op("pe", lambda: nc.tensor.matmul(C.ps[b][:, H:2 * H], ones32[:], dtA, start=True, stop=True), reads=[bK, bdtk], writes=[C.bps[b]])
        S.op("dve", lambda: nc.vector.tensor_copy(acs[:], C.ps[b][:, 0:2 * H]), reads=[C.bps[b]], writes=[bacs])
        a_cs, a_tot = acs[:, 0:H], acs[:, H:2 * H]
        S.op("dve", lambda: nc.vector.tensor_tensor(dte[:], a_tot, a_cs, ALU.subtract), reads=[bacs], writes=[bdte])
        S.op("act", lambda: nc.scalar.activation(dte[:], dte[:], AF.Exp), reads=[bdte], writes=[bdte])
        S.op("dve", lambda: nc.vector.tensor_tensor(dte[:], dte[:], dt_tok, ALU.mult), reads=[bdte, bdtk], writes=[bdte])
        S.op("act", lambda: nc.scalar.activation(ea[:], acs[:], AF.Exp), reads=[bacs], writes=[bea])
        shp3 = [128, H, P]
        S.op("dve", lambda: nc.vector.tensor_tensor(xdt[:].rearrange("p (h q) -> p h q", q=P), xs_tok[:].rearrange("p (h q) -> p h q", q=P),
                                                    _bc_last(dt_tok, shp3), ALU.mult), reads=[bxs, bdtk], writes=[bxdt])
        for g0 in range(0, G, 4):
            b = _bank(C)
            for g in range(g0, g0 + 4):
                S.op("pe", lambda: nc.tensor.matmul(C.ps[b][:, (g - g0) * 128:(g - g0 + 1) * 128], xcb[:, NXT + g, :], xcb[:, NXT + G + g, :],
                                                    start=True, stop=True), reads=[bxcb], writes=[C.bps[b]], inc=(g == g0 + 3))
            S.op("dve", lambda: nc.vector.tensor_tensor(cbm[:, g0:g0 + 4, :], C.ps[b][:].rearrange("p (a i) -> p a i", i=128),
                                                        _bc_mid(U[:], [128, 4, 128]), ALU.mult), reads=[C.bps[b], bK], writes=[bcbm])
        for sl in range(NSL):
            h8 = slice(sl * 8, sl * 8 + 8)
            cols = slice(sl * 512, sl * 512 + 512)
            byo = _bank(C)
            byd = _bank(C)
            glist = list(range(sl * 8 // Hg, (sl * 8 + 8) // Hg))
            for gi, g in enumerate(glist):
                S.op("pe", lambda: nc.tensor.matmul(C.ps[byo][:, gi * GW:(gi + 1) * GW], xcb[:, NXT + G + g, :], Sbf[:, g * GW:(g + 1) * GW],
                                                    start=True, stop=True), reads=[bxcb, bSbf], writes=[C.bps[byo]], inc=(gi == len(glist) - 1))
            for q4 in range(2):
                h0 = sl * 8 + q4 * 4
                r = (sl * 2 + q4) % NR
                g = h0 // Hg
                sh4 = [128, 4, 128]
                S.op("pool", lambda: nc.gpsimd.tensor_tensor(R1[r][:], _bc_mid(U[:], sh4), _bc_last(dtA[:, h0:h0 + 4], sh4), ALU.mult),
                     reads=[bK, bdtk], writes=[bR1[r]])
                b = _bank(C)
                S.op("pe", lambda: nc.tensor.matmul(C.ps[b][:, 0:512], ones32[:], R1[r][:].rearrange("p a i -> p (a i)"), start=True, stop=True),
                     reads=[bK, bR1[r]], writes=[C.bps[b]])
                acs_b = _bc_last(a_cs[:, h0:h0 + 4], sh4)
                S.op("dve", lambda: nc.vector.tensor_tensor(tt[r][:], C.ps[b][:].rearrange("p (a i) -> p a i", i=128), acs_b, ALU.min),
                     reads=[C.bps[b], bacs], writes=[btt[r]])
                S.op("dve", lambda: nc.vector.tensor_tensor(tt[r][:], tt[r][:], acs_b, ALU.subtract), reads=[btt[r], bacs], writes=[btt[r]])
                S.op("act", lambda: nc.scalar.activation(tt[r][:], tt[r][:], AF.Exp), reads=[btt[r]], writes=[btt[r]])
                S.op("dve", lambda: nc.vector.tensor_tensor(Mt[r][:], tt[r][:], _bc_mid(cbm[:, g, :], sh4), ALU.mult),
                     reads=[btt[r], bcbm], writes=[bMt[r]])
                for k in range(4):
                    h = h0 + k
                    co = (h - sl * 8) * P
                    S.op("pe", lambda: nc.tensor.matmul(C.ps[byd][:, co:co + P], Mt[r][:, k, :], xdt[:, h * P:(h + 1) * P], start=True, stop=True),
                         reads=[bMt[r], bxdt], writes=[C.bps[byd]], inc=(k == 3))
            sh8 = [128, 8, P]
            v3 = lambda ap: ap.rearrange("p (h q) -> p h q", q=P)
            S.op("dve", lambda: nc.vector.tensor_tensor(v3(y[:, cols]), v3(C.ps[byo][:, 0:512]), _bc_last(ea[:, h8], sh8), ALU.mult),
                 reads=[C.bps[byo], bea], writes=[by])
            S.op("dve", lambda: nc.vector.tensor_tensor(y[:, cols], y[:, cols], C.ps[byd][:, 0:512], ALU.add), reads=[by, C.bps[byd]], writes=[by])
            S.op("dve", lambda: nc.vector.tensor_tensor(y[:, cols], y[:, cols], y2f[:, cols], ALU.add), reads=[by, by2f], writes=[by])
        S.op("pool", lambda: nc.gpsimd.tensor_tensor(xw[:].rearrange("p (h q) -> p h q", q=P), xs_tok[:].rearrange("p (h q) -> p h q", q=P),
                                                     _bc_last(dte[:], shp3), ALU.mult), reads=[bxs, bdte], writes=[bxw])
        for g in range(G):
            gc = slice(g * GW, (g + 1) * GW)
            b = _bank(C)
            S.op("pe", lambda: nc.tensor.matmul(C.ps[b][:, 0:GW], B_tok[:, g * 128:(g + 1) * 128], xw[:, gc], start=True, stop=True),
                 reads=[bB, bxw], writes=[C.bps[b]])
            vg = lambda ap: ap.rearrange("p (h q) -> p h q", q=P)
            S.op("dve", lambda: nc.vector.tensor_tensor(vg(S32[:, gc]), vg(S32[:, gc]), _bc_last(ea[:, H + g * Hg:H + (g + 1) * Hg], [128, Hg, P]), ALU.mult),
                 reads=[bS32, bea], writes=[bS32])
            S.op("dve", lambda: nc.vector.tensor_tensor(S32[:, gc], S32[:, gc], C.ps[b][:, 0:GW], ALU.add), reads=[bS32, C.bps[b]], writes=[bS32])
            S.op("act", lambda: nc.scalar.copy(Sbf[:, gc], S32[:, gc]), reads=[bS32], writes=[bSbf])
        for sl in range(NSL):
            cols = slice(sl * 512, sl * 512 + 512)
            S.op("act", lambda: nc.scalar.activation(szs[sl % 2][:], z_tok[:, cols], AF.Silu), reads=[bz], writes=[bszs[sl % 2]])
            S.op("dve", lambda: nc.vector.tensor_tensor(y[:, cols], y[:, cols], szs[sl % 2][:], ALU.mult), reads=[bszs[sl % 2], by], writes=[by])
        for g in range(G):
            S.op("act", lambda: nc.scalar.activation(junk[:], y[:, g * GW:(g + 1) * GW], AF.Square, accum_out=ssq[:, g:g + 1]),
                 reads=[by], writes=[bjunk, bssq])
        S.op("act", lambda: nc.scalar.activation(ssq[:], ssq[:], AF.Sqrt, bias=1e-5, scale=1.0 / GW), reads=[bssq], writes=[bssq])
        S.op("dve", lambda: nc.vector.reciprocal(ssq[:], ssq[:]), reads=[bssq], writes=[bssq])
        S.op("dve", lambda: nc.vector.tensor_tensor(yn[:].rearrange("p (g q) -> p g q", q=GW), y[:].rearrange("p (g q) -> p g q", q=GW),
                                                    _bc_last(ssq[:], [128, G, GW]), ALU.mult), reads=[by, bssq], writes=[byn])

        def evac(t0, n, psb, bps):
            S.op("dve", lambda: nc.vector.tensor_tensor(yT[s][:, t0:t0 + n, :], psb[:, 0:n * 128].rearrange("p (a i) -> p a i", i=128),
                                                        _bc_last(normw[:, t0:t0 + n], [128, n, 128]), ALU.mult),
                 reads=[bps, bl], writes=[byT[s]])
        transpose_tiles(C, lambda t: yn[:, t * 128:(t + 1) * 128], NXT, byn, None, None, I_bf, bI, evac=evac)
        S.dma("sp", YO[c][:, 0:NXT, :], yT[s][:], reads=[byT[s]])


def gla_pass(C, K, lp, QT, KTd, VT, GT, GKL, YO, NCH, tile_off):
    nc, S, cfg = C.nc, C.S, C.cfg
    GH, DK, DV, R = cfg.GH, cfg.DK, cfg.DV, cfg.R
    HK, HV = DK // GH, DV // GH
    NKT, NVT = DK // 128, DV // 128
    KPH = HK // 128
    assert HK % 128 == 0 and HV <= 512 and R == 16
    scale = float(HK) ** -0.5
    I_bf, bK = K["ident_bf"], K["b"]
    U, ones32, SU = K["U"], K["ones32"], K["SU"]
    gkw, gnw, bl = lp["gkw"], lp["gnw"], lp["b"]
    sb = C.sb
    qT = [sb([128, NKT, 128], BF16) for _ in range(2)]; bq = [Buf(), Buf()]
    kT = [sb([128, NKT, 128], BF16) for _ in range(2)]; bk = [Buf(), Buf()]
    vT = [sb([128, NVT, 128], BF16) for _ in range(2)]; bv = [Buf(), Buf()]
    gT = [sb([128, NVT, 128], BF16) for _ in range(2)]; bg = [Buf(), Buf()]
    gl = [sb([32, 128], F32) for _ in range(2)]; bgl = [Buf(), Buf()]
    sp = sb([128, DK], F32); bsp = Buf()
    eg = sb([128, NKT, 128], F32); beg = Buf()
    ek = sb([128, NKT, 128], F32); bek = Buf()
    qg = sb([128, NKT, 128], BF16); bqg = Buf()
    kg = sb([128, NKT, 128], BF16); bkg = Buf()
    k_tok = sb([128, DK], BF16); bkt = Buf()
    v_tok = sb([128, DV], BF16); bvt = Buf()
    g_tok = sb([128, DV], BF16); bgt = Buf()
    erev = sb([128, DK], F32); berev = Buf()
    kd = sb([128, DK], BF16); bkd = Buf()
    attm = sb([128, GH, 128], BF16); battm = Buf()
    Sg32 = sb([128, NKT, HV], F32); bSg = Buf()
    Sgb = sb([128, NKT, HV], BF16); bSgb = Buf()
    o32 = sb([128, DV], F32); bo = Buf()
    sg = sb([128, DV], F32); bsg = Buf()
    junk = sb([128, HV], F32); bjunk = Buf()
    ssq = sb([128, GH], F32); bssq = Buf()
    on = sb([128, DV], BF16); bon = Buf()
    oT = [sb([128, NVT, 128], BF16) for _ in range(2)]; boT = [Buf(), Buf()]

    S.op("dve", lambda: nc.vector.memset(Sg32[:], 0.0), writes=[bSg])
    S.op("dve", lambda: nc.vector.memset(Sgb[:], 0.0), writes=[bSgb])
    for s in range(2):
        S.op("pool", lambda: nc.gpsimd.memset(gl[s][:], 1.0), writes=[bgl[s]])

    def load(c):
        s = c % 2
        S.dma("sp", qT[s][:], QT[c], writes=[bq[s]])
        S.dma("sp", kT[s][:], KTd[c], writes=[bk[s]])
        S.dma("sp", vT[s][:], VT[c], writes=[bv[s]])
        S.dma("sp", gT[s][:], GT[c], writes=[bg[s]])
        S.dma("sp", gl[s][0:16, :], GKL[c], writes=[bgl[s]])

    load(0)
    for c in range(NCH):
        s = c % 2
        if c + 1 < NCH:
            load(c + 1)
        for n0 in range(0, DK, 512):
            nw = min(512, DK - n0)
            b = _bank(C)
            S.op("pe", lambda: nc.tensor.matmul(C.ps[b][:, 0:nw], gl[s][0:17, :], gkw[0:17, n0:n0 + nw], start=True, stop=True),
                 reads=[bgl[s], bl], writes=[C.bps[b]])
            S.op("act", lambda: nc.scalar.activation(sp[:, n0:n0 + nw], C.ps[b][:, 0:nw], AF.Exp, scale=-1.0), reads=[C.bps[b]], writes=[bsp])
        S.op("act", lambda: nc.scalar.activation(sp[:], sp[:], AF.Ln, bias=1.0), reads=[bsp], writes=[bsp])
        if getattr(cfg, "gla_stop", 99) == 1:
            return
        for t0 in range(0, NKT, 4):
            n = min(4, NKT - t0)
            b = _bank(C)
            for t in range(n):
                S.op("pe", lambda: nc.tensor.matmul(C.ps[b][:, t * 128:(t + 1) * 128], sp[:, (t0 + t) * 128:(t0 + t + 1) * 128], U[:], start=True, stop=True),
                     reads=[bsp, bK], writes=[C.bps[b]], inc=(t == n - 1))
            pv = C.ps[b][:, 0:n * 128].rearrange("p (a i) -> p a i", i=128)
            S.op("act", lambda: nc.scalar.activation(eg[:, t0:t0 + n, :], pv, AF.Exp, scale=-1.0 / 16), reads=[C.bps[b]], writes=[beg])
            S.op("act", lambda: nc.scalar.activation(ek[:, t0:t0 + n, :], pv, AF.Exp, scale=1.0 / 16), reads=[C.bps[b]], writes=[bek])
        S.op("dve", lambda: nc.vector.scalar_tensor_tensor(qg[:], qT[s][:], scale, eg[:], ALU.mult, ALU.mult), reads=[bq[s], beg], writes=[bqg])
        S.op("pool", lambda: nc.gpsimd.tensor_tensor(kg[:], kT[s][:], ek[:], ALU.mult), reads=[bk[s], bek], writes=[bkg])
        if getattr(cfg, "gla_stop", 99) == 2:
            return
        transpose_tiles(C, lambda t: kT[s][:, t, :], NKT, bk[s], k_tok, bkt, I_bf, bK)
        transpose_tiles(C, lambda t: vT[s][:, t, :], NVT, bv[s], v_tok, bvt, I_bf, bK)
        transpose_tiles(C, lambda t: gT[s][:, t, :], NVT, bg[s], g_tok, bgt, I_bf, bK)
        if getattr(cfg, "gla_stop", 99) == 3:
            return
        for n0 in range(0, DK, 512):
            nw = min(512, DK - n0)
            b = _bank(C)
            S.op("pe", lambda: nc.tensor.matmul(C.ps[b][:, 0:nw], SU[:], sp[:, n0:n0 + nw], start=True, stop=True), reads=[bK, bsp], writes=[C.bps[b]])
            S.op("act", lambda: nc.scalar.activation(erev[:, n0:n0 + nw], C.ps[b][:, 0:nw], AF.Exp, scale=-1.0 / 16), reads=[C.bps[b]], writes=[berev])
        S.op("dve", lambda: nc.vector.tensor_tensor(kd[:], k_tok[:], erev[:], ALU.mult), reads=[bkt, berev], writes=[bkd])
        if getattr(cfg, "gla_stop", 99) == 4:
            return
        for h0 in range(0, GH, 4):
            nh = min(4, GH - h0)
            b = _bank(C)
            for hh in range(nh):
                h = h0 + hh
                for kk in range(KPH):
                    t = h * KPH + kk
                    S.op("pe", lambda: nc.tensor.matmul(C.ps[b][:, hh * 128:(hh + 1) * 128], kg[:, t, :], qg[:, t, :], start=(kk == 0), stop=(kk == KPH - 1)),
                         reads=[bkg, bqg], writes=[C.bps[b]], inc=(hh == nh - 1 and kk == KPH - 1))
            S.op("dve", lambda: nc.vector.tensor_tensor(attm[:, h0:h0 + nh, :], C.ps[b][:, 0:nh * 128].rearrange("p (a i) -> p a i", i=128),
                                                        _bc_mid(U[:], [128, nh, 128]), ALU.mult), reads=[C.bps[b], bK], writes=[battm])
        if getattr(cfg, "gla_stop", 99) == 5:
            return
        for h in range(GH):
            b = _bank(C)
            vc = slice(h * HV, (h + 1) * HV)
            S.op("pe", lambda: nc.tensor.matmul(C.ps[b][:, 0:HV], attm[:, h, :], v_tok[:, vc], start=True, stop=False),
                 reads=[battm, bvt], writes=[C.bps[b]], inc=False)
            for kk in range(KPH):
                t = h * KPH + kk
                S.op("pe", lambda: nc.tensor.matmul(C.ps[b][:, 0:HV], qg[:, t, :], Sgb[:, t, :], start=False, stop=(kk == KPH - 1)),
                     reads=[bqg, bSgb], writes=[C.bps[b]], inc=(kk == KPH - 1))
            S.op("dve", lambda: nc.vector.tensor_copy(o32[:, vc], C.ps[b][:, 0:HV]), reads=[C.bps[b]], writes=[bo])
            S.op("act", lambda: nc.scalar.activation(junk[:], o32[:, vc], AF.Square, accum_out=ssq[:, h:h + 1]), reads=[bo], writes=[bjunk, bssq])
        if getattr(cfg, "gla_stop", 99) == 6:
            return
        for t in range(NKT):
            h = t // KPH
            b = _bank(C)
            S.op("pe", lambda: nc.tensor.matmul(C.ps[b][:, 0:HV], kd[:, t * 128:(t + 1) * 128], v_tok[:, h * HV:(h + 1) * HV], start=True, stop=True),
                 reads=[bkd, bvt], writes=[C.bps[b]])
            S.op("dve", lambda: nc.vector.scalar_tensor_tensor(Sg32[:, t, :], Sg32[:, t, :], eg[:, t, 127:128], C.ps[b][:, 0:HV], ALU.mult, ALU.add),
                 reads=[bSg, beg, C.bps[b]], writes=[bSg])
            S.op("act", lambda: nc.scalar.copy(Sgb[:, t, :], Sg32[:, t, :]), reads=[bSg], writes=[bSgb])
        if getattr(cfg, "gla_stop", 99) == 7:
            return
        S.op("act", lambda: nc.scalar.activation(ssq[:], ssq[:], AF.Sqrt, bias=1e-5, scale=1.0 / HV), reads=[bssq], writes=[bssq])
        S.op("dve", lambda: nc.vector.reciprocal(ssq[:], ssq[:]), reads=[bssq], writes=[bssq])
        S.op("act", lambda: nc.scalar.activation(sg[:], g_tok[:], AF.Silu), reads=[bgt], writes=[bsg])
        S.op("pool", lambda: nc.gpsimd.tensor_tensor(sg[:], sg[:], o32[:], ALU.mult), reads=[bsg, bo], writes=[bsg])
        S.op("dve", lambda: nc.vector.tensor_tensor(on[:].rearrange("p (h q) -> p h q", q=HV), sg[:].rearrange("p (h q) -> p h q", q=HV),
                                                    _bc_last(ssq[:], [128, GH, HV]), ALU.mult), reads=[bsg, bssq], writes=[bon])

        def evac(t0, n, psb, bps):
            S.op("dve", lambda: nc.vector.tensor_tensor(oT[s][:, t0:t0 + n, :], psb[:, 0:n * 128].rearrange("p (a i) -> p a i", i=128),
                                                        _bc_last(gnw[:, t0:t0 + n], [128, n, 128]), ALU.mult),
                 reads=[bps, bl], writes=[boT[s]])
        transpose_tiles(C, lambda t: on[:, t * 128:(t + 1) * 128], NVT, bon, None, None, I_bf, bK, evac=evac)
        S.dma("sp", YO[c][:, tile_off:tile_off + NVT, :], oT[s][:], reads=[boT[s]])


def barrier(C):
    S = C.S
    toks = [(S.sem[e], S.cnt[e]) for e in ("pe", "act", "dve", "pool") if S.cnt[e] > 0]
    toks += [(sem, 16 * S.dma_cnt[i]) for i, sem in enumerate(S.dma_sems) if S.dma_cnt[i] > 0]
    toks += [(sem, 16) for sem in S.sw.values()]
    for e in ("pe", "act", "dve", "pool", "sp"):
        for t in toks:
            S._wait(e, t)


class Phase:
    def __init__(self, C):
        self.C = C

    def __enter__(self):
        self.stack = contextlib.ExitStack()
        self.stack.__enter__()
        self.prev = self.C.st
        self.C.st = self.stack
        return self

    def __exit__(self, *a):
        barrier(self.C)
        self.C.st = self.prev
        return self.stack.__exit__(*a)


def gemm_multi(C, W, K, n0, n1, rhs_tiles, brhs, tbs, T, epilogue):
    nc, S = C.nc, C.S
    KT = K // 128
    assert KT <= 16
    bc = 512
    jblk = 0
    for c0 in range(n0, n1, bc):
        cw = min(bc, n1 - c0)
        w = wnext(C, W, jblk)
        wv = C.wb[w]
        jblk += 1
        for lo in range(0, cw, 128):
            ncol = min(128, cw - lo)
            for tb in tbs:
                b = C.psi % 8
                C.psi += 1
                for kt in range(KT):
                    S.op("pe", lambda: nc.tensor.matmul(C.ps[b][0:ncol, 0:T], wv[:, kt, lo:lo + ncol], rhs_tiles(kt, tb),
                                                        start=(kt == 0), stop=(kt == KT - 1)),
                         reads=[C.bwb[w], brhs], writes=[C.bps[b]], inc=(kt == KT - 1))
                epilogue(c0 + lo, ncol, C.ps[b][0:ncol, 0:T], C.bps[b], tb)


def alloc_wbufs(C):
    C.wb = [C.sb([128, 16, 512], BF16) for _ in range(C.NW)]
    C.bwb = [Buf(f"wb{i}") for i in range(C.NW)]


_CACHE = {}


def lp_layout(cfg):
    NXT = cfg.DI // 128
    NCT = NXT + 2 * cfg.G
    NVT = cfg.DV // 128
    NDT = cfg.D // 128
    off, o = {}, 0
    for name, n in [("cw", 4 * NCT), ("cb", NCT), ("dtb", 1), ("alog", 1), ("Drep", cfg.H), ("normw", NXT), ("gnw", NVT),
                    ("gb0", NDT), ("gb1", NDT), ("ln1g", NDT), ("ln1b", NDT), ("ln2g", NDT), ("ln2b", NDT)]:
        off[name] = (o, n)
        o += n
    return off, o


def build_program(cfg):
    nc = bass.Bass("TRN2", target_bir_lowering=False)
    D, L, depth, H, G = cfg.D, cfg.L, cfg.depth, cfg.H, cfg.G
    DI, CONV, DK, DV, FF, DIN = cfg.DI, cfg.CONV, cfg.DK, cfg.DV, cfg.FF, cfg.DIN
    NDT, NXT, NVT, NKT = D // 128, DI // 128, DV // 128, DK // 128
    NCT = NXT + 2 * G
    NB, NCH, T = L // 512, L // 128, 512
    NYO = NXT + NVT
    off, NLP = lp_layout(cfg)
    ext = lambda n, sh, dt=F32: nc.dram_tensor(n, list(sh), dt, kind="ExternalInput").ap()
    X0 = ext("X0", [NB, 128, NDT, T])
    OUT = nc.dram_tensor("OUT", [NB, 128, NDT, T], F32, kind="ExternalOutput").ap()
    Win = ext("w_in", [depth, D, DIN]); Wsb = ext("w_ssd_branch", [depth, DI, D]); Wgb = ext("w_gla_branch", [depth, DV, D])
    Wout = ext("w_out", [depth, D, D]); Wup = ext("w_up", [depth, D, FF]); Wdn = ext("w_down", [depth, FF, D])
    LP = ext("LP", [depth, 128, NLP]); GKW = ext("GKW", [depth, 17, DK]); KC = ext("KC", [128, 4, 128])
    scr = lambda n, sh, dt: nc.dram_tensor(n, list(sh), dt, kind="Internal").ap()
    XR = scr("XR", [NB, 128, NDT, T], F32); XB = scr("XB", [NB, 128, NDT, T], BF16)
    ZT = scr("ZT", [NCH, 128, NXT, 128], BF16); XBCT = scr("XBCT", [NCH, 128, NCT, 128], BF16)
    DTT = scr("DTT", [NCH, H, 128], F32); GKL = scr("GKL", [NCH, 16, 128], F32)
    QT = scr("QT", [NCH, 128, NKT, 128], BF16); KTd = scr("KTd", [NCH, 128, NKT, 128], BF16)
    VT = scr("VT", [NCH, 128, NVT, 128], BF16); GT = scr("GT", [NCH, 128, NVT, 128], BF16)
    GATE = scr("GATE", [NB, 128, 2 * NDT, T], BF16); YO = scr("YO", [NCH, 128, NYO, 128], BF16)
    NFF = 4
    FFh_ = FF // NFF
    names = ["z", "xbc", "dt", "q", "k", "v", "g", "gkl", "gs", "gg"]
    segs, o = {}, 0
    for nme, n in zip(names, cfg.splits):
        segs[nme] = (o, o + n)
        o += n
    with contextlib.ExitStack() as st:
        C = Ctx(nc, st, cfg)
        S = C.S
        kc = C.sb([128, 4, 128], F32); bK = Buf()
        ib = C.sb([128, 128], BF16)
        ones_bf = C.sb([128, 128], BF16)
        lpt = C.sb([128, NLP], F32); bl = Buf()
        gkw = C.sb([32, DK], F32)
        IA = C.sb([128, 2 * H], F32)
        negA = C.sb([128, 1], F32)
        S.dma("sp", kc[:], KC[:], writes=[bK])
        S.op("dve", lambda: nc.vector.tensor_copy(ib[:], kc[:, 0, :]), reads=[bK], writes=[bK])
        S.op("dve", lambda: nc.vector.tensor_copy(ones_bf[:], kc[:, 2, :]), reads=[bK], writes=[bK])
        K = dict(ident_bf=ib, U=kc[:, 1, :], ones32=kc[:, 2, :], SU=kc[:, 3, :], b=bK)
        P = lambda nme: lpt[:, off[nme][0]:off[nme][0] + off[nme][1]]

        with Phase(C):
            xf = [C.sb([128, NDT, T], F32) for _ in range(2)]; bxf = [Buf(), Buf()]
            xb = [C.sb([128, NDT, T], BF16) for _ in range(2)]; bxb = [Buf(), Buf()]
            for tb in range(NB):
                s = tb % 2
                S.dma("sp", xf[s][:], X0[tb], writes=[bxf[s]])
                S.op("dve" if s else "act", (lambda: nc.vector.tensor_copy(xb[s][:], xf[s][:])) if s else (lambda: nc.scalar.copy(xb[s][:], xf[s][:])),
                     reads=[bxf[s]], writes=[bxb[s]])
                S.dma("sp", XB[tb], xb[s][:], reads=[bxb[s]])

        stop_after = getattr(cfg, "stop_after", None)
        for l in range(depth):
            XRsrc = X0 if l == 0 else XR
            XRdst = OUT if l == depth - 1 else XR
            S.dma("sp", lpt[:], LP[l], writes=[bl])
            S.dma("sp", gkw[0:17, :], GKW[l], writes=[bl])
            S.op("act", lambda: nc.scalar.activation(negA[0:H, :], P("alog")[0:H, :], AF.Exp), reads=[bl], writes=[bl])
            S.op("dve", lambda: nc.vector.tensor_copy(IA[0:H, 0:H], kc[0:H, 0, 0:H]), reads=[bK, bl], writes=[bl])
            S.op("dve", lambda: nc.vector.tensor_scalar(IA[0:H, H:2 * H], kc[0:H, 0, 0:H], negA[0:H, 0:1], -1.0, ALU.mult, ALU.mult),
                 reads=[bK, bl], writes=[bl])
            for e in ("pe", "act", "dve", "pool"):
                S.wait_all(e, [bl, bK])
            lp = dict(cw=P("cw").rearrange("p (k t) -> p k t", t=NCT), cb=P("cb"), dtb=P("dtb"), IA=IA, Drep=P("Drep"), normw=P("normw"),
                      gkw=gkw, gnw=P("gnw"), b=bl)

            specs = [(("in", nme), Win[l], D, segs[nme][0], segs[nme][1]) for nme in names]
            specs += [("sb", Wsb[l], DI, 0, D), ("gb", Wgb[l], DV, 0, D), ("out", Wout[l], D, 0, D)]
            for hf in range(NFF):
                specs += [(("up", hf), Wup[l], D, hf * FFh_, (hf + 1) * FFh_), (("dn", hf), Wdn[l][hf * FFh_:(hf + 1) * FFh_, :], FFh_, 0, D)]
            if C.WBF is None:
                nblk = sum(len(wblock_list(Kk, a, b_)) for (_, _, Kk, a, b_) in specs)
                C.WBF = scr("WBF", [nblk, 128, 8192], BF16)
            with Phase(C):
                wf = [C.sb([128, 16, 512], F32) for _ in range(2)]; bwf = [Buf(), Buf()]
                wq = [C.sb([128, 16, 512], BF16) for _ in range(3)]; bwq = [Buf(), Buf(), Buf()]
                idx = 0
                for (key, Wap, Kk, a, b_) in specs:
                    C.wbase[key] = idx
                    for (k0, nk, c0, cw) in wblock_list(Kk, a, b_):
                        s2 = idx % 2
                        s3 = idx % 3
                        S.dma("sp", wf[s2][:, 0:nk, 0:cw], Wap[k0 * 128:(k0 + nk) * 128, c0:c0 + cw].rearrange("(kt p) n -> p kt n", p=128), writes=[bwf[s2]])
                        if s3 == 0:
                            S.op("act", lambda: nc.scalar.copy(wq[s3][:, 0:nk, 0:cw], wf[s2][:, 0:nk, 0:cw]), reads=[bwf[s2]], writes=[bwq[s3]])
                        elif s3 == 1:
                            S.op("dve", lambda: nc.vector.tensor_copy(wq[s3][:, 0:nk, 0:cw], wf[s2][:, 0:nk, 0:cw]), reads=[bwf[s2]], writes=[bwq[s3]])
                        else:
                            S.op("pool", lambda: nc.gpsimd.tensor_copy(wq[s3][:, 0:nk, 0:cw], wf[s2][:, 0:nk, 0:cw]), reads=[bwf[s2]], writes=[bwq[s3]])
                        S.dma("sp", C.WBF[idx][:, 0:nk * cw].rearrange("p (a b) -> p a b", b=cw), wq[s3][:, 0:nk, 0:cw], reads=[bwq[s3]])
                        idx += 1

            if stop_after == "precast":
                break
            with Phase(C):
                alloc_wbufs(C)
                TBn = min(4, NB)
                xT = C.sb([128, NDT, TBn * T], BF16); bxT = Buf()
                stg = [C.sb([128, T], BF16) for _ in range(4)]; bstg = [Buf() for _ in range(4)]
                stf = [C.sb([128, T], F32) for _ in range(2)]; bstf = [Buf() for _ in range(2)]
                pre = [C.sb([128, T + 3], F32) for _ in range(2)]; bpre = [Buf(), Buf()]
                cacc = [C.sb([128, T], F32) for _ in range(2)]; bcacc = [Buf(), Buf()]
                ct1 = [C.sb([128, T], F32) for _ in range(2)]; bct1 = [Buf(), Buf()]
                ct2 = [C.sb([128, T], F32) for _ in range(2)]; bct2 = [Buf(), Buf()]
                hal = C.sb([128, NCT, 3], F32); bhal = Buf()
                S.op("pool", lambda: nc.gpsimd.memset(hal[:], 0.0), writes=[bhal])
                cwv, cbv = lp["cw"], lp["cb"]
                cnt = [0]
                ccnt = [0]

                def make_epi(nme, sb0):
                    n0 = segs[nme][0]

                    def epi(col, ncol, ps, bps, tb):
                        tile = (col - n0) // 128
                        i = cnt[0]; cnt[0] += 1
                        if nme in ("dt", "gkl"):
                            j = i % 2
                            S.op("dve", lambda: nc.vector.tensor_copy(stf[j][0:ncol, :], ps), reads=[bps], writes=[bstf[j]])
                            dst = (DTT if nme == "dt" else GKL)[tb * 4:tb * 4 + 4, 0:ncol, :].rearrange("c p i -> p c i")
                            S.dma("sp", dst, stf[j][0:ncol, :].rearrange("p (c i) -> p c i", i=128), reads=[bstf[j]])
                            return
                        j = i % 4
                        if nme == "xbc":
                            q = ccnt[0] % 2; ccnt[0] += 1
                            assert ncol == 128
                            S.op("pool", lambda: nc.gpsimd.tensor_copy(pre[q][:, 0:3], hal[:, tile, :]), reads=[bhal], writes=[bpre[q]])
                            S.op("act", lambda: nc.scalar.copy(pre[q][:, 3:T + 3], ps), reads=[bps], writes=[bpre[q]])
                            S.op("pool", lambda: nc.gpsimd.tensor_copy(hal[:, tile, :], pre[q][:, T:T + 3]), reads=[bpre[q]], writes=[bhal])
                            S.op("act", lambda: nc.scalar.activation(ct1[q][:], pre[q][:, 1:1 + T], AF.Identity, scale=cwv[:, 1, tile:tile + 1]),
                                 reads=[bpre[q], bl], writes=[bct1[q]])
                            S.op("dve", lambda: nc.vector.scalar_tensor_tensor(cacc[q][:], pre[q][:, 0:T], cwv[:, 0, tile:tile + 1], ct1[q][:], ALU.mult, ALU.add),
                                 reads=[bpre[q], bct1[q], bl], writes=[bcacc[q]])
                            for k in (2, 3):
                                S.op("dve", lambda: nc.vector.scalar_tensor_tensor(cacc[q][:], pre[q][:, k:k + T], cwv[:, k, tile:tile + 1], cacc[q][:], ALU.mult, ALU.add),
                                     reads=[bpre[q], bcacc[q], bl], writes=[bcacc[q]])
                            S.op("act", lambda: nc.scalar.activation(stg[j][:, :], cacc[q][:], AF.Silu, bias=cbv[:, tile:tile + 1]), reads=[bcacc[q], bl], writes=[bstg[j]])
                        elif i % 2:
                            S.op("act", lambda: nc.scalar.copy(stg[j][0:ncol, :], ps), reads=[bps], writes=[bstg[j]])
                        else:
                            S.op("dve", lambda: nc.vector.tensor_copy(stg[j][0:ncol, :], ps), reads=[bps], writes=[bstg[j]])
                        if nme in ("gs", "gg"):
                            S.dma("sp", GATE[tb, 0:ncol, tile + (NDT if nme == "gg" else 0), :], stg[j][0:ncol, :], reads=[bstg[j]])
                        else:
                            dt_ = dict(z=ZT, xbc=XBCT, q=QT, k=KTd, v=VT, g=GT)[nme]
                            dst = dt_[tb * 4:tb * 4 + 4, 0:ncol, tile, :].rearrange("c p i -> p c i")
                            S.dma("sp", dst, stg[j][0:ncol, :].rearrange("p (c i) -> p c i", i=128), reads=[bstg[j]])
                    return epi

                ent = []
                for sb0 in range(0, NB, TBn):
                    for nme in names:
                        ent += wstream_entries(("in", nme), D, segs[nme][0], segs[nme][1])
                wstream_begin(C, ent)
                for sb0 in range(0, NB, TBn):
                    tbs = list(range(sb0, min(sb0 + TBn, NB)))
                    for j, tb in enumerate(tbs):
                        S.dma("sp", xT[:, :, j * T:(j + 1) * T], XB[tb], writes=[bxT])
                    for nme in names:
                        gemm_multi(C, ("in", nme), D, segs[nme][0], segs[nme][1], lambda kt, tb: xT[:, kt, (tb - sb0) * T:(tb - sb0 + 1) * T],
                                   bxT, tbs, T, make_epi(nme, sb0))

            if stop_after == "p1":
                break
            with Phase(C):
                ssd_pass(C, K, lp, ZT, XBCT, DTT, YO, NCH)
            if stop_after == "ssd":
                break
            with Phase(C):
                gla_pass(C, K, lp, QT, KTd, VT, GT, GKL, YO, NCH, NXT)
            if stop_after == "gla":
                break

            with Phase(C):
                alloc_wbufs(C)
                FFh = FF // NFF
                yo = C.sb([128, max(NXT, NVT), T], BF16); byo = Buf()
                ubf = C.sb([128, FFh // 128, T], BF16); bubf = Buf()
                assert max(NXT, NVT) >= 2 * NDT
                C.ln_vb = yo[:, 0:NDT, :]; C.b_ln_vb = byo
                C.ln_sq = yo[:, NDT:2 * NDT, :]; C.b_ln_sq = byo
                C.ln_st = C.sb([128, 3, T], F32); C.b_ln_st = Buf()
                f32a = C.sb([128, NDT, T], F32); bfa32 = Buf()
                f32b = C.sb([128, NDT, T], F32); bfb32 = Buf()
                bfa = C.sb([128, NDT, T], BF16); bbfa = Buf()
                gt = [C.sb([128, T], BF16) for _ in range(2)]; bgt = [Buf(), Buf()]
                tmp = [C.sb([128, T], F32) for _ in range(2)]; btmp = [Buf(), Buf()]
                tm2 = [C.sb([128, T], F32) for _ in range(2)]; btm2 = [Buf(), Buf()]
                ec = [0]
                gb0, gb1 = P("gb0"), P("gb1")
                alpha = float(cfg.alpha)
                ent = []
                for tb in range(NB):
                    ent += wstream_entries("sb", DI, 0, D) + wstream_entries("gb", DV, 0, D) + wstream_entries("out", D, 0, D)
                    for hf in range(NFF):
                        ent += wstream_entries(("up", hf), D, hf * FFh, (hf + 1) * FFh) + wstream_entries(("dn", hf), FFh, 0, D)
                wstream_begin(C, ent)
                for tb in range(NB):
                    S.dma("sp", f32b[:], XRsrc[tb], writes=[bfb32])
                    for c4 in range(4):
                        S.dma("sp", yo[:, 0:NXT, c4 * 128:(c4 + 1) * 128], YO[tb * 4 + c4][:, 0:NXT, :], writes=[byo])

                    def epi_bs(col, ncol, ps, bps):
                        nt = col // 128
                        i = ec[0] % 2; ec[0] += 1
                        S.dma("sp", gt[i][:], GATE[tb, :, nt, :], writes=[bgt[i]])
                        S.op("act", lambda: nc.scalar.activation(tmp[i][:], gt[i][:], AF.Sigmoid, bias=gb0[:, nt:nt + 1]), reads=[bgt[i], bl], writes=[btmp[i]])
                        S.op("dve", lambda: nc.vector.tensor_tensor(f32a[:, nt, :], tmp[i][:], ps, ALU.mult), reads=[btmp[i], bps], writes=[bfa32])
                    gemm_fm(C, "sb", DI, 0, D, lambda kt: yo[:, kt, :], byo, T, epi_bs)
                    for c4 in range(4):
                        S.dma("sp", yo[:, 0:NVT, c4 * 128:(c4 + 1) * 128], YO[tb * 4 + c4][:, NXT:NXT + NVT, :], writes=[byo])

                    def epi_bg(col, ncol, ps, bps):
                        nt = col // 128
                        i = ec[0] % 2; ec[0] += 1
                        S.dma("sp", gt[i][:], GATE[tb, :, NDT + nt, :], writes=[bgt[i]])
                        S.op("act", lambda: nc.scalar.activation(tmp[i][:], gt[i][:], AF.Sigmoid, bias=gb1[:, nt:nt + 1]), reads=[bgt[i], bl], writes=[btmp[i]])
                        S.op("dve", lambda: nc.vector.tensor_tensor(tm2[i][:], tmp[i][:], ps, ALU.mult), reads=[btmp[i], bps], writes=[btm2[i]])
                        S.op("pool", lambda: nc.gpsimd.tensor_tensor(bfa[:, nt, :], tm2[i][:], f32a[:, nt, :], ALU.add), reads=[btm2[i], bfa32], writes=[bbfa])
                    gemm_fm(C, "gb", DV, 0, D, lambda kt: yo[:, kt, :], byo, T, epi_bg)

                    def epi_out(col, ncol, ps, bps):
                        nt = col // 128
                        S.op("dve", lambda: nc.vector.scalar_tensor_tensor(f32a[:, nt, :], f32b[:, nt, :], alpha, ps, ALU.mult, ALU.add),
                             reads=[bfb32, bps], writes=[bfa32])
                    gemm_fm(C, "out", D, 0, D, lambda kt: bfa[:, kt, :], bbfa, T, epi_out)
                    layernorm_fm(C, f32a, bfa32, NDT, T, P("ln1g"), P("ln1b"), f32b, bfb32, bfa, bbfa, ones_bf, bK)
                    for hf in range(NFF):
                        def epi_up(col, ncol, ps, bps):
                            ft = (col - hf * FFh) // 128
                            i = ec[0] % 2; ec[0] += 1
                            S.op("act", lambda: nc.scalar.activation(tmp[i][:], ps, AF.Relu), reads=[bps], writes=[btmp[i]])
                            if i:
                                S.op("pool", lambda: nc.gpsimd.tensor_tensor(ubf[:, ft, :], tmp[i][:], tmp[i][:], ALU.mult), reads=[btmp[i]], writes=[bubf])
                            else:
                                S.op("dve", lambda: nc.vector.tensor_tensor(ubf[:, ft, :], tmp[i][:], tmp[i][:], ALU.mult), reads=[btmp[i]], writes=[bubf])
                        gemm_fm(C, ("up", hf), D, hf * FFh, (hf + 1) * FFh, lambda kt: bfa[:, kt, :], bbfa, T, epi_up)

                        def epi_dn(col, ncol, ps, bps):
                            nt = col // 128
                            if hf == 0:
                                S.op("dve", lambda: nc.vector.scalar_tensor_tensor(f32a[:, nt, :], f32b[:, nt, :], alpha, ps, ALU.mult, ALU.add),
                                     reads=[bfb32, bps], writes=[bfa32])
                            else:
                                S.op("dve", lambda: nc.vector.tensor_tensor(f32a[:, nt, :], f32a[:, nt, :], ps, ALU.add), reads=[bfa32, bps], writes=[bfa32])
                        gemm_fm(C, ("dn", hf), FFh, 0, D, lambda kt: ubf[:, kt, :], bubf, T, epi_dn)
                    layernorm_fm(C, f32a, bfa32, NDT, T, P("ln2g"), P("ln2b"), f32b, bfb32, bfa, bbfa, ones_bf, bK)
                    S.dma("sp", XRdst[tb], f32b[:], reads=[bfb32])
                    if l < depth - 1:
                        S.dma("sp", XB[tb], bfa[:], reads=[bbfa])
        barrier(C)
        _CACHE["n_ins"] = dict(S.n_ins)
        _CACHE["nsem"] = S.nsem
    return nc


def host_inputs(cfg, inp, b):
    D, L, depth, H, G = cfg.D, cfg.L, cfg.depth, cfg.H, cfg.G
    NDT, NXT, NVT = D // 128, cfg.DI // 128, cfg.DV // 128
    NCT = NXT + 2 * G
    NB = L // 512
    HV = cfg.DV // cfg.GH
    off, NLP = lp_layout(cfg)
    f = lambda a: np.asarray(a, dtype=np.float32)
    x = f(inp["x"])[b]
    X0 = np.ascontiguousarray(x.reshape(NB, 512, NDT, 128).transpose(0, 3, 2, 1))
    LP = np.zeros((depth, 128, NLP), np.float32)
    tile_pm = lambda v, nt: v.reshape(nt, 128).T

    def put(l, nme, arr):
        o, n = off[nme]
        LP[l, :arr.shape[0], o:o + n] = arr
    for l in range(depth):
        put(l, "cw", f(inp["ssd_conv_w"])[l].reshape(4, NCT, 128).transpose(2, 0, 1).reshape(128, 4 * NCT))
        put(l, "cb", tile_pm(f(inp["ssd_conv_b"])[l], NCT))
        put(l, "dtb", f(inp["ssd_dt_bias"])[l][:, None])
        put(l, "alog", f(inp["ssd_A_log"])[l][:, None])
        put(l, "Drep", np.broadcast_to(f(inp["ssd_D"])[l][None, :], (128, H)))
        put(l, "normw", tile_pm(f(inp["ssd_norm_w"])[l], NXT))
        put(l, "gnw", tile_pm(np.tile(f(inp["gla_norm_w"])[l], cfg.GH), NVT))
        put(l, "gb0", tile_pm(f(inp["gate_bias"])[l, 0], NDT))
        put(l, "gb1", tile_pm(f(inp["gate_bias"])[l, 1], NDT))
        for nme in ("ln1_g", "ln1_b", "ln2_g", "ln2_b"):
            put(l, nme.replace("_", ""), tile_pm(f(inp[nme])[l], NDT))
    GKW = np.concatenate([f(inp["gla_gk_w"]), f(inp["gla_gk_b"])[:, None, :]], axis=1)
    KC = np.zeros((128, 4, 128), np.float32)
    KC[:, 0, :] = np.eye(128)
    KC[:, 1, :] = np.triu(np.ones((128, 128)))
    KC[:, 2, :] = 1.0
    KC[:, 3, :] = np.tril(np.ones((128, 128)), -1)
    m = {"X0": X0, "LP": LP, "GKW": np.ascontiguousarray(GKW), "KC": KC}
    for k in ("w_in", "w_ssd_branch", "w_gla_branch", "w_out", "w_up", "w_down"):
        m[k] = f(inp[k])
    return m


def host_output(cfg, OUT):
    NB, _, NDT, T = OUT.shape
    return OUT.transpose(0, 3, 2, 1).reshape(NB * T, NDT * 128)


def kernel(**inputs):
    cfg = Cfg()
    B = np.asarray(inputs["x"]).shape[0]
    assert B == N_CORES_USED
    if "nc" not in _CACHE:
        _CACHE["nc"] = build_program(cfg)
    nc = _CACHE["nc"]
    in_maps = [host_inputs(cfg, inputs, b) for b in range(B)]
    res = run_bass_kernel_spmd(nc, in_maps, core_ids=list(range(B)))
    out = np.stack([host_output(cfg, np.asarray(res.results[b]["OUT"], dtype=np.float32)) for b in range(B)], 0)
    return out.astype(np.float32)
```

```python
import contextlib
import numpy as np
import concourse.bass as bass
import concourse.mybir as mybir
from concourse.bass_utils import run_bass_kernel_spmd

F32 = mybir.dt.float32
BF16 = mybir.dt.bfloat16
AF = mybir.ActivationFunctionType
ALU = mybir.AluOpType

N_CORES_USED = 2


class Cfg:
    def __init__(self, d_model=2048, seq=8192, depth=4, ssd_heads=64, ssd_groups=8, gla_heads=4):
        self.D = d_model
        self.L = seq
        self.depth = depth
        self.P = 64
        self.N = 128
        self.H = ssd_heads
        self.G = ssd_groups
        self.DI = self.H * self.P
        self.CONV = self.DI + 2 * self.G * self.N
        self.GH = gla_heads
        self.DK = d_model // 2
        self.DV = d_model
        self.R = 16
        self.FF = 4 * d_model
        self.splits = (self.DI, self.CONV, self.H, self.DK, self.DK, self.DV, self.DV, self.R, self.D, self.D)
        self.DIN = sum(self.splits)
        self.alpha = (2.0 * depth) ** 0.25


class Buf:
    __slots__ = ("name", "w", "r")

    def __init__(self, name=""):
        self.name = name
        self.w = None
        self.r = []


class Sched:
    SEM_ROT = 30000

    def __init__(self, nc, stack, n_dma_sems=40):
        self.nc = nc
        self.stack = stack
        self.eng = {"pe": nc.tensor, "act": nc.scalar, "dve": nc.vector, "pool": nc.gpsimd, "sp": nc.sync}
        self.sem = {}
        self.cnt = {}
        self.nsem = 0
        self.pe_sems = set()
        for e in ("pe", "act", "dve", "pool"):
            self._new_sem(e)
        self.dma_sems = [stack.enter_context(nc.semaphore(f"dsem{i}")) for i in range(n_dma_sems)]
        self.dma_cnt = [0] * n_dma_sems
        self.dma_next = 0
        self.waited = {e: {} for e in self.eng}
        self.semobj = {}
        self.sw = {}
        self.n_ins = {e: 0 for e in self.eng}

    def _new_sem(self, e):
        s = self.stack.enter_context(self.nc.semaphore(f"sem_{e}_{self.nsem}"))
        self.nsem += 1
        self.sem[e] = s
        self.cnt[e] = 0
        if e == "pe":
            self.pe_sems.add(id(s))

    def _wait(self, e, tok):
        if tok is None:
            return
        sem, val = tok
        d = self.waited[e]
        k = id(sem)
        if e == "pe" and k in self.pe_sems:
            return
        if d.get(k, 0) >= val:
            return
        d[k] = val
        self.eng[e].wait_ge(sem, val)

    def _deps(self, e, reads, writes, skip_same_pe=False):
        for b in reads:
            self._wait(e, b.w)
        for b in writes:
            self._wait(e, b.w)
            for t in b.r:
                self._wait(e, t)

    def _commit(self, tok, reads, writes):
        for b in reads:
            b.r.append(tok)
            if len(b.r) > 12:
                latest = {}
                for (s, v) in b.r:
                    if id(s) not in latest or latest[id(s)][1] < v:
                        latest[id(s)] = (s, v)
                b.r = list(latest.values())
        for b in writes:
            b.w = tok
            b.r = []

    def op(self, e, fn, reads=(), writes=(), inc=True):
        if self.cnt[e] >= self.SEM_ROT:
            self._new_sem(e)
        self._deps(e, reads, writes)
        ins = fn()
        self.n_ins[e] += 1
        if inc:
            self.cnt[e] += 1
            ins.then_inc(self.sem[e], 1)
            tok = (self.sem[e], self.cnt[e])
        else:
            tok = (self.sem[e], self.cnt[e] + 1)
        self._commit(tok, reads, writes)
        return ins

    def dma(self, q, out, in_, reads=(), writes=(), **kw):
        i = self.dma_next
        self.dma_next = (self.dma_next + 1) % len(self.dma_sems)
        sem = self.dma_sems[i]
        if self.dma_cnt[i] > 0:
            self._wait(q, (sem, 16 * self.dma_cnt[i]))
        self._deps(q, reads, writes)
        ins = self.eng[q].dma_start(out=out, in_=in_, **kw)
        self.n_ins[q] += 1
        self.dma_cnt[i] += 1
        ins.then_inc(sem, 16)
        tok = (sem, 16 * self.dma_cnt[i])
        self._commit(tok, reads, writes)
        return tok

    def dma_sw(self, key, out, in_, reads=(), writes=(), **kw):
        sem = self.sw.get(key)
        first = sem is None
        if first:
            sem = self.stack.enter_context(self.nc.semaphore(f"swsem{len(self.sw)}"))
            self.sw[key] = sem
        self._deps("pool", reads, writes)
        if not first:
            self.eng["pool"].wait_ge(sem, 16)
            self.eng["pool"].sem_clear(sem)
            for e in self.waited:
                self.waited[e].pop(id(sem), None)
        ins = self.eng["pool"].dma_start(out=out, in_=in_, **kw)
        self.n_ins["pool"] += 1
        ins.then_inc(sem, 16)
        tok = (sem, 16)
        self._commit(tok, reads, writes)
        return tok

    def wait_all(self, e, bufs):
        for b in bufs:
            self._wait(e, b.w)
            for t in b.r:
                self._wait(e, t)


class Ctx:
    def __init__(self, nc, st, cfg):
        self.nc, self.st, self.cfg = nc, st, cfg
        self.S = Sched(nc, st)
        self._n = 0
        self.ps = [st.enter_context(nc.psum_tensor(f"ps{i}", [128, 512], F32)) for i in range(8)]
        self.bps = [Buf(f"ps{i}") for i in range(8)]
        self.NW = 3
        self.wbase = {}
        self.wset = 0
        self.WBF = None
        self.wb = None
        self.bwb = None
        self.wi = 0
        self.psi = 0
        self.ev = 0

    def sb(self, shape, dt):
        self._n += 1
        return self.st.enter_context(self.nc.sbuf_tensor(f"t{self._n}", list(shape), dt))

    def dram(self, shape, dt):
        self._n += 1
        return self.nc.dram_tensor(f"scr{self._n}", list(shape), dt, kind="Internal").ap()


def wblock_list(K, n0, n1):
    KT = K // 128
    out = []
    for c0 in range(n0, n1, 512):
        cw = min(512, n1 - c0)
        for k0 in range(0, KT, 16):
            out.append((k0, min(16, KT - k0), c0, cw))
    return out


def wstream_begin(C, entries):
    C.wq = list(entries)
    C.wq_issued = 0
    C.wq_pos = 0


def wstream_entries(key, K, n0, n1):
    return [(key, j, nk, cw) for j, (k0, nk, c0, cw) in enumerate(wblock_list(K, n0, n1))]


def wnext(C, key, j):
    pos = C.wq_pos
    assert C.wq[pos][0] == key and C.wq[pos][1] == j, (C.wq[pos], key, j)
    while C.wq_issued < min(len(C.wq), pos + C.NW):
        k_, j_, nk, cw = C.wq[C.wq_issued]
        w = C.wq_issued % C.NW
        blk = C.wbase[k_] + j_
        src = C.WBF[C.wset][blk][:, 0:nk * cw].rearrange("p (a b) -> p a b", b=cw)
        C.S.dma("sp", C.wb[w][:, 0:nk, 0:cw], src, writes=[C.bwb[w]])
        C.wq_issued += 1
    C.wq_pos += 1
    return pos % C.NW


def wload(C, key, j, w, nk, cw):
    blk = C.wbase[key] + j
    src = C.WBF[blk][:, 0:nk * cw].rearrange("p (a b) -> p a b", b=cw)
    C.S.dma("act", C.wb[w][:, 0:nk, 0:cw], src, writes=[C.bwb[w]])


def gemm_fm(C, W, K, n0, n1, rhs_tiles, brhs, T, epilogue, ps_banks=(0, 1, 2, 3, 4, 5, 6, 7)):
    nc, S = C.nc, C.S
    KT = K // 128
    assert K % 128 == 0 and T <= 512
    jblk = 0
    for c0 in range(n0, n1, 512):
        cw = min(512, n1 - c0)
        kblocks = [(k0, min(16, KT - k0)) for k0 in range(0, KT, 16)]
        ntiles = [(c0 + j, min(128, c0 + cw - (c0 + j))) for j in range(0, cw, 128)]
        banks = [ps_banks[(C.psi + i) % len(ps_banks)] for i in range(len(ntiles))]
        C.psi += len(ntiles)
        for bi, (k0, nk) in enumerate(kblocks):
            w = wnext(C, W, jblk)
            jblk += 1
            for ti, (col, ncol) in enumerate(ntiles):
                b = banks[ti]
                for kk in range(nk):
                    kt = k0 + kk
                    first = (kt == 0)
                    last = (kt == KT - 1)
                    lo = col - c0
                    S.op("pe", lambda: nc.tensor.matmul(C.ps[b][0:ncol, 0:T], C.wb[w][:, kk, lo:lo + ncol],
                                                        rhs_tiles(kt), start=first, stop=last),
                         reads=[C.bwb[w], brhs], writes=[C.bps[b]], inc=last)
        for ti, (col, ncol) in enumerate(ntiles):
            b = banks[ti]
            epilogue(col, ncol, C.ps[b][0:ncol, 0:T], C.bps[b])


def layernorm_fm(C, v32, bv, KT, T, gam, bet, out32, bo32, outbf, bobf, ones_bf, bones, eps=1e-5):
    nc, S = C.nc, C.S
    D = KT * 128
    vb = C.ln_vb; sq = C.ln_sq; st = C.ln_st
    S.op("dve", lambda: nc.vector.tensor_copy(vb[:, 0:KT, 0:T], v32[:, 0:KT, 0:T]), reads=[bv], writes=[C.b_ln_vb])
    S.op("pool", lambda: nc.gpsimd.tensor_tensor(sq[:, 0:KT, 0:T], v32[:, 0:KT, 0:T], v32[:, 0:KT, 0:T], ALU.mult),
         reads=[bv], writes=[C.b_ln_sq])
    b_s, b_q = 4, 5
    for kt in range(KT):
        S.op("pe", lambda: nc.tensor.matmul(C.ps[b_s][:, 0:T], ones_bf[:, 0:128], vb[:, kt, 0:T], start=(kt == 0), stop=(kt == KT - 1)),
             reads=[bones, C.b_ln_vb], writes=[C.bps[b_s]], inc=(kt == KT - 1))
    for kt in range(KT):
        S.op("pe", lambda: nc.tensor.matmul(C.ps[b_q][:, 0:T], ones_bf[:, 0:128], sq[:, kt, 0:T], start=(kt == 0), stop=(kt == KT - 1)),
             reads=[bones, C.b_ln_sq], writes=[C.bps[b_q]], inc=(kt == KT - 1))
    S.op("act", lambda: nc.scalar.mul(st[:, 0, 0:T], C.ps[b_s][:, 0:T], 1.0 / D), reads=[C.bps[b_s]], writes=[C.b_ln_st])
    S.op("dve", lambda: nc.vector.tensor_tensor(st[:, 2, 0:T], st[:, 0, 0:T], st[:, 0, 0:T], ALU.mult), reads=[C.b_ln_st], writes=[C.b_ln_st])
    S.op("dve", lambda: nc.vector.scalar_tensor_tensor(st[:, 1, 0:T], C.ps[b_q][:, 0:T], 1.0 / D, st[:, 2, 0:T], ALU.mult, ALU.subtract),
         reads=[C.bps[b_q], C.b_ln_st], writes=[C.b_ln_st])
    S.op("act", lambda: nc.scalar.activation(st[:, 1, 0:T], st[:, 1, 0:T], AF.Sqrt, bias=eps), reads=[C.b_ln_st], writes=[C.b_ln_st])
    S.op("dve", lambda: nc.vector.reciprocal(st[:, 1, 0:T], st[:, 1, 0:T]), reads=[C.b_ln_st], writes=[C.b_ln_st])
    for kt in range(KT):
        e = "dve" if kt % 2 == 0 else "pool"
        eng = nc.vector if e == "dve" else nc.gpsimd
        S.op(e, lambda: eng.tensor_tensor(out32[:, kt, 0:T], v32[:, kt, 0:T], st[:, 0, 0:T], ALU.subtract), reads=[bv, C.b_ln_st], writes=[bo32])
        S.op(e, lambda: eng.tensor_tensor(out32[:, kt, 0:T], out32[:, kt, 0:T], st[:, 1, 0:T], ALU.mult), reads=[bo32, C.b_ln_st], writes=[bo32])
        S.op("act", lambda: nc.scalar.activation(out32[:, kt, 0:T], out32[:, kt, 0:T], AF.Identity, bias=bet[:, kt:kt + 1], scale=gam[:, kt:kt + 1]),
             reads=[bo32], writes=[bo32])
        S.op("act", lambda: nc.scalar.copy(outbf[:, kt, 0:T], out32[:, kt, 0:T]), reads=[bo32], writes=[bobf])


def ln_alloc(C, KT, T):
    C.ln_vb = C.sb([128, KT, T], BF16); C.b_ln_vb = Buf()
    C.ln_sq = C.sb([128, KT, T], BF16); C.b_ln_sq = Buf()
    C.ln_st = C.sb([128, 3, T], F32); C.b_ln_st = Buf()


def _bank(C):
    b = C.ev % 8
    C.ev += 1
    return b


def _bc_last(ap, shape):
    return ap.unsqueeze(2).to_broadcast(list(shape))


def _bc_mid(ap, shape):
    return ap.unsqueeze(1).to_broadcast(list(shape))


def transpose_tiles(C, src_tile, ntiles, bsrc, dst2d, bdst, ident_bf, bident, evac=None):
    nc, S = C.nc, C.S
    for t0 in range(0, ntiles, 8):
        n = min(8, ntiles - t0)
        b = _bank(C)
        psb = C.ps[b][:].bitcast(BF16)
        for t in range(n):
            S.op("pe", lambda: nc.tensor.transpose(psb[:, t * 128:(t + 1) * 128], src_tile(t0 + t), ident_bf[:]),
                 reads=[bsrc, bident], writes=[C.bps[b]], inc=(t == n - 1))
        if evac is not None:
            evac(t0, n, psb, C.bps[b])
        else:
            e = "act" if (C.ev % 2) else "dve"
            if e == "act":
                S.op("act", lambda: nc.scalar.copy(dst2d[:, t0 * 128:(t0 + n) * 128], psb[:, 0:n * 128]), reads=[C.bps[b]], writes=[bdst])
            else:
                S.op("dve", lambda: nc.vector.tensor_copy(dst2d[:, t0 * 128:(t0 + n) * 128], psb[:, 0:n * 128]), reads=[C.bps[b]], writes=[bdst])


def ssd_pass(C, K, lp, ZT, XBCT, DTT, YO, NCH, bg=None):
    nc, S, cfg = C.nc, C.S, C.cfg
    H, G, P, N = cfg.H, cfg.G, cfg.P, cfg.N
    Hg = H // G
    assert Hg in (4, 8) and N == 128 and P == 64
    DI = cfg.DI
    NXT = DI // 128
    NCT = NXT + 2 * G
    GW = Hg * P
    NSL = H // 8
    I_bf, bI = K["ident_bf"], K["b"]
    U, ones32 = K["U"], K["ones32"]
    bK = K["b"]
    dtb, IA, Drep, normw, bl = lp["dtb"], lp["IA"], lp["Drep"], lp["normw"], lp["b"]
    sb = C.sb
    zt = [sb([128, NXT, 128], BF16) for _ in range(2)]; bzt = [Buf(), Buf()]
    dtr = [sb([H, 128], F32) for _ in range(2)]; bdtr = [Buf(), Buf()]
    xcbs = [sb([128, NCT, 128], BF16) for _ in range(2)]; bxcbs = [Buf(), Buf()]
    xs_tok = sb([128, DI], BF16); bxs = Buf()
    B_tok = sb([128, G * 128], BF16); bB = Buf()
    z_tok = sb([128, DI], BF16); bz = Buf()
    e1 = sb([H, 128], F32); be1 = Buf()
    dtT = sb([H, 128], F32); bdtT = Buf()
    dtk = sb([128, 2 * H], F32); bdtk = Buf()
    acs = sb([128, 2 * H], F32); bacs = Buf()
    dte = sb([128, H], F32); bdte = Buf()
    ea = sb([128, 2 * H], F32); bea = Buf()
    xdt = sb([128, DI], BF16); bxdt = Buf()
    xw = sb([128, DI], BF16); bxw = Buf()
    cbm = sb([128, G, 128], F32); bcbm = Buf()
    NR = 4
    R1 = [sb([128, 4, 128], F32) for _ in range(NR)]; bR1 = [Buf() for _ in range(NR)]
    tt = [sb([128, 4, 128], F32) for _ in range(NR)]; btt = [Buf() for _ in range(NR)]
    Mt = [sb([128, 4, 128], BF16) for _ in range(NR)]; bMt = [Buf() for _ in range(NR)]
    y = sb([128, DI], F32); by = Buf()
    y2f = sb([128, DI], F32); by2f = Buf()
    S32 = sb([128, DI], F32); bS32 = Buf()
    Sbf = sb([128, DI], BF16); bSbf = Buf()
    szs = [sb([128, 512], F32) for _ in range(2)]; bszs = [Buf(), Buf()]
    junk = sb([128, GW], F32); bjunk = Buf()
    ssq = sb([128, G], F32); bssq = Buf()
    yn, byn = xw, bxw
    yT1 = sb([128, NXT, 128], BF16); byT1 = Buf()
    yT = [yT1, yT1]; byT = [byT1, byT1]

    S.op("dve", lambda: nc.vector.memset(S32[:], 0.0), writes=[bS32])
    S.op("dve", lambda: nc.vector.memset(Sbf[:], 0.0), writes=[bSbf])
    def load(c):
        s = c % 2
        S.dma("sp", xcbs[s][:], XBCT[c], writes=[bxcbs[s]])
        S.dma("sp", zt[s][:], ZT[c], writes=[bzt[s]])
        S.dma("sp", dtr[s][:], DTT[c], writes=[bdtr[s]])

    load(0)
    for c in range(NCH):
        s = c % 2
        if c + 1 < NCH:
            load(c + 1)
        xcb, bxcb = xcbs[s], bxcbs[s]
        transpose_tiles(C, lambda t: xcb[:, t, :], NXT, bxcb, xs_tok, bxs, I_bf, bI)
        S.op("pool", lambda: nc.gpsimd.tensor_tensor(y2f[:].rearrange("p (h q) -> p h q", q=P), xs_tok[:].rearrange("p (h q) -> p h q", q=P),
                                                     _bc_last(Drep[:, 0:H], [128, H, P]), ALU.mult), reads=[bxs, bl], writes=[by2f])
        transpose_tiles(C, lambda t: xcb[:, NXT + t, :], G, bxcb, B_tok, bB, I_bf, bI)
        transpose_tiles(C, lambda t: zt[s][:, t, :], NXT, bzt[s], z_tok, bz, I_bf, bI)
        S.op("act", lambda: nc.scalar.activation(e1[:], dtr[s][:], AF.Exp, bias=dtb[0:H, 0:1]), reads=[bdtr[s], bl], writes=[be1])
        S.op("act", lambda: nc.scalar.activation(dtT[:], e1[:], AF.Ln, bias=1.0), reads=[be1], writes=[bdtT])
        b = _bank(C)
        S.op("pe", lambda: nc.tensor.matmul(C.ps[b][:, 0:2 * H], dtT[:], IA[0:H, 0:2 * H], start=True, stop=True),
             reads=[bdtT, bl], writes=[C.bps[b]])
        S.op("dve", lambda: nc.vector.tensor_copy(dtk[:], C.ps[b][:, 0:2 * H]), reads=[C.bps[b]], writes=[bdtk])
        dt_tok, dtA = dtk[:, 0:H], dtk[:, H:2 * H]
        b = _bank(C)
        S.op("pe", lambda: nc.tensor.matmul(C.ps[b][:, 0:H], U[:], dtA, start=True, stop=True), reads=[bK, bdtk], writes=[C.bps[b]], inc=False)
        S.op("pe", lambda: nc.tensor.matmul(C.ps[b][:, H:2 * H], ones32[:], dtA, start=True, stop=True), reads=[bK, bdtk], writes=[C.bps[b]])
        S.op("dve", lambda: nc.vector.tensor_copy(acs[:], C.ps[b][:, 0:2 * H]), reads=[C.bps[b]], writes=[bacs])
        a_cs, a_tot = acs[:, 0:H], acs[:, H:2 * H]
        S.op("dve", lambda: nc.vector.tensor_tensor(dte[:], a_tot, a_cs, ALU.subtract), reads=[bacs], writes=[bdte])
        S.op("act", lambda: nc.scalar.activation(dte[:], dte[:], AF.Exp), reads=[bdte], writes=[bdte])
        S.op("dve", lambda: nc.vector.tensor_tensor(dte[:], dte[:], dt_tok, ALU.mult), reads=[bdte, bdtk], writes=[bdte])
        S.op("act", lambda: nc.scalar.activation(ea[:], acs[:], AF.Exp), reads=[bacs], writes=[bea])
        shp3 = [128, H, P]
        S.op("dve", lambda: nc.vector.tensor_tensor(xdt[:].rearrange("p (h q) -> p h q", q=P), xs_tok[:].rearrange("p (h q) -> p h q", q=P),
                                                    _bc_last(dt_tok, shp3), ALU.mult), reads=[bxs, bdtk], writes=[bxdt])
        for g0 in range(0, G, 4):
            b = _bank(C)
            for g in range(g0, g0 + 4):
                S.op("pe", lambda: nc.tensor.matmul(C.ps[b][:, (g - g0) * 128:(g - g0 + 1) * 128], xcb[:, NXT + g, :], xcb[:, NXT + G + g, :],
                                                    start=True, stop=True), reads=[bxcb], writes=[C.bps[b]], inc=(g == g0 + 3))
            S.op("dve", lambda: nc.vector.tensor_tensor(cbm[:, g0:g0 + 4, :], C.ps[b][:].rearrange("p (a i) -> p a i", i=128),
                                                        _bc_mid(U[:], [128, 4, 128]), ALU.mult), reads=[C.bps[b], bK], writes=[bcbm])
        for sl in range(NSL):
            if bg is not None:
                next(bg, None)
            h8 = slice(sl * 8, sl * 8 + 8)
            cols = slice(sl * 512, sl * 512 + 512)
            byo = _bank(C)
            byd = _bank(C)
            glist = list(range(sl * 8 // Hg, (sl * 8 + 8) // Hg))
            for gi, g in enumerate(glist):
                S.op("pe", lambda: nc.tensor.matmul(C.ps[byo][:, gi * GW:(gi + 1) * GW], xcb[:, NXT + G + g, :], Sbf[:, g * GW:(g + 1) * GW],
                                                    start=True, stop=True), reads=[bxcb, bSbf], writes=[C.bps[byo]], inc=(gi == len(glist) - 1))
            for q4 in range(2):
                h0 = sl * 8 + q4 * 4
                r = (sl * 2 + q4) % NR
                g = h0 // Hg
                sh4 = [128, 4, 128]
                S.op("pool", lambda: nc.gpsimd.tensor_tensor(R1[r][:], _bc_mid(U[:], sh4), _bc_last(dtA[:, h0:h0 + 4], sh4), ALU.mult),
                     reads=[bK, bdtk], writes=[bR1[r]])
                b = _bank(C)
                S.op("pe", lambda: nc.tensor.matmul(C.ps[b][:, 0:512], ones32[:], R1[r][:].rearrange("p a i -> p (a i)"), start=True, stop=True),
                     reads=[bK, bR1[r]], writes=[C.bps[b]])
                acs_b = _bc_last(a_cs[:, h0:h0 + 4], sh4)
                S.op("dve", lambda: nc.vector.tensor_tensor(tt[r][:], C.ps[b][:].rearrange("p (a i) -> p a i", i=128), acs_b, ALU.min),
                     reads=[C.bps[b], bacs], writes=[btt[r]])
                S.op("dve", lambda: nc.vector.tensor_tensor(tt[r][:], tt[r][:], acs_b, ALU.subtract), reads=[btt[r], bacs], writes=[btt[r]])
                S.op("act", lambda: nc.scalar.activation(tt[r][:], tt[r][:], AF.Exp), reads=[btt[r]], writes=[btt[r]])
                S.op("dve", lambda: nc.vector.tensor_tensor(Mt[r][:], tt[r][:], _bc_mid(cbm[:, g, :], sh4), ALU.mult),
                     reads=[btt[r], bcbm], writes=[bMt[r]])
                for k in range(4):
                    h = h0 + k
                    co = (h - sl * 8) * P
                    S.op("pe", lambda: nc.tensor.matmul(C.ps[byd][:, co:co + P], Mt[r][:, k, :], xdt[:, h * P:(h + 1) * P], start=True, stop=True),
                         reads=[bMt[r], bxdt], writes=[C.bps[byd]], inc=(k == 3))
            sh8 = [128, 8, P]
            v3 = lambda ap: ap.rearrange("p (h q) -> p h q", q=P)
            S.op("dve", lambda: nc.vector.tensor_tensor(v3(y[:, cols]), v3(C.ps[byo][:, 0:512]), _bc_last(ea[:, h8], sh8), ALU.mult),
                 reads=[C.bps[byo], bea], writes=[by])
            S.op("dve", lambda: nc.vector.tensor_tensor(y[:, cols], y[:, cols], C.ps[byd][:, 0:512], ALU.add), reads=[by, C.bps[byd]], writes=[by])
            S.op("dve", lambda: nc.vector.tensor_tensor(y[:, cols], y[:, cols], y2f[:, cols], ALU.add), reads=[by, by2f], writes=[by])
        S.op("pool", lambda: nc.gpsimd.tensor_tensor(xw[:].rearrange("p (h q) -> p h q", q=P), xs_tok[:].rearrange("p (h q) -> p h q", q=P),
                                                     _bc_last(dte[:], shp3), ALU.mult), reads=[bxs, bdte], writes=[bxw])
        for g in range(G):
            gc = slice(g * GW, (g + 1) * GW)
            b = _bank(C)
            S.op("pe", lambda: nc.tensor.matmul(C.ps[b][:, 0:GW], B_tok[:, g * 128:(g + 1) * 128], xw[:, gc], start=True, stop=True),
                 reads=[bB, bxw], writes=[C.bps[b]])
            vg = lambda ap: ap.rearrange("p (h q) -> p h q", q=P)
            S.op("dve", lambda: nc.vector.tensor_tensor(vg(S32[:, gc]), vg(S32[:, gc]), _bc_last(ea[:, H + g * Hg:H + (g + 1) * Hg], [128, Hg, P]), ALU.mult),
                 reads=[bS32, bea], writes=[bS32])
            S.op("dve", lambda: nc.vector.tensor_tensor(S32[:, gc], S32[:, gc], C.ps[b][:, 0:GW], ALU.add), reads=[bS32, C.bps[b]], writes=[bS32])
            S.op("act", lambda: nc.scalar.copy(Sbf[:, gc], S32[:, gc]), reads=[bS32], writes=[bSbf])
        for sl in range(NSL):
            cols = slice(sl * 512, sl * 512 + 512)
            S.op("act", lambda: nc.scalar.activation(szs[sl % 2][:], z_tok[:, cols], AF.Silu), reads=[bz], writes=[bszs[sl % 2]])
            S.op("dve", lambda: nc.vector.tensor_tensor(y[:, cols], y[:, cols], szs[sl % 2][:], ALU.mult), reads=[bszs[sl % 2], by], writes=[by])
        for g in range(G):
            S.op("act", lambda: nc.scalar.activation(junk[:], y[:, g * GW:(g + 1) * GW], AF.Square, accum_out=ssq[:, g:g + 1]),
                 reads=[by], writes=[bjunk, bssq])
        S.op("act", lambda: nc.scalar.activation(ssq[:], ssq[:], AF.Sqrt, bias=1e-5, scale=1.0 / GW), reads=[bssq], writes=[bssq])
        S.op("dve", lambda: nc.vector.reciprocal(ssq[:], ssq[:]), reads=[bssq], writes=[bssq])
        S.op("dve", lambda: nc.vector.tensor_tensor(yn[:].rearrange("p (g q) -> p g q", q=GW), y[:].rearrange("p (g q) -> p g q", q=GW),
                                                    _bc_last(ssq[:], [128, G, GW]), ALU.mult), reads=[by, bssq], writes=[byn])

        def evac(t0, n, psb, bps):
            S.op("dve", lambda: nc.vector.tensor_tensor(yT[s][:, t0:t0 + n, :], psb[:, 0:n * 128].rearrange("p (a i) -> p a i", i=128),
                                                        _bc_last(normw[:, t0:t0 + n], [128, n, 128]), ALU.mult),
                 reads=[bps, bl], writes=[byT[s]])
        transpose_tiles(C, lambda t: yn[:, t * 128:(t + 1) * 128], NXT, byn, None, None, I_bf, bI, evac=evac)
        S.dma("sp", YO[c][:, 0:NXT, :], yT[s][:], reads=[byT[s]])


def gla_pass(C, K, lp, QT, KTd, VT, GT, GKL, YO, NCH, tile_off):
    nc, S, cfg = C.nc, C.S, C.cfg
    GH, DK, DV, R = cfg.GH, cfg.DK, cfg.DV, cfg.R
    HK, HV = DK // GH, DV // GH
    NKT, NVT = DK // 128, DV // 128
    KPH = HK // 128
    assert HK % 128 == 0 and HV <= 512 and R == 16
    scale = float(HK) ** -0.5
    I_bf, bK = K["ident_bf"], K["b"]
    U, ones32, SU = K["U"], K["ones32"], K["SU"]
    gkw, gnw, bl = lp["gkw"], lp["gnw"], lp["b"]
    sb = C.sb
    qT = [sb([128, NKT, 128], BF16) for _ in range(2)]; bq = [Buf(), Buf()]
    kT = [sb([128, NKT, 128], BF16) for _ in range(2)]; bk = [Buf(), Buf()]
    vT = [sb([128, NVT, 128], BF16) for _ in range(2)]; bv = [Buf(), Buf()]
    gT = [sb([128, NVT, 128], BF16) for _ in range(2)]; bg = [Buf(), Buf()]
    gl = [sb([32, 128], F32) for _ in range(2)]; bgl = [Buf(), Buf()]
    sp = sb([128, DK], F32); bsp = Buf()
    eg = sb([128, NKT, 128], F32); beg = Buf()
    ek = sb([128, NKT, 128], F32); bek = Buf()
    qg = sb([128, NKT, 128], BF16); bqg = Buf()
    kg = sb([128, NKT, 128], BF16); bkg = Buf()
    k_tok = sb([128, DK], BF16); bkt = Buf()
    v_tok = sb([128, DV], BF16); bvt = Buf()
    g_tok = sb([128, DV], BF16); bgt = Buf()
    erev = sb([128, DK], F32); berev = Buf()
    kd = sb([128, DK], BF16); bkd = Buf()
    attm = sb([128, GH, 128], BF16); battm = Buf()
    Sg32 = sb([128, NKT, HV], F32); bSg = Buf()
    Sgb = sb([128, NKT, HV], BF16); bSgb = Buf()
    o32 = sb([128, DV], F32); bo = Buf()
    sg = sb([128, DV], F32); bsg = Buf()
    junk = sb([128, HV], F32); bjunk = Buf()
    ssq = sb([128, GH], F32); bssq = Buf()
    on = sb([128, DV], BF16); bon = Buf()
    oT = [sb([128, NVT, 128], BF16) for _ in range(2)]; boT = [Buf(), Buf()]

    S.op("dve", lambda: nc.vector.memset(Sg32[:], 0.0), writes=[bSg])
    S.op("dve", lambda: nc.vector.memset(Sgb[:], 0.0), writes=[bSgb])
    for s in range(2):
        S.op("pool", lambda: nc.gpsimd.memset(gl[s][:], 1.0), writes=[bgl[s]])

    def load(c):
        s = c % 2
        S.dma("sp", qT[s][:], QT[c], writes=[bq[s]])
        S.dma("sp", kT[s][:], KTd[c], writes=[bk[s]])
        S.dma("sp", vT[s][:], VT[c], writes=[bv[s]])
        S.dma("sp", gT[s][:], GT[c], writes=[bg[s]])
        S.dma("sp", gl[s][0:16, :], GKL[c], writes=[bgl[s]])

    load(0)
    for c in range(NCH):
        s = c % 2
        if c + 1 < NCH:
            load(c + 1)
        for n0 in range(0, DK, 512):
            nw = min(512, DK - n0)
            b = _bank(C)
            S.op("pe", lambda: nc.tensor.matmul(C.ps[b][:, 0:nw], gl[s][0:17, :], gkw[0:17, n0:n0 + nw], start=True, stop=True),
                 reads=[bgl[s], bl], writes=[C.bps[b]])
            S.op("act", lambda: nc.scalar.activation(sp[:, n0:n0 + nw], C.ps[b][:, 0:nw], AF.Exp, scale=-1.0), reads=[C.bps[b]], writes=[bsp])
        S.op("act", lambda: nc.scalar.activation(sp[:], sp[:], AF.Ln, bias=1.0), reads=[bsp], writes=[bsp])
        if getattr(cfg, "gla_stop", 99) == 1:
            return
        for t0 in range(0, NKT, 4):
            n = min(4, NKT - t0)
            b = _bank(C)
            for t in range(n):
                S.op("pe", lambda: nc.tensor.matmul(C.ps[b][:, t * 128:(t + 1) * 128], sp[:, (t0 + t) * 128:(t0 + t + 1) * 128], U[:], start=True, stop=True),
                     reads=[bsp, bK], writes=[C.bps[b]], inc=(t == n - 1))
            pv = C.ps[b][:, 0:n * 128].rearrange("p (a i) -> p a i", i=128)
            S.op("act", lambda: nc.scalar.activation(eg[:, t0:t0 + n, :], pv, AF.Exp, scale=-1.0 / 16), reads=[C.bps[b]], writes=[beg])
            S.op("act", lambda: nc.scalar.activation(ek[:, t0:t0 + n, :], pv, AF.Exp, scale=1.0 / 16), reads=[C.bps[b]], writes=[bek])
        S.op("dve", lambda: nc.vector.scalar_tensor_tensor(qg[:], qT[s][:], scale, eg[:], ALU.mult, ALU.mult), reads=[bq[s], beg], writes=[bqg])
        S.op("pool", lambda: nc.gpsimd.tensor_tensor(kg[:], kT[s][:], ek[:], ALU.mult), reads=[bk[s], bek], writes=[bkg])
        if getattr(cfg, "gla_stop", 99) == 2:
            return
        transpose_tiles(C, lambda t: kT[s][:, t, :], NKT, bk[s], k_tok, bkt, I_bf, bK)
        transpose_tiles(C, lambda t: vT[s][:, t, :], NVT, bv[s], v_tok, bvt, I_bf, bK)
        transpose_tiles(C, lambda t: gT[s][:, t, :], NVT, bg[s], g_tok, bgt, I_bf, bK)
        if getattr(cfg, "gla_stop", 99) == 3:
            return
        for n0 in range(0, DK, 512):
            nw = min(512, DK - n0)
            b = _bank(C)
            S.op("pe", lambda: nc.tensor.matmul(C.ps[b][:, 0:nw], SU[:], sp[:, n0:n0 + nw], start=True, stop=True), reads=[bK, bsp], writes=[C.bps[b]])
            S.op("act", lambda: nc.scalar.activation(erev[:, n0:n0 + nw], C.ps[b][:, 0:nw], AF.Exp, scale=-1.0 / 16), reads=[C.bps[b]], writes=[berev])
        S.op("dve", lambda: nc.vector.tensor_tensor(kd[:], k_tok[:], erev[:], ALU.mult), reads=[bkt, berev], writes=[bkd])
        if getattr(cfg, "gla_stop", 99) == 4:
            return
        for h0 in range(0, GH, 4):
            nh = min(4, GH - h0)
            b = _bank(C)
            for hh in range(nh):
                h = h0 + hh
                for kk in range(KPH):
                    t = h * KPH + kk
                    S.op("pe", lambda: nc.tensor.matmul(C.ps[b][:, hh * 128:(hh + 1) * 128], kg[:, t, :], qg[:, t, :], start=(kk == 0), stop=(kk == KPH - 1)),
                         reads=[bkg, bqg], writes=[C.bps[b]], inc=(hh == nh - 1 and kk == KPH - 1))
            S.op("dve", lambda: nc.vector.tensor_tensor(attm[:, h0:h0 + nh, :], C.ps[b][:, 0:nh * 128].rearrange("p (a i) -> p a i", i=128),
                                                        _bc_mid(U[:], [128, nh, 128]), ALU.mult), reads=[C.bps[b], bK], writes=[battm])
        if getattr(cfg, "gla_stop", 99) == 5:
            return
        for h in range(GH):
            b = _bank(C)
            vc = slice(h * HV, (h + 1) * HV)
            S.op("pe", lambda: nc.tensor.matmul(C.ps[b][:, 0:HV], attm[:, h, :], v_tok[:, vc], start=True, stop=False),
                 reads=[battm, bvt], writes=[C.bps[b]], inc=False)
            for kk in range(KPH):
                t = h * KPH + kk
                S.op("pe", lambda: nc.tensor.matmul(C.ps[b][:, 0:HV], qg[:, t, :], Sgb[:, t, :], start=False, stop=(kk == KPH - 1)),
                     reads=[bqg, bSgb], writes=[C.bps[b]], inc=(kk == KPH - 1))
            S.op("dve", lambda: nc.vector.tensor_copy(o32[:, vc], C.ps[b][:, 0:HV]), reads=[C.bps[b]], writes=[bo])
            S.op("act", lambda: nc.scalar.activation(junk[:], o32[:, vc], AF.Square, accum_out=ssq[:, h:h + 1]), reads=[bo], writes=[bjunk, bssq])
        if getattr(cfg, "gla_stop", 99) == 6:
            return
        for t in range(NKT):
            h = t // KPH
            b = _bank(C)
            S.op("pe", lambda: nc.tensor.matmul(C.ps[b][:, 0:HV], kd[:, t * 128:(t + 1) * 128], v_tok[:, h * HV:(h + 1) * HV], start=True, stop=True),
                 reads=[bkd, bvt], writes=[C.bps[b]])
            S.op("dve", lambda: nc.vector.scalar_tensor_tensor(Sg32[:, t, :], Sg32[:, t, :], eg[:, t, 127:128], C.ps[b][:, 0:HV], ALU.mult, ALU.add),
                 reads=[bSg, beg, C.bps[b]], writes=[bSg])
            S.op("act", lambda: nc.scalar.copy(Sgb[:, t, :], Sg32[:, t, :]), reads=[bSg], writes=[bSgb])
        if getattr(cfg, "gla_stop", 99) == 7:
            return
        S.op("act", lambda: nc.scalar.activation(ssq[:], ssq[:], AF.Sqrt, bias=1e-5, scale=1.0 / HV), reads=[bssq], writes=[bssq])
        S.op("dve", lambda: nc.vector.reciprocal(ssq[:], ssq[:]), reads=[bssq], writes=[bssq])
        S.op("act", lambda: nc.scalar.activation(sg[:], g_tok[:], AF.Silu), reads=[bgt], writes=[bsg])
        S.op("pool", lambda: nc.gpsimd.tensor_tensor(sg[:], sg[:], o32[:], ALU.mult), reads=[bsg, bo], writes=[bsg])
        S.op("dve", lambda: nc.vector.tensor_tensor(on[:].rearrange("p (h q) -> p h q", q=HV), sg[:].rearrange("p (h q) -> p h q", q=HV),
                                                    _bc_last(ssq[:], [128, GH, HV]), ALU.mult), reads=[bsg, bssq], writes=[bon])

        def evac(t0, n, psb, bps):
            S.op("dve", lambda: nc.vector.tensor_tensor(oT[s][:, t0:t0 + n, :], psb[:, 0:n * 128].rearrange("p (a i) -> p a i", i=128),
                                                        _bc_last(gnw[:, t0:t0 + n], [128, n, 128]), ALU.mult),
                 reads=[bps, bl], writes=[boT[s]])
        transpose_tiles(C, lambda t: on[:, t * 128:(t + 1) * 128], NVT, bon, None, None, I_bf, bK, evac=evac)
        S.dma("sp", YO[c][:, tile_off:tile_off + NVT, :], oT[s][:], reads=[boT[s]])


def barrier(C):
    S = C.S
    toks = [(S.sem[e], S.cnt[e]) for e in ("pe", "act", "dve", "pool") if S.cnt[e] > 0]
    toks += [(sem, 16 * S.dma_cnt[i]) for i, sem in enumerate(S.dma_sems) if S.dma_cnt[i] > 0]
    toks += [(sem, 16) for sem in S.sw.values()]
    for e in ("pe", "act", "dve", "pool", "sp"):
        for t in toks:
            S._wait(e, t)


class Phase:
    def __init__(self, C):
        self.C = C

    def __enter__(self):
        self.stack = contextlib.ExitStack()
        self.stack.__enter__()
        self.prev = self.C.st
        self.C.st = self.stack
        return self

    def __exit__(self, *a):
        barrier(self.C)
        self.C.st = self.prev
        return self.stack.__exit__(*a)


def gemm_multi(C, W, K, n0, n1, rhs_tiles, brhs, tbs, T, epilogue):
    nc, S = C.nc, C.S
    KT = K // 128
    assert KT <= 16
    bc = 512
    jblk = 0
    for c0 in range(n0, n1, bc):
        cw = min(bc, n1 - c0)
        w = wnext(C, W, jblk)
        wv = C.wb[w]
        jblk += 1
        for lo in range(0, cw, 128):
            ncol = min(128, cw - lo)
            for tb in tbs:
                b = C.psi % 8
                C.psi += 1
                for kt in range(KT):
                    S.op("pe", lambda: nc.tensor.matmul(C.ps[b][0:ncol, 0:T], wv[:, kt, lo:lo + ncol], rhs_tiles(kt, tb),
                                                        start=(kt == 0), stop=(kt == KT - 1)),
                         reads=[C.bwb[w], brhs], writes=[C.bps[b]], inc=(kt == KT - 1))
                epilogue(c0 + lo, ncol, C.ps[b][0:ncol, 0:T], C.bps[b], tb)


def alloc_wbufs(C):
    C.wb = [C.sb([128, 16, 512], BF16) for _ in range(C.NW)]
    C.bwb = [Buf(f"wb{i}") for i in range(C.NW)]


_CACHE = {}


def lp_layout(cfg):
    NXT = cfg.DI // 128
    NCT = NXT + 2 * cfg.G
    NVT = cfg.DV // 128
    NDT = cfg.D // 128
    off, o = {}, 0
    for name, n in [("cw", 4 * NCT), ("cb", NCT), ("dtb", 1), ("alog", 1), ("Drep", cfg.H), ("normw", NXT), ("gnw", NVT),
                    ("gb0", NDT), ("gb1", NDT), ("ln1g", NDT), ("ln1b", NDT), ("ln2g", NDT), ("ln2b", NDT)]:
        off[name] = (o, n)
        o += n
    return off, o


def build_program(cfg):
    nc = bass.Bass("TRN2", target_bir_lowering=False)
    D, L, depth, H, G = cfg.D, cfg.L, cfg.depth, cfg.H, cfg.G
    DI, CONV, DK, DV, FF, DIN = cfg.DI, cfg.CONV, cfg.DK, cfg.DV, cfg.FF, cfg.DIN
    NDT, NXT, NVT, NKT = D // 128, DI // 128, DV // 128, DK // 128
    NCT = NXT + 2 * G
    NB, NCH, T = L // 512, L // 128, 512
    NYO = NXT + NVT
    off, NLP = lp_layout(cfg)
    ext = lambda n, sh, dt=F32: nc.dram_tensor(n, list(sh), dt, kind="ExternalInput").ap()
    X0 = ext("X0", [NB, 128, NDT, T])
    OUT = nc.dram_tensor("OUT", [NB, 128, NDT, T], F32, kind="ExternalOutput").ap()
    Win = ext("w_in", [depth, D, DIN]); Wsb = ext("w_ssd_branch", [depth, DI, D]); Wgb = ext("w_gla_branch", [depth, DV, D])
    Wout = ext("w_out", [depth, D, D]); Wup = ext("w_up", [depth, D, FF]); Wdn = ext("w_down", [depth, FF, D])
    LP = ext("LP", [depth, 128, NLP]); GKW = ext("GKW", [depth, 17, DK]); KC = ext("KC", [128, 4, 128])
    scr = lambda n, sh, dt: nc.dram_tensor(n, list(sh), dt, kind="Internal").ap()
    XR = scr("XR", [NB, 128, NDT, T], F32); XB = scr("XB", [NB, 128, NDT, T], BF16)
    ZT = scr("ZT", [NCH, 128, NXT, 128], BF16); XBCT = scr("XBCT", [NCH, 128, NCT, 128], BF16)
    DTT = scr("DTT", [NCH, H, 128], F32); GKL = scr("GKL", [NCH, 16, 128], F32)
    QT = scr("QT", [NCH, 128, NKT, 128], BF16); KTd = scr("KTd", [NCH, 128, NKT, 128], BF16)
    VT = scr("VT", [NCH, 128, NVT, 128], BF16); GT = scr("GT", [NCH, 128, NVT, 128], BF16)
    GATE = scr("GATE", [NB, 128, 2 * NDT, T], BF16); YO = scr("YO", [NCH, 128, NYO, 128], BF16)
    NFF = 4
    FFh_ = FF // NFF
    names = ["z", "xbc", "dt", "q", "k", "v", "g", "gkl", "gs", "gg"]
    segs, o = {}, 0
    for nme, n in zip(names, cfg.splits):
        segs[nme] = (o, o + n)
        o += n
    with contextlib.ExitStack() as st:
        C = Ctx(nc, st, cfg)
        S = C.S
        kc = C.sb([128, 4, 128], F32); bK = Buf()
        ib = C.sb([128, 128], BF16)
        ones_bf = C.sb([128, 128], BF16)
        lpt = C.sb([128, NLP], F32); bl = Buf()
        gkw = C.sb([32, DK], F32)
        IA = C.sb([128, 2 * H], F32)
        negA = C.sb([128, 1], F32)
        S.dma("sp", kc[:], KC[:], writes=[bK])
        S.op("dve", lambda: nc.vector.tensor_copy(ib[:], kc[:, 0, :]), reads=[bK], writes=[bK])
        S.op("dve", lambda: nc.vector.tensor_copy(ones_bf[:], kc[:, 2, :]), reads=[bK], writes=[bK])
        K = dict(ident_bf=ib, U=kc[:, 1, :], ones32=kc[:, 2, :], SU=kc[:, 3, :], b=bK)
        P = lambda nme: lpt[:, off[nme][0]:off[nme][0] + off[nme][1]]

        with Phase(C):
            xf = [C.sb([128, NDT, T], F32) for _ in range(2)]; bxf = [Buf(), Buf()]
            xb = [C.sb([128, NDT, T], BF16) for _ in range(2)]; bxb = [Buf(), Buf()]
            for tb in range(NB):
                s = tb % 2
                S.dma("sp", xf[s][:], X0[tb], writes=[bxf[s]])
                S.op("dve" if s else "act", (lambda: nc.vector.tensor_copy(xb[s][:], xf[s][:])) if s else (lambda: nc.scalar.copy(xb[s][:], xf[s][:])),
                     reads=[bxf[s]], writes=[bxb[s]])
                S.dma("sp", XB[tb], xb[s][:], reads=[bxb[s]])

        stop_after = getattr(cfg, "stop_after", None)
        for l in range(depth):
            XRsrc = X0 if l == 0 else XR
            XRdst = OUT if l == depth - 1 else XR
            S.dma("sp", lpt[:], LP[l], writes=[bl])
            S.dma("sp", gkw[0:17, :], GKW[l], writes=[bl])
            S.op("act", lambda: nc.scalar.activation(negA[0:H, :], P("alog")[0:H, :], AF.Exp), reads=[bl], writes=[bl])
            S.op("dve", lambda: nc.vector.tensor_copy(IA[0:H, 0:H], kc[0:H, 0, 0:H]), reads=[bK, bl], writes=[bl])
            S.op("dve", lambda: nc.vector.tensor_scalar(IA[0:H, H:2 * H], kc[0:H, 0, 0:H], negA[0:H, 0:1], -1.0, ALU.mult, ALU.mult),
                 reads=[bK, bl], writes=[bl])
            for e in ("pe", "act", "dve", "pool"):
                S.wait_all(e, [bl, bK])
            lp = dict(cw=P("cw").rearrange("p (k t) -> p k t", t=NCT), cb=P("cb"), dtb=P("dtb"), IA=IA, Drep=P("Drep"), normw=P("normw"),
                      gkw=gkw, gnw=P("gnw"), b=bl)

            def layer_specs(ll):
                sp_ = [(("in", nme), Win[ll], D, segs[nme][0], segs[nme][1]) for nme in names]
                sp_ += [("sb", Wsb[ll], DI, 0, D), ("gb", Wgb[ll], DV, 0, D), ("out", Wout[ll], D, 0, D)]
                for hf in range(NFF):
                    sp_ += [(("up", hf), Wup[ll], D, hf * FFh_, (hf + 1) * FFh_), (("dn", hf), Wdn[ll][hf * FFh_:(hf + 1) * FFh_, :], FFh_, 0, D)]
                return sp_

            if C.WBF is None:
                nblk = sum(len(wblock_list(Kk, a, b_)) for (_, _, Kk, a, b_) in layer_specs(0))
                C.WBF = [scr("WBF0", [nblk, 128, 8192], BF16), scr("WBF1", [nblk, 128, 8192], BF16)]

            def precast_gen(ll, SUB):
                wf = [C.sb([128, SUB, 512], F32) for _ in range(2)]; bwf = [Buf(), Buf()]
                wq = [C.sb([128, SUB, 512], BF16) for _ in range(2)]; bwq = [Buf(), Buf()]
                dstset = C.WBF[ll % 2]
                pend = [None]
                step = [0]

                def finish(p):
                    i, n, cw, dst = p
                    S.op("act", lambda: nc.scalar.copy(wq[i][:, 0:n, 0:cw], wf[i][:, 0:n, 0:cw]), reads=[bwf[i]], writes=[bwq[i]])
                    S.dma("sp", dst, wq[i][:, 0:n, 0:cw], reads=[bwq[i]])
                idx = 0
                for (key, Wap, Kk, a, b_) in layer_specs(ll):
                    C.wbase[key] = idx
                    for (k0, nk, c0, cw) in wblock_list(Kk, a, b_):
                        for s0 in range(0, nk, SUB):
                            n = min(SUB, nk - s0)
                            i = step[0] % 2
                            step[0] += 1
                            S.dma("sp", wf[i][:, 0:n, 0:cw],
                                  Wap[(k0 + s0) * 128:(k0 + s0 + n) * 128, c0:c0 + cw].rearrange("(kt p) n -> p kt n", p=128), writes=[bwf[i]])
                            if pend[0] is not None:
                                finish(pend[0])
                            pend[0] = (i, n, cw, dstset[idx][:, s0 * cw:(s0 + n) * cw].rearrange("p (a b) -> p a b", b=cw))
                            yield
                        idx += 1
                if pend[0] is not None:
                    finish(pend[0])

            if l == 0:
                with Phase(C):
                    for _ in precast_gen(0, 16):
                        pass
            C.wset = l % 2

            if stop_after == "precast":
                break
            with Phase(C):
                alloc_wbufs(C)
                TBn = min(4, NB)
                xT = C.sb([128, NDT, TBn * T], BF16); bxT = Buf()
                stg = [C.sb([128, T], BF16) for _ in range(4)]; bstg = [Buf() for _ in range(4)]
                stf = [C.sb([128, T], F32) for _ in range(2)]; bstf = [Buf() for _ in range(2)]
                pre = [C.sb([128, T + 3], F32) for _ in range(2)]; bpre = [Buf(), Buf()]
                cacc = [C.sb([128, T], F32) for _ in range(2)]; bcacc = [Buf(), Buf()]
                ct1 = [C.sb([128, T], F32) for _ in range(2)]; bct1 = [Buf(), Buf()]
                ct2 = [C.sb([128, T], F32) for _ in range(2)]; bct2 = [Buf(), Buf()]
                hal = C.sb([128, NCT, 3], F32); bhal = Buf()
                S.op("pool", lambda: nc.gpsimd.memset(hal[:], 0.0), writes=[bhal])
                cwv, cbv = lp["cw"], lp["cb"]
                cnt = [0]
                ccnt = [0]

                def make_epi(nme, sb0):
                    n0 = segs[nme][0]

                    def epi(col, ncol, ps, bps, tb):
                        tile = (col - n0) // 128
                        i = cnt[0]; cnt[0] += 1
                        if nme in ("dt", "gkl"):
                            j = i % 2
                            S.op("dve", lambda: nc.vector.tensor_copy(stf[j][0:ncol, :], ps), reads=[bps], writes=[bstf[j]])
                            dst = (DTT if nme == "dt" else GKL)[tb * 4:tb * 4 + 4, 0:ncol, :].rearrange("c p i -> p c i")
                            S.dma("sp", dst, stf[j][0:ncol, :].rearrange("p (c i) -> p c i", i=128), reads=[bstf[j]])
                            return
                        j = i % 4
                        if nme == "xbc":
                            q = ccnt[0] % 2; ccnt[0] += 1
                            assert ncol == 128
                            S.op("pool", lambda: nc.gpsimd.tensor_copy(pre[q][:, 0:3], hal[:, tile, :]), reads=[bhal], writes=[bpre[q]])
                            S.op("act", lambda: nc.scalar.copy(pre[q][:, 3:T + 3], ps), reads=[bps], writes=[bpre[q]])
                            S.op("pool", lambda: nc.gpsimd.tensor_copy(hal[:, tile, :], pre[q][:, T:T + 3]), reads=[bpre[q]], writes=[bhal])
                            S.op("act", lambda: nc.scalar.activation(ct1[q][:], pre[q][:, 1:1 + T], AF.Identity, scale=cwv[:, 1, tile:tile + 1]),
                                 reads=[bpre[q], bl], writes=[bct1[q]])
                            S.op("dve", lambda: nc.vector.scalar_tensor_tensor(cacc[q][:], pre[q][:, 0:T], cwv[:, 0, tile:tile + 1], ct1[q][:], ALU.mult, ALU.add),
                                 reads=[bpre[q], bct1[q], bl], writes=[bcacc[q]])
                            for k in (2, 3):
                                S.op("dve", lambda: nc.vector.scalar_tensor_tensor(cacc[q][:], pre[q][:, k:k + T], cwv[:, k, tile:tile + 1], cacc[q][:], ALU.mult, ALU.add),
                                     reads=[bpre[q], bcacc[q], bl], writes=[bcacc[q]])
                            S.op("act", lambda: nc.scalar.activation(stg[j][:, :], cacc[q][:], AF.Silu, bias=cbv[:, tile:tile + 1]), reads=[bcacc[q], bl], writes=[bstg[j]])
                        elif i % 2:
                            S.op("act", lambda: nc.scalar.copy(stg[j][0:ncol, :], ps), reads=[bps], writes=[bstg[j]])
                        else:
                            S.op("dve", lambda: nc.vector.tensor_copy(stg[j][0:ncol, :], ps), reads=[bps], writes=[bstg[j]])
                        if nme in ("gs", "gg"):
                            S.dma("sp", GATE[tb, 0:ncol, tile + (NDT if nme == "gg" else 0), :], stg[j][0:ncol, :], reads=[bstg[j]])
                        else:
                            dt_ = dict(z=ZT, xbc=XBCT, q=QT, k=KTd, v=VT, g=GT)[nme]
                            dst = dt_[tb * 4:tb * 4 + 4, 0:ncol, tile, :].rearrange("c p i -> p c i")
                            S.dma("sp", dst, stg[j][0:ncol, :].rearrange("p (c i) -> p c i", i=128), reads=[bstg[j]])
                    return epi

                ent = []
                for sb0 in range(0, NB, TBn):
                    for nme in names:
                        ent += wstream_entries(("in", nme), D, segs[nme][0], segs[nme][1])
                wstream_begin(C, ent)
                for sb0 in range(0, NB, TBn):
                    tbs = list(range(sb0, min(sb0 + TBn, NB)))
                    for j, tb in enumerate(tbs):
                        S.dma("sp", xT[:, :, j * T:(j + 1) * T], XB[tb], writes=[bxT])
                    for nme in names:
                        gemm_multi(C, ("in", nme), D, segs[nme][0], segs[nme][1], lambda kt, tb: xT[:, kt, (tb - sb0) * T:(tb - sb0 + 1) * T],
                                   bxT, tbs, T, make_epi(nme, sb0))

            if stop_after == "p1":
                break
            with Phase(C):
                bg = precast_gen(l + 1, 4) if (l + 1 < depth and stop_after is None) else None
                ssd_pass(C, K, lp, ZT, XBCT, DTT, YO, NCH, bg=bg)
                if bg is not None:
                    for _ in bg:
                        pass
            if stop_after == "ssd":
                break
            with Phase(C):
                gla_pass(C, K, lp, QT, KTd, VT, GT, GKL, YO, NCH, NXT)
            if stop_after == "gla":
                break

            with Phase(C):
                alloc_wbufs(C)
                FFh = FF // NFF
                yo = C.sb([128, max(NXT, NVT), T], BF16); byo = Buf()
                ubf = C.sb([128, FFh // 128, T], BF16); bubf = Buf()
                assert max(NXT, NVT) >= 2 * NDT
                C.ln_vb = yo[:, 0:NDT, :]; C.b_ln_vb = byo
                C.ln_sq = yo[:, NDT:2 * NDT, :]; C.b_ln_sq = byo
                C.ln_st = C.sb([128, 3, T], F32); C.b_ln_st = Buf()
                f32a = C.sb([128, NDT, T], F32); bfa32 = Buf()
                f32b = C.sb([128, NDT, T], F32); bfb32 = Buf()
                bfa = C.sb([128, NDT, T], BF16); bbfa = Buf()
                gt = [C.sb([128, T], BF16) for _ in range(2)]; bgt = [Buf(), Buf()]
                tmp = [C.sb([128, T], F32) for _ in range(2)]; btmp = [Buf(), Buf()]
                tm2 = [C.sb([128, T], F32) for _ in range(2)]; btm2 = [Buf(), Buf()]
                ec = [0]
                gb0, gb1 = P("gb0"), P("gb1")
                alpha = float(cfg.alpha)
                ent = []
                for tb in range(NB):
                    ent += wstream_entries("sb", DI, 0, D) + wstream_entries("gb", DV, 0, D) + wstream_entries("out", D, 0, D)
                    for hf in range(NFF):
                        ent += wstream_entries(("up", hf), D, hf * FFh, (hf + 1) * FFh) + wstream_entries(("dn", hf), FFh, 0, D)
                wstream_begin(C, ent)
                for tb in range(NB):
                    S.dma("sp", f32b[:], XRsrc[tb], writes=[bfb32])
                    for c4 in range(4):
                        S.dma("sp", yo[:, 0:NXT, c4 * 128:(c4 + 1) * 128], YO[tb * 4 + c4][:, 0:NXT, :], writes=[byo])

                    def epi_bs(col, ncol, ps, bps):
                        nt = col // 128
                        i = ec[0] % 2; ec[0] += 1
                        S.dma("sp", gt[i][:], GATE[tb, :, nt, :], writes=[bgt[i]])
                        S.op("act", lambda: nc.scalar.activation(tmp[i][:], gt[i][:], AF.Sigmoid, bias=gb0[:, nt:nt + 1]), reads=[bgt[i], bl], writes=[btmp[i]])
                        S.op("dve", lambda: nc.vector.tensor_tensor(f32a[:, nt, :], tmp[i][:], ps, ALU.mult), reads=[btmp[i], bps], writes=[bfa32])
                    gemm_fm(C, "sb", DI, 0, D, lambda kt: yo[:, kt, :], byo, T, epi_bs)
                    for c4 in range(4):
                        S.dma("sp", yo[:, 0:NVT, c4 * 128:(c4 + 1) * 128], YO[tb * 4 + c4][:, NXT:NXT + NVT, :], writes=[byo])

                    def epi_bg(col, ncol, ps, bps):
                        nt = col // 128
                        i = ec[0] % 2; ec[0] += 1
                        S.dma("sp", gt[i][:], GATE[tb, :, NDT + nt, :], writes=[bgt[i]])
                        S.op("act", lambda: nc.scalar.activation(tmp[i][:], gt[i][:], AF.Sigmoid, bias=gb1[:, nt:nt + 1]), reads=[bgt[i], bl], writes=[btmp[i]])
                        S.op("dve", lambda: nc.vector.tensor_tensor(tm2[i][:], tmp[i][:], ps, ALU.mult), reads=[btmp[i], bps], writes=[btm2[i]])
                        S.op("pool", lambda: nc.gpsimd.tensor_tensor(bfa[:, nt, :], tm2[i][:], f32a[:, nt, :], ALU.add), reads=[btm2[i], bfa32], writes=[bbfa])
                    gemm_fm(C, "gb", DV, 0, D, lambda kt: yo[:, kt, :], byo, T, epi_bg)

                    def epi_out(col, ncol, ps, bps):
                        nt = col // 128
                        S.op("dve", lambda: nc.vector.scalar_tensor_tensor(f32a[:, nt, :], f32b[:, nt, :], alpha, ps, ALU.mult, ALU.add),
                             reads=[bfb32, bps], writes=[bfa32])
                    gemm_fm(C, "out", D, 0, D, lambda kt: bfa[:, kt, :], bbfa, T, epi_out)
                    layernorm_fm(C, f32a, bfa32, NDT, T, P("ln1g"), P("ln1b"), f32b, bfb32, bfa, bbfa, ones_bf, bK)
                    for hf in range(NFF):
                        def epi_up(col, ncol, ps, bps):
                            ft = (col - hf * FFh) // 128
                            i = ec[0] % 2; ec[0] += 1
                            S.op("act", lambda: nc.scalar.activation(tmp[i][:], ps, AF.Relu), reads=[bps], writes=[btmp[i]])
                            if i:
                                S.op("pool", lambda: nc.gpsimd.tensor_tensor(ubf[:, ft, :], tmp[i][:], tmp[i][:], ALU.mult), reads=[btmp[i]], writes=[bubf])
                            else:
                                S.op("dve", lambda: nc.vector.tensor_tensor(ubf[:, ft, :], tmp[i][:], tmp[i][:], ALU.mult), reads=[btmp[i]], writes=[bubf])
                        gemm_fm(C, ("up", hf), D, hf * FFh, (hf + 1) * FFh, lambda kt: bfa[:, kt, :], bbfa, T, epi_up)

                        def epi_dn(col, ncol, ps, bps):
                            nt = col // 128
                            if hf == 0:
                                S.op("dve", lambda: nc.vector.scalar_tensor_tensor(f32a[:, nt, :], f32b[:, nt, :], alpha, ps, ALU.mult, ALU.add),
                                     reads=[bfb32, bps], writes=[bfa32])
                            else:
                                S.op("dve", lambda: nc.vector.tensor_tensor(f32a[:, nt, :], f32a[:, nt, :], ps, ALU.add), reads=[bfa32, bps], writes=[bfa32])
                        gemm_fm(C, ("dn", hf), FFh, 0, D, lambda kt: ubf[:, kt, :], bubf, T, epi_dn)
                    layernorm_fm(C, f32a, bfa32, NDT, T, P("ln2g"), P("ln2b"), f32b, bfb32, bfa, bbfa, ones_bf, bK)
                    S.dma("sp", XRdst[tb], f32b[:], reads=[bfb32])
                    if l < depth - 1:
                        S.dma("sp", XB[tb], bfa[:], reads=[bbfa])
        barrier(C)
        _CACHE["n_ins"] = dict(S.n_ins)
        _CACHE["nsem"] = S.nsem
    return nc


def host_inputs(cfg, inp, b):
    D, L, depth, H, G = cfg.D, cfg.L, cfg.depth, cfg.H, cfg.G
    NDT, NXT, NVT = D // 128, cfg.DI // 128, cfg.DV // 128
    NCT = NXT + 2 * G
    NB = L // 512
    HV = cfg.DV // cfg.GH
    off, NLP = lp_layout(cfg)
    f = lambda a: np.asarray(a, dtype=np.float32)
    x = f(inp["x"])[b]
    X0 = np.ascontiguousarray(x.reshape(NB, 512, NDT, 128).transpose(0, 3, 2, 1))
    LP = np.zeros((depth, 128, NLP), np.float32)
    tile_pm = lambda v, nt: v.reshape(nt, 128).T

    def put(l, nme, arr):
        o, n = off[nme]
        LP[l, :arr.shape[0], o:o + n] = arr
    for l in range(depth):
        put(l, "cw", f(inp["ssd_conv_w"])[l].reshape(4, NCT, 128).transpose(2, 0, 1).reshape(128, 4 * NCT))
        put(l, "cb", tile_pm(f(inp["ssd_conv_b"])[l], NCT))
        put(l, "dtb", f(inp["ssd_dt_bias"])[l][:, None])
        put(l, "alog", f(inp["ssd_A_log"])[l][:, None])
        put(l, "Drep", np.broadcast_to(f(inp["ssd_D"])[l][None, :], (128, H)))
        put(l, "normw", tile_pm(f(inp["ssd_norm_w"])[l], NXT))
        put(l, "gnw", tile_pm(np.tile(f(inp["gla_norm_w"])[l], cfg.GH), NVT))
        put(l, "gb0", tile_pm(f(inp["gate_bias"])[l, 0], NDT))
        put(l, "gb1", tile_pm(f(inp["gate_bias"])[l, 1], NDT))
        for nme in ("ln1_g", "ln1_b", "ln2_g", "ln2_b"):
            put(l, nme.replace("_", ""), tile_pm(f(inp[nme])[l], NDT))
    GKW = np.concatenate([f(inp["gla_gk_w"]), f(inp["gla_gk_b"])[:, None, :]], axis=1)
    KC = np.zeros((128, 4, 128), np.float32)
    KC[:, 0, :] = np.eye(128)
    KC[:, 1, :] = np.triu(np.ones((128, 128)))
    KC[:, 2, :] = 1.0
    KC[:, 3, :] = np.tril(np.ones((128, 128)), -1)
    m = {"X0": X0, "LP": LP, "GKW": np.ascontiguousarray(GKW), "KC": KC}
    for k in ("w_in", "w_ssd_branch", "w_gla_branch", "w_out", "w_up", "w_down"):
        m[k] = f(inp[k])
    return m


def host_output(cfg, OUT):
    NB, _, NDT, T = OUT.shape
    return OUT.transpose(0, 3, 2, 1).reshape(NB * T, NDT * 128)


def kernel(**inputs):
    cfg = Cfg()
    B = np.asarray(inputs["x"]).shape[0]
    assert B == N_CORES_USED
    if "nc" not in _CACHE:
        _CACHE["nc"] = build_program(cfg)
    nc = _CACHE["nc"]
    in_maps = [host_inputs(cfg, inputs, b) for b in range(B)]
    res = run_bass_kernel_spmd(nc, in_maps, core_ids=list(range(B)))
    out = np.stack([host_output(cfg, np.asarray(res.results[b]["OUT"], dtype=np.float32)) for b in range(B)], 0)
    return out.astype(np.float32)
```
